# Optimizing a Trainium2 kernel written in Bass

```python
import jax, jax.numpy as jnp
from jax import lax
import numpy as np

D_MODEL = 1024
BATCH = 4
SEQ = 8192
DEPTH = 2

CHUNK = 64
Q_BLOCK = 128
FOX_HEADS = 8
FOX_HEAD_DIM = 64
FOX_WIDTH = FOX_HEADS * FOX_HEAD_DIM
POOL_WINDOWS = (2, 4, 8, 16)
POOL_GROUPS = len(POOL_WINDOWS)
POOL_WIDTH = D_MODEL - FOX_WIDTH
POOL_GROUP_DIM = POOL_WIDTH // POOL_GROUPS
EVEN_IN_WIDTH = 3 * FOX_WIDTH + FOX_HEADS + POOL_WIDTH
LRU_WIDTH = D_MODEL
LRU_HEADS = 4
LRU_HEAD_DIM = LRU_WIDTH // LRU_HEADS
CONV_WIDTH = 4
LRU_C = 8.0
D_FF = -(-8 * D_MODEL // (3 * 256)) * 256
RMS_EPS = 1e-6

kernel_name = "fox_pool_rglru_hybrid_trunk"


def rmsnorm(x, g):
    xf = x.astype(jnp.float32)
    y = xf * lax.rsqrt(jnp.mean(xf * xf, axis=-1, keepdims=True) + RMS_EPS)
    return (y * g.astype(jnp.float32)).astype(x.dtype)


def swiglu(h, w_gate, w_up, w_down):
    return (jax.nn.silu(h @ w_gate) * (h @ w_up)) @ w_down


def forgetting_attention(q, k, v, log_f):
    B, S, H, Dh = q.shape
    n_blk = S // Q_BLOCK
    scale = Dh ** -0.5
    c = jnp.cumsum(log_f, axis=1).transpose(0, 2, 1)
    q_blocks = q.reshape(B, n_blk, Q_BLOCK, H, Dh).transpose(1, 0, 3, 2, 4)
    c_blocks = c.reshape(B, H, n_blk, Q_BLOCK).transpose(2, 0, 1, 3)
    k_pos = jnp.arange(S, dtype=jnp.int32)

    def one_block(args):
        qb, cb, i = args
        s = jnp.einsum('bhqd,bshd->bhqs', qb, k, preferred_element_type=jnp.float32)
        s = s * scale + cb[..., None] - c[:, :, None, :]
        q_pos = i * Q_BLOCK + jnp.arange(Q_BLOCK, dtype=jnp.int32)
        mask = k_pos[None, :] <= q_pos[:, None]
        s = jnp.where(mask, s, -jnp.inf)
        p = jax.nn.softmax(s, axis=-1)
        return jnp.einsum('bhqs,bshd->bqhd', p.astype(v.dtype), v)

    out = lax.map(one_block, (q_blocks, c_blocks, jnp.arange(n_blk, dtype=jnp.int32)))
    return out.transpose(1, 0, 2, 3, 4).reshape(B, S, H * Dh)


def multiscale_pool(u, pool_w, pool_scale):
    B, S, _ = u.shape
    uf = u.astype(jnp.float32).reshape(B, S, POOL_GROUPS, POOL_GROUP_DIM)
    cs = jnp.cumsum(uf, axis=1)
    t1 = jnp.arange(1, S + 1, dtype=jnp.float32)
    pooled = []
    for g, w in enumerate(POOL_WINDOWS):
        cg = cs[:, :, g]
        lagged = jnp.pad(cg, ((0, 0), (w, 0), (0, 0)))[:, :S]
        mean = (cg - lagged) / jnp.minimum(t1, float(w))[None, :, None]
        pooled.append(mean - uf[:, :, g])
    pooled = jnp.stack(pooled, axis=2)
    mixed = jnp.einsum('bsgd,gde->bsge', pooled, pool_w.astype(jnp.float32))
    return (mixed.reshape(B, S, POOL_WIDTH) * pool_scale.astype(jnp.float32)).astype(u.dtype)


def even_mixer(h, w_in, b_f, pool_w, pool_scale, w_out):
    B, S, _ = h.shape
    proj = h @ w_in
    q, k, v, f_logit, u = jnp.split(
        proj, [FOX_WIDTH, 2 * FOX_WIDTH, 3 * FOX_WIDTH, 3 * FOX_WIDTH + FOX_HEADS], axis=-1)
    q = q.reshape(B, S, FOX_HEADS, FOX_HEAD_DIM)
    k = k.reshape(B, S, FOX_HEADS, FOX_HEAD_DIM)
    v = v.reshape(B, S, FOX_HEADS, FOX_HEAD_DIM)
    log_f = jax.nn.log_sigmoid(f_logit.astype(jnp.float32) + b_f.astype(jnp.float32))
    attn = forgetting_attention(q, k, v, log_f)
    pool = multiscale_pool(u, pool_w, pool_scale)
    return jnp.concatenate([attn, pool.astype(attn.dtype)], axis=-1) @ w_out


def causal_depthwise_conv(x, conv_w, conv_b):
    C = x.shape[-1]
    y = lax.conv_general_dilated(
        x, conv_w[:, None, :], window_strides=(1,), padding=[(CONV_WIDTH - 1, 0)],
        dimension_numbers=('NWC', 'WIO', 'NWC'), feature_group_count=C)
    return y + conv_b


def rg_lru(x, w_a, b_a, w_x, b_x, lam):
    B, S, W = x.shape
    xf = x.astype(jnp.float32)
    xh = xf.reshape(B, S, LRU_HEADS, LRU_HEAD_DIM)
    r = jax.nn.sigmoid(jnp.einsum('bshd,hde->bshe', xh, w_a.astype(jnp.float32)).reshape(B, S, W)
                       + b_a.astype(jnp.float32))
    i = jax.nn.sigmoid(jnp.einsum('bshd,hde->bshe', xh, w_x.astype(jnp.float32)).reshape(B, S, W)
                       + b_x.astype(jnp.float32))
    log_a = -LRU_C * r * jax.nn.softplus(-lam.astype(jnp.float32))
    a = jnp.exp(log_a)
    b = jnp.sqrt(-jnp.expm1(2.0 * log_a)) * (i * xf)

    def combine(c1, c2):
        a1, b1 = c1
        a2, b2 = c2
        return a1 * a2, a2 * b1 + b2

    _, hs = lax.associative_scan(combine, (a, b), axis=1)
    return hs.astype(x.dtype)


def odd_mixer(h, w_in, conv_w, conv_b, w_a, b_a, w_x, b_x, lam, w_out):
    proj = h @ w_in
    gate, xr = jnp.split(proj, 2, axis=-1)
    xr = causal_depthwise_conv(xr, conv_w, conv_b)
    y = rg_lru(xr, w_a, b_a, w_x, b_x, lam)
    return (jax.nn.gelu(gate) * y) @ w_out


def setup_inputs(seed: int = 0) -> dict:
    key = jax.random.key(seed)
    ks = jax.random.split(key, 32)
    f32 = jnp.float32
    ne = (DEPTH + 1) // 2
    no = DEPTH // 2

    def nrm(k, shape, scale):
        return jax.random.normal(k, shape, f32) * scale

    def gain(k, shape):
        return 1.0 + 0.05 * jax.random.normal(k, shape, f32)

    u = jax.random.uniform(ks[24], (no, LRU_WIDTH), f32, minval=0.9, maxval=0.999)
    s = u ** (1.0 / LRU_C)
    lam = jnp.log(s) - jnp.log1p(-s)
    return {
        "x": jax.random.normal(ks[0], (BATCH, SEQ, D_MODEL), f32),
        "mix_pre_g": gain(ks[1], (DEPTH, D_MODEL)),
        "mix_post_g": gain(ks[2], (DEPTH, D_MODEL)),
        "ffn_pre_g": gain(ks[3], (DEPTH, D_MODEL)),
        "ffn_post_g": gain(ks[4], (DEPTH, D_MODEL)),
        "ffn_w_gate": nrm(ks[5], (DEPTH, D_MODEL, D_FF), D_MODEL ** -0.5),
        "ffn_w_up": nrm(ks[6], (DEPTH, D_MODEL, D_FF), D_MODEL ** -0.5),
        "ffn_w_down": nrm(ks[7], (DEPTH, D_FF, D_MODEL), D_FF ** -0.5),
        "ev_w_in": nrm(ks[8], (ne, D_MODEL, EVEN_IN_WIDTH), D_MODEL ** -0.5),
        "ev_b_f": jax.random.uniform(ks[9], (ne, FOX_HEADS), f32, minval=1.0, maxval=6.0),
        "ev_pool_w": nrm(ks[10], (ne, POOL_GROUPS, POOL_GROUP_DIM, POOL_GROUP_DIM), POOL_GROUP_DIM ** -0.5),
        "ev_pool_scale": 1.0 + 0.1 * jax.random.normal(ks[11], (ne, POOL_WIDTH), f32),
        "ev_w_out": nrm(ks[12], (ne, D_MODEL, D_MODEL), D_MODEL ** -0.5),
        "od_w_in": nrm(ks[13], (no, D_MODEL, 2 * LRU_WIDTH), D_MODEL ** -0.5),
        "od_conv_w": nrm(ks[14], (no, CONV_WIDTH, LRU_WIDTH), CONV_WIDTH ** -0.5),
        "od_conv_b": nrm(ks[15], (no, LRU_WIDTH), 0.01),
        "od_w_a": nrm(ks[16], (no, LRU_HEADS, LRU_HEAD_DIM, LRU_HEAD_DIM), LRU_HEAD_DIM ** -0.5),
        "od_b_a": nrm(ks[17], (no, LRU_WIDTH), 0.01),
        "od_w_x": nrm(ks[18], (no, LRU_HEADS, LRU_HEAD_DIM, LRU_HEAD_DIM), LRU_HEAD_DIM ** -0.5),
        "od_b_x": nrm(ks[19], (no, LRU_WIDTH), 0.01),
        "od_lam": lam,
        "od_w_out": nrm(ks[20], (no, LRU_WIDTH, D_MODEL), LRU_WIDTH ** -0.5),
    }


def reference(x, mix_pre_g, mix_post_g, ffn_pre_g, ffn_post_g, ffn_w_gate, ffn_w_up, ffn_w_down,
              ev_w_in, ev_b_f, ev_pool_w, ev_pool_scale, ev_w_out,
              od_w_in, od_conv_w, od_conv_b, od_w_a, od_b_a, od_w_x, od_b_x, od_lam, od_w_out):
    for layer in range(DEPTH):
        h = rmsnorm(x, mix_pre_g[layer])
        if layer % 2 == 0:
            e = layer // 2
            m = even_mixer(h, ev_w_in[e], ev_b_f[e], ev_pool_w[e], ev_pool_scale[e], ev_w_out[e])
        else:
            o = layer // 2
            m = odd_mixer(h, od_w_in[o], od_conv_w[o], od_conv_b[o], od_w_a[o], od_b_a[o],
                          od_w_x[o], od_b_x[o], od_lam[o], od_w_out[o])
        x = x + rmsnorm(m, mix_post_g[layer])
        h = rmsnorm(x, ffn_pre_g[layer])
        x = x + rmsnorm(swiglu(h, ffn_w_gate[layer], ffn_w_up[layer], ffn_w_down[layer]), ffn_post_g[layer])
    return x
```

```python
import numpy as np
from contextlib import ExitStack
import concourse.bass as bass
import concourse.mybir as mybir
from concourse.bass_utils import run_bass_kernel_spmd

F32 = mybir.dt.float32
BF16 = mybir.dt.bfloat16
AF = mybir.ActivationFunctionType
ALU = mybir.AluOpType

D = 1024
KC = 8
TT = 512
DFF = 2816
FC = 22
NH = 8
DH = 64
EIN = 2056
KA = 70
EPS = 1e-6

V_PRE, V_POST, V_FPRE, V_FPOST = 0, 16, 32, 48
V_PSC = 64
V_CW = 68
V_CB, V_BA, V_BX, V_LAM = 100, 108, 116, 124
V_INVC = 132
NV = 196


class Tok:
    __slots__ = ("key", "val")

    def __init__(self, key, val):
        self.key = key
        self.val = val


class Eng:
    def __init__(self, fw, name):
        self.fw = fw
        self.name = name
        self.ops = []
        self.cnt = 0
        self.waited = {}

    def wait(self, deps):
        for t in deps:
            if t is None:
                continue
            if self.waited.get(t.key, 0) < t.val:
                self.waited[t.key] = t.val
                self.ops.append(("wait", t.key, t.val))

    def op(self, fn, deps=(), mark=True):
        self.wait(deps)
        if mark:
            self.cnt += 1
            self.ops.append(("op", fn, self.name, 1))
            return Tok(self.name, self.cnt)
        self.ops.append(("op", fn, None, 0))
        return None

    def dma(self, out, in_, slot, deps=(), **kw):
        self.wait(deps)
        fw = self.fw
        fw.dma_cnt[slot] = fw.dma_cnt.get(slot, 0) + 16
        self.ops.append(("op", lambda e: e.dma_start(out=out, in_=in_, **kw), slot, 16))
        return Tok(slot, fw.dma_cnt[slot])


class T:
    __slots__ = ("w", "r", "const")

    def __init__(self, const=False):
        self.w = None
        self.r = {}
        self.const = const


class FW:
    uid = 0

    def __init__(self, nc):
        self.nc = nc
        self.dma_cnt = {}
        self.pe = Eng(self, "pe")
        self.act = Eng(self, "act")
        self.dve = Eng(self, "dve")
        self.pool = Eng(self, "pool")
        self.sp = Eng(self, "sp")
        self.engs = [self.pe, self.act, self.dve, self.pool, self.sp]
        self.alltoks = []

    @staticmethod
    def _deps(rd, wr, extra):
        deps = []
        for b in rd:
            deps.append(b.w)
        for b in wr:
            deps.append(b.w)
            deps.extend(b.r.values())
        deps.extend(extra)
        return deps

    @staticmethod
    def _upd(tok, rd, wr):
        for b in rd:
            if not b.const:
                b.r[tok.key] = tok
        for b in wr:
            b.w = tok
            b.r = {}
        return tok

    def do(self, eng, fn, rd=(), wr=(), extra=()):
        tok = eng.op(fn, self._deps(rd, wr, extra), True)
        return self._upd(tok, rd, wr)

    def group(self, fns, rd=(), wr=(), extra=()):
        pe = self.pe
        pe.wait(self._deps(rd, wr, extra))
        for fn in fns[:-1]:
            pe.op(fn, (), False)
        tok = pe.op(fns[-1], (), True)
        return self._upd(tok, rd, wr)

    def dma(self, eng, out, in_, sem, rd=(), wr=(), extra=(), **kw):
        tok = eng.dma(out, in_, sem, self._deps(rd, wr, extra), **kw)
        self.alltoks.append(tok)
        return self._upd(tok, rd, wr)

    def finish(self):
        mx = {}
        for t in self.alltoks:
            if t.key not in mx or mx[t.key].val < t.val:
                mx[t.key] = t
        self.sp.wait(list(mx.values()))
        self.sp.wait([Tok(e.name, e.cnt) for e in self.engs if e.cnt > 0 and e is not self.sp])
        self.emit()

    def emit(self):
        nc = self.nc
        keys = [e.name for e in self.engs] + list(self.dma_cnt.keys())
        FW.uid += 1
        sems = {k: nc.alloc_semaphore(name=f"s{FW.uid}_{k}") for k in keys}
        with nc.Block() as block:

            def run(eng, h):
                for o in eng.ops:
                    if o[0] == "wait":
                        h.wait_ge(sems[o[1]], o[2])
                    else:
                        ins = o[1](h)
                        if o[2] is not None:
                            ins.then_inc(sems[o[2]], o[3])

            @block.tensor
            def _(h):
                run(self.pe, h)

            @block.scalar
            def _(h):
                run(self.act, h)

            @block.vector
            def _(h):
                run(self.dve, h)

            @block.gpsimd
            def _(h):
                run(self.pool, h)

            @block.sync
            def _(h):
                run(self.sp, h)

        nc.clear_and_free_semaphores(list(sems.values()))
        nc.all_engine_barrier()


def build(S, nphase=99, debug=False):
    NT = S // TT
    NB = S // 128
    nc = bass.Bass("TRN2", target_bir_lowering=False)

    def din(name, shape):
        return nc.dram_tensor(name, list(shape), F32, kind="ExternalInput").ap()

    x_d = din("x", [S, D])
    vecs_d = din("vecs", [128, NV])
    bf_d = din("b_f", [NH, 1])
    wg_d = din("ffn_w_gate", [2, D, DFF])
    wu_d = din("ffn_w_up", [2, D, DFF])
    wd_d = din("ffn_w_down", [2, DFF, D])
    ewin_d = din("ev_w_in", [D, EIN])
    epw_d = din("ev_pool_w", [4, 128, 128])
    ewout_d = din("ev_w_out", [D, D])
    owin_d = din("od_w_in", [D, 2 * D])
    owa_d = din("od_w_a", [4, 256, 256])
    owx_d = din("od_w_x", [4, 256, 256])
    owout_d = din("od_w_out", [D, D])
    y_d = nc.dram_tensor("y", [S, D], F32, kind="ExternalOutput").ap()

    skind = "ExternalOutput" if debug else "Internal"

    def dscr(name, shape, dt):
        return nc.dram_tensor(name, list(shape), dt, kind=skind).ap()

    res_d = dscr("res", [KC, 128, S], F32)
    q_d = dscr("qaug", [NH, KA, S], BF16)
    k_d = dscr("kaug", [NH, KA, S], BF16)
    at_d = dscr("attnT", [4, 128, S], BF16)
    u_d = dscr("uT", [4, 128, S + 16], F32)
    a_d = dscr("actT", [FC, 128, S], BF16)
    g_d = dscr("gT", [KC, 128, S], F32)
    xr_d = dscr("xrT", [KC, 128, S + 3], F32)
    if debug:
        dbg = {n: dscr("dbg_" + n, [KC, 128, S], F32) for n in ("xc", "a", "b", "hs", "z", "r", "i")}

    outer = ExitStack()
    with outer:
        nuid = [0]

        def sb_in(st, name, shape, dt):
            nuid[0] += 1
            return st.enter_context(nc.sbuf_tensor(f"sb{nuid[0]}_{name}", list(shape), dt))

        def ps_in(st, name):
            nuid[0] += 1
            return st.enter_context(nc.psum_tensor(f"ps{nuid[0]}_{name}", [128, TT], F32))

        vecs = sb_in(outer, "vecs", [128, NV], F32)
        ident = sb_in(outer, "ident", [128, 128], F32)
        ones_bf = sb_in(outer, "ones_bf", [128, 128], BF16)
        onesf = sb_in(outer, "onesf", [128, 64], F32)
        tri = sb_in(outer, "tri", [128, 128], BF16)
        sc1 = sb_in(outer, "sc1", [128, KC], F32)
        sc2 = sb_in(outer, "sc2", [128, KC], F32)
        nbf = sb_in(outer, "nbf", [NH, 1], F32)
        zer = sb_in(outer, "zer", [128, 16], F32)
        epsb = sb_in(outer, "epsb", [128, 1], F32)
        tmpc = sb_in(outer, "tmpc", [128, KC], F32)

        def vcol(base, k):
            return vecs[:, base + k:base + k + 1]

        def phase0():
            fw = FW(nc)
            sp, act, dve, pool = fw.sp, fw.act, fw.dve, fw.pool
            tv = T(); tb = T(); tz = T(); tt = T(); ti = T()
            fw.dma(sp, vecs[:], vecs_d[:, :], "c_vecs", wr=[tv])
            fw.dma(sp, nbf[:], bf_d[:, :], "c_bf", wr=[tb])
            fw.do(dve, lambda e: e.tensor_scalar(out=nbf[:], in0=nbf[:], scalar1=-1.0, scalar2=None, op0=ALU.mult), rd=[], wr=[tb])
            fw.do(pool, lambda e: e.memset(ident[:], 1.0), wr=[ti])
            fw.do(pool, lambda e: e.affine_select(out=ident[:], in_=ident[:], pattern=[[-1, 128]], compare_op=ALU.is_equal, fill=0.0, base=0, channel_multiplier=1), wr=[ti])
            fw.do(pool, lambda e: e.memset(tri[:], 1.0), wr=[tt])
            fw.do(pool, lambda e: e.affine_select(out=tri[:], in_=tri[:], pattern=[[1, 128]], compare_op=ALU.is_ge, fill=0.0, base=0, channel_multiplier=-1), wr=[tt])
            fw.do(dve, lambda e: e.memset(ones_bf[:], 1.0))
            fw.do(dve, lambda e: e.memset(onesf[:], 1.0))
            fw.do(dve, lambda e: e.memset(epsb[:], EPS))
            fw.do(dve, lambda e: e.memset(zer[:], 0.0), wr=[tz])
            for g in range(4):
                fw.dma(sp, u_d[g, :, 0:16], zer[:, 0:16], "c_z%d" % (g % 2), rd=[tz])
            for k in range(KC):
                fw.dma(sp, xr_d[k, :, 0:3], zer[:, 0:3], "c_y%d" % (k % 2), rd=[tz])
            tc_ = T()
            fw.do(act, lambda e: e.activation(out=tmpc[:], in_=vecs[:, V_LAM:V_LAM + KC], func=AF.Exp, scale=-1.0), rd=[tv], wr=[tc_])
            fw.do(act, lambda e: e.activation(out=tmpc[:], in_=tmpc[:], func=AF.Ln, bias=1.0, scale=1.0), wr=[tc_])
            fw.do(dve, lambda e: e.tensor_scalar(out=sc1[:], in0=tmpc[:], scalar1=-8.0, scalar2=None, op0=ALU.mult), rd=[tc_])
            fw.do(dve, lambda e: e.tensor_scalar(out=sc2[:], in0=tmpc[:], scalar1=-16.0, scalar2=None, op0=ALU.mult), rd=[tc_])
            fw.finish()

        class NormCtx:
            def __init__(self, fw, st, tag):
                self.fw = fw
                self.sq = [sb_in(st, f"sq{tag}{i}", [128, TT], BF16) for i in range(3)]
                self.sqt = [T() for _ in range(3)]
                self.sqi = 0
                self.rstd = [sb_in(st, f"rstd{tag}{i}", [128, TT], F32) for i in range(2)]
                self.rstdt = [T() for _ in range(2)]
                self.ri = 0
                self.stat = ps_in(st, f"stat{tag}")
                self.statt = T()

            def square(self, src, srct):
                fw = self.fw
                i = self.sqi % 3
                self.sqi += 1
                sq = self.sq[i]
                fw.do(fw.act, lambda e: e.activation(out=sq[:], in_=src, func=AF.Square), rd=[srct], wr=[self.sqt[i]])
                return i

            def accum(self, i, first, last):
                fw = self.fw
                sq = self.sq[i]
                stat = self.stat
                tok = fw.group([lambda e: e.matmul(stat[:], lhsT=ones_bf[:], rhs=sq[:], start=first, stop=last)],
                               rd=[self.sqt[i]], wr=[self.statt] if first else [])
                if not first:
                    self.statt.w = tok

            def finish(self):
                fw = self.fw
                j = self.ri % 2
                self.ri += 1
                r = self.rstd[j]
                rt = self.rstdt[j]
                stat = self.stat
                fw.do(fw.act, lambda e: e.activation(out=r[:], in_=stat[:], func=AF.Sqrt, bias=EPS, scale=1.0 / D), rd=[self.statt], wr=[rt])
                fw.do(fw.dve, lambda e: e.reciprocal(out=r[:], in_=r[:]), wr=[rt])
                return r, rt

        def load_w(fw, dst, src, sem, tr, nsplit=1):
            kcs = dst.shape[1]
            v = src.rearrange("(kc p) m -> p kc m", p=128)
            last = {}
            for k in range(kcs):
                t = fw.dma(fw.pool, dst[:, k, :], v[:, k, :], f"{sem}{k % 4}", max_dma_last_dim=4096)
                last[t.key] = t
            return list(last.values())

        def load_wc(fw, dst, src, sem, bounds):
            v = src.rearrange("(kc p) m -> p kc m", p=128)
            toks = []
            for bi in range(len(bounds) - 1):
                toks.append(fw.dma(fw.pool, dst[:, :, bounds[bi]:bounds[bi + 1]], v[:, :, bounds[bi]:bounds[bi + 1]], f"{sem}{bi}", max_dma_last_dim=4096))
            return toks

        def blk_of(col, bounds):
            for bi in range(len(bounds) - 1):
                if col < bounds[bi + 1]:
                    return bi
            raise ValueError(col)

        def phase1(st12, vres, vrest):
            with ExitStack() as st:
                fw = FW(nc)
                pe, act, dve, pool, sp = fw.pe, fw.act, fw.dve, fw.pool, fw.sp
                w_in = sb_in(st, "w_in0", [128, KC, EIN], BF16)
                wt = T(const=True)
                WB1 = [0, 512, 1024, 1544, EIN]
                wtk = load_wc(fw, w_in, ewin_d, "w1_", WB1)
                xin = [sb_in(st, "xin0", [128, 4, D], F32)] * 2
                xint = [T()] * 2
                xT = sb_in(st, "xT1", [128, KC, TT], F32)
                xTt = [T() for _ in range(KC)]
                hT = [sb_in(st, f"hT1{i}", [128, KC, TT], BF16) for i in range(2)]
                hTt = [[T() for _ in range(KC)] for _ in range(2)]
                nrm = NormCtx(fw, st, "1")
                stg = [sb_in(st, f"stg1{i}", [128, 4, TT], BF16) for i in range(2)]
                stgt = [[T() for _ in range(4)] for _ in range(2)]
                ustg = sb_in(st, "ustg", [128, 4, TT], F32)
                ustgt = [T() for _ in range(4)]
                fe = sb_in(st, "fe", [NH, TT], F32); fet = T()
                Cc = [sb_in(st, f"Cc{i}", [NH, TT], F32) for i in range(2)]
                Cct = [T() for _ in range(2)]
                r1 = sb_in(st, "r1", [NH, TT], F32); r1t = T()
                r2 = sb_in(st, "r2", [NH, TT], F32); r2t = T()
                prt = [sb_in(st, f"prt{i}", [NH, 3, TT], BF16) for i in range(2)]
                nprt = [sb_in(st, f"nprt{i}", [NH, 3, TT], BF16) for i in range(2)]
                prtt = [T() for _ in range(2)]
                nprtt = [T() for _ in range(2)]
                ones8 = sb_in(st, "ones8", [NH, TT], BF16); ones8t = T()
                onesrow = sb_in(st, "onesrow", [NH, TT], F32); onesrowt = T()
                tp = [ps_in(st, f"tp{i}") for i in range(2)]
                tpt = [T() for _ in range(2)]
                pj = [ps_in(st, f"pj{i}") for i in range(4)]
                pjt = [T() for _ in range(4)]
                cnt = {"pj": 0, "stg": 0, "ustg": 0, "tp": 0}

                fw.do(dve, lambda e: e.memset(ones8[:], 1.0), wr=[ones8t])
                fw.do(dve, lambda e: e.memset(onesrow[:], 1.0), wr=[onesrowt])
                fw.do(dve, lambda e: e.memset(vres[:, :, :, DH:DH + 1], 1.0), wr=[vrest])

                def load_x(i):
                    s = i % 2
                    fw.dma(sp, xin[s][:], x_d[i * TT:(i + 1) * TT, :].rearrange("(s p) d -> p s d", p=128), "ldx0", wr=[xint[s]])

                def prepA(i):
                    s = i % 2
                    for k in range(KC):
                        b = cnt["tp"] % 2
                        cnt["tp"] += 1
                        fns = [(lambda e, ss=ss, k=k, b=b: e.transpose(tp[b][:, ss * 128:(ss + 1) * 128], in_=xin[s][:, ss, k * 128:(k + 1) * 128], identity=ident[:])) for ss in range(4)]
                        fw.group(fns, rd=[xint[s]], wr=[tpt[b]])
                        if k % 2 == 0:
                            fw.do(act, lambda e, k=k, b=b: e.activation(out=xT[:, k, :], in_=tp[b][:], func=AF.Copy), rd=[tpt[b]], wr=[xTt[k]])
                        else:
                            fw.do(dve, lambda e, k=k, b=b: e.tensor_copy(out=xT[:, k, :], in_=tp[b][:]), rd=[tpt[b]], wr=[xTt[k]])
                    fw.dma(pool, res_d[:, :, i * TT:(i + 1) * TT].rearrange("k p t -> p k t"), xT[:], "st_res", rd=xTt)
                    sqs = []
                    for k in range(KC):
                        sqs.append(nrm.square(xT[:, k, :], xTt[k]))
                        if k >= 1:
                            nrm.accum(sqs[k - 1], k - 1 == 0, False)
                    nrm.accum(sqs[KC - 1], False, True)

                def prepB(i):
                    s = i % 2
                    r, rt = nrm.finish()
                    for k in range(KC):
                        fw.do(dve, lambda e, k=k: e.scalar_tensor_tensor(out=hT[s][:, k, :], in0=xT[:, k, :], scalar=vcol(V_PRE, k), in1=r[:], op0=ALU.mult, op1=ALU.mult),
                              rd=[xTt[k], rt], wr=[hTt[s][k]])

                def fm_group(i, col0, M):
                    s = i % 2
                    b = cnt["pj"] % 4
                    cnt["pj"] += 1
                    fns = [(lambda e, k=k: e.matmul(pj[b][0:M, :], lhsT=w_in[:, k, col0:col0 + M], rhs=hT[s][:, k, :], start=(k == 0), stop=(k == KC - 1))) for k in range(KC)]
                    fw.group(fns, rd=hTt[s], wr=[pjt[b]], extra=[wtk[blk_of(col0, WB1)]])
                    return b

                def proj_qk(i, c, isq):
                    b = fm_group(i, (0 if isq else 512) + c * 128, 128)
                    w_ = 0 if isq else 1
                    fw.do(act, lambda e: e.activation(out=stg[w_][:, c, :], in_=pj[b][:], func=AF.Copy, scale=(0.125 if isq else 1.0)), rd=[pjt[b]], wr=[stgt[w_][c]])
                    if c == 3:
                        dst = q_d if isq else k_d
                        dv = dst[:, 0:DH, i * TT:(i + 1) * TT].rearrange("(c hh) d t -> hh d c t", hh=2)
                        for hh in range(2):
                            fw.dma(pool, dv[hh], stg[w_][hh * DH:(hh + 1) * DH, :, :], f"st_qk{w_}{hh}", rd=stgt[w_])

                def proj_v(i, ss):
                    s = i % 2
                    b = cnt["pj"] % 4
                    cnt["pj"] += 1
                    fns = [(lambda e, k=k: e.matmul(pj[b][:], lhsT=hT[s][:, k, ss * 128:(ss + 1) * 128], rhs=w_in[:, k, 1024:1536], start=(k == 0), stop=(k == KC - 1))) for k in range(KC)]
                    fw.group(fns, rd=hTt[s], wr=[pjt[b]], extra=[wtk[2]])
                    blk = i * 4 + ss
                    fw.do(dve, lambda e: e.tensor_copy(out=vres[:, blk, :, 0:DH], in_=pj[b][:].rearrange("p (h d) -> p h d", h=NH)), rd=[pjt[b]], wr=[vrest])

                def proj_f(i):
                    b = fm_group(i, 1536, NH)
                    s = i % 2
                    fw.do(act, lambda e: e.activation(out=fe[:], in_=pj[b][0:NH, :], func=AF.Exp, bias=nbf[:], scale=-1.0), rd=[pjt[b]], wr=[fet])
                    fw.do(act, lambda e: e.activation(out=fe[:], in_=fe[:], func=AF.Ln, bias=1.0, scale=1.0), wr=[fet])
                    if i == 0:
                        fw.do(dve, lambda e: e.tensor_tensor_scan(out=Cc[s][:], data0=onesrow[:], data1=fe[:], initial=0.0, op0=ALU.mult, op1=ALU.add),
                              rd=[fet, onesrowt], wr=[Cct[s]])
                    else:
                        fw.do(dve, lambda e: e.tensor_tensor_scan(out=Cc[s][:], data0=onesrow[:], data1=fe[:], initial=Cc[1 - s][:, TT - 1:TT], op0=ALU.mult, op1=ALU.add),
                              rd=[fet, onesrowt, Cct[1 - s]], wr=[Cct[s]])
                    P_, N_ = prt[s], nprt[s]
                    fw.do(dve, lambda e: e.tensor_copy(out=P_[:, 0, :], in_=Cc[s][:]), rd=[Cct[s]], wr=[prtt[s]])
                    fw.do(dve, lambda e: e.tensor_tensor(out=r1[:], in0=Cc[s][:], in1=P_[:, 0, :], op=ALU.subtract), rd=[Cct[s], prtt[s]], wr=[r1t])
                    fw.do(dve, lambda e: e.tensor_copy(out=P_[:, 1, :], in_=r1[:]), rd=[r1t], wr=[prtt[s]])
                    fw.do(dve, lambda e: e.tensor_tensor(out=r2[:], in0=r1[:], in1=P_[:, 1, :], op=ALU.subtract), rd=[r1t, prtt[s]], wr=[r2t])
                    fw.do(dve, lambda e: e.tensor_copy(out=P_[:, 2, :], in_=r2[:]), rd=[r2t], wr=[prtt[s]])
                    fw.do(dve, lambda e: e.tensor_scalar(out=N_[:], in0=P_[:], scalar1=-1.0, scalar2=None, op0=ALU.mult), rd=[prtt[s]], wr=[nprtt[s]])
                    sl = slice(i * TT, (i + 1) * TT)
                    fw.dma(pool, q_d[:, DH:DH + 3, sl], N_[:], f"st_c{s}", rd=[nprtt[s]])
                    for jj in range(3):
                        fw.dma(sp, q_d[:, DH + 3 + jj, sl], ones8[:], f"st_1{s}", rd=[ones8t])
                        fw.dma(sp, k_d[:, DH + jj, sl], ones8[:], f"st_1{s}", rd=[ones8t])
                    fw.dma(pool, k_d[:, DH + 3:DH + 6, sl], P_[:], f"st_c{s}", rd=[prtt[s]])

                def proj_u(i, g):
                    b = fm_group(i, 1544 + g * 128, 128)
                    fw.do(act, lambda e: e.activation(out=ustg[:, g, :], in_=pj[b][:], func=AF.Copy), rd=[pjt[b]], wr=[ustgt[g]])
                    if g == 3:
                        fw.dma(pool, u_d[:, :, 16 + i * TT:16 + (i + 1) * TT].rearrange("g p t -> p g t"), ustg[:], "st_u0", rd=ustgt)

                load_x(0)
                prepA(0)
                prepB(0)
                for i in range(NT):
                    if i + 1 < NT:
                        load_x(i + 1)
                    for c in range(4):
                        proj_qk(i, c, True)
                    if i + 1 < NT:
                        prepA(i + 1)
                    for c in range(4):
                        proj_qk(i, c, False)
                    for ss in range(4):
                        proj_v(i, ss)
                    if i + 1 < NT:
                        prepB(i + 1)
                    proj_f(i)
                    for g in range(4):
                        proj_u(i, g)
                fw.finish()

        def phase2(vres, vrest):
            with ExitStack() as st:
                fw = FW(nc)
                pe, act, dve, pool, sp = fw.pe, fw.act, fw.dve, fw.pool, fw.sp
                Ks = [sb_in(st, f"Ks{i}", [KA, S], BF16) for i in range(2)]
                Qs = [sb_in(st, f"Qs{i}", [KA, S], BF16) for i in range(2)]
                Kt = [T() for _ in range(2)]
                Qt = [T() for _ in range(2)]
                NSB = 4
                sbk = [ps_in(st, f"sbk{i}") for i in range(NSB)]
                sbkt = [T() for _ in range(NSB)]
                Pb = [sb_in(st, f"Pb{i}", [128, TT], BF16) for i in range(NSB)]
                Pbt = [T() for _ in range(NSB)]
                ob = [ps_in(st, f"ob{i}") for i in range(2)]
                obt = [T() for _ in range(2)]
                bc = ps_in(st, "bc"); bct = T()
                rden = sb_in(st, "rden", [128, TT], F32); rdent = T()
                bcs = sb_in(st, "bcs", [DH, TT], F32); bcst = T()
                ostg = [sb_in(st, f"ostg{i}", [DH, TT], BF16) for i in range(2)]
                ostgt = [T() for _ in range(2)]

                def load_kq(h):
                    s = h % 2
                    fw.dma(sp, Ks[s][:], k_d[h, :, :], f"ldk{s}", wr=[Kt[s]])
                    fw.dma(sp, Qs[s][:], q_d[h, :, :], f"ldq{s}", wr=[Qt[s]])

                blocks = []
                for h in range(NH):
                    for qi in range(NT):
                        nkb = 4 * (qi + 1)
                        for kb in range(nkb):
                            blocks.append((h, qi, kb, nkb))
                nblk = len(blocks)

                def s_mm(n):
                    h, qi, kb, nkb = blocks[n]
                    s = h % 2
                    j = kb - 4 * qi
                    c0 = max(j, 0) * 128
                    b = n % NSB
                    fw.group([lambda e: e.matmul(sbk[b][:, c0:TT], lhsT=Ks[s][:, kb * 128:(kb + 1) * 128], rhs=Qs[s][:, qi * TT + c0:(qi + 1) * TT], start=True, stop=True)],
                             rd=[Kt[s], Qt[s]], wr=[sbkt[b]])
                    fw.do(act, lambda e: e.activation(out=Pb[b][:, c0:TT], in_=sbk[b][:, c0:TT], func=AF.Exp), rd=[sbkt[b]], wr=[Pbt[b]])
                    if j >= 0:
                        fw.do(dve, lambda e: e.tensor_tensor(out=Pb[b][:, c0:c0 + 128], in0=Pb[b][:, c0:c0 + 128], in1=tri[:], op=ALU.mult), wr=[Pbt[b]])

                fin_q = []

                def pv_mm(n):
                    h, qi, kb, nkb = blocks[n]
                    j = kb - 4 * qi
                    c0 = max(j, 0) * 128
                    b = n % NSB
                    o = (h * NT + qi) % 2
                    tok = fw.group([lambda e: e.matmul(ob[o][0:DH + 1, c0:TT], lhsT=vres[:, kb, h, 0:DH + 1], rhs=Pb[b][:, c0:TT], start=(kb == 0), stop=(kb == nkb - 1))],
                                   rd=[Pbt[b], vrest], wr=[obt[o]] if kb == 0 else [])
                    if kb > 0:
                        obt[o].w = tok
                    if kb == nkb - 1:
                        fw.do(dve, lambda e: e.reciprocal(out=rden[DH:DH + 1, :], in_=ob[o][DH:DH + 1, :]), rd=[obt[o]], wr=[rdent])
                        fin_q.append((n + 2, h, qi, o))

                def finalize(h, qi, o):
                    fw.group([lambda e: e.matmul(bc[0:DH, :], lhsT=onesf[DH:DH + 1, 0:DH], rhs=rden[DH:DH + 1, :], start=True, stop=True)], rd=[rdent], wr=[bct])
                    fw.do(act, lambda e: e.activation(out=bcs[:], in_=bc[0:DH, :], func=AF.Copy), rd=[bct], wr=[bcst])
                    g = (h * NT + qi) % 2
                    fw.do(dve, lambda e: e.tensor_tensor(out=ostg[g][:], in0=ob[o][0:DH, :], in1=bcs[:], op=ALU.mult), rd=[obt[o], bcst], wr=[ostgt[g]])
                    fw.dma(pool, at_d[h // 2, (h % 2) * DH:(h % 2 + 1) * DH, qi * TT:(qi + 1) * TT], ostg[g][:], f"st_o{g}", rd=[ostgt[g]])

                LOOK = 3
                load_kq(0)
                if NH > 1:
                    load_kq(1)
                loaded = 2
                for n in range(min(LOOK, nblk)):
                    s_mm(n)
                for n in range(nblk):
                    pv_mm(n)
                    if n + LOOK < nblk:
                        hn = blocks[n + LOOK][0]
                        s_mm(n + LOOK)
                    while fin_q and fin_q[0][0] <= n:
                        _, h_, qi_, o_ = fin_q.pop(0)
                        finalize(h_, qi_, o_)
                    h, qi, kb, nkb = blocks[n]
                    if qi == NT - 1 and kb == nkb - 1 and h + 2 < NH:
                        load_kq(h + 2)
                while fin_q:
                    _, h_, qi_, o_ = fin_q.pop(0)
                    finalize(h_, qi_, o_)
                fw.finish()

        class Tail:
            def __init__(self, fw, st, tag, gbase):
                self.fw = fw
                self.nrm = NormCtx(fw, st, tag)
                self.m = sb_in(st, f"m{tag}", [128, KC, TT], F32)
                self.mt = [T() for _ in range(KC)]
                self.gbase = gbase
                self.pend = None

            def chunk(self, c, bank, bankt):
                fw = self.fw
                m = self.m
                fw.do(fw.act, lambda e: e.activation(out=m[:, c, :], in_=bank[:], func=AF.Copy), rd=[bankt], wr=[self.mt[c]])
                i = self.nrm.square(bank[:], bankt)
                if self.pend is not None:
                    self.nrm.accum(self.pend[0], self.pend[1] == 0, False)
                self.pend = (i, c)

            def finish(self, xT, xTt):
                fw = self.fw
                self.nrm.accum(self.pend[0], False, True)
                self.pend = None
                r, rt = self.nrm.finish()
                m = self.m
                for c in range(KC):
                    fw.do(fw.dve, lambda e, c=c: e.scalar_tensor_tensor(out=m[:, c, :], in0=m[:, c, :], scalar=vcol(self.gbase, c), in1=r[:], op0=ALU.mult, op1=ALU.mult),
                          rd=[rt], wr=[self.mt[c]])
                    fw.do(fw.dve, lambda e, c=c: e.tensor_tensor(out=xT[:, c, :], in0=xT[:, c, :], in1=m[:, c, :], op=ALU.add), rd=[self.mt[c]], wr=[xTt[c]])

        def wload_tokens(fw, dst, src, sem):
            t = T(const=True)
            return load_w(fw, dst, src, sem, t)

        def phase3():
            with ExitStack() as st:
                fw = FW(nc)
                pe, act, dve, pool, sp = fw.pe, fw.act, fw.dve, fw.pool, fw.sp
                w_out = sb_in(st, "w_out0", [128, KC, D], BF16)
                wtoks = wload_tokens(fw, w_out, ewout_d, "w3_")
                pw = sb_in(st, "pw", [128, 4, 128], BF16)
                fw.dma(pool, pw[:], epw_d.rearrange("g d e -> d g e"), "w3p")
                wtoks = wtoks + [fw.alltoks[-1]]
                xT = [sb_in(st, f"xT3{i}", [128, KC, TT], F32) for i in range(2)]
                xTt = [[T() for _ in range(KC)] for _ in range(2)]
                cat = [sb_in(st, f"cat{i}", [128, KC, TT], BF16) for i in range(2)]
                catA = [T() for _ in range(2)]
                catP = [[T() for _ in range(4)] for _ in range(2)]
                ut = [sb_in(st, f"ut{i}", [128, 4, TT + 16], F32) for i in range(2)]
                utt = [T() for _ in range(2)]
                wa = sb_in(st, "wa", [128, TT + 16], F32); wat = T()
                wb = sb_in(st, "wb", [128, TT + 16], F32); wbt = T()
                pl = [sb_in(st, f"pl{i}", [128, TT], BF16) for i in range(2)]
                plt = [T() for _ in range(2)]
                fx = sb_in(st, "fx", [128, 16], F32); fxt = T()
                tail = Tail(fw, st, "3", V_POST + 0)
                pp = [ps_in(st, f"pp{i}") for i in range(2)]
                ppt = [T() for _ in range(2)]
                po = [ps_in(st, f"po{i}") for i in range(3)]
                pot = [T() for _ in range(3)]
                cnt = {"po": 0, "pl": 0}

                def load(i):
                    s = i % 2
                    sl = slice(i * TT, (i + 1) * TT)
                    fw.dma(sp, ut[s][:], u_d[:, :, i * TT:(i + 1) * TT + 16].rearrange("g p t -> p g t"), f"ldu{s}", wr=[utt[s]])
                    fw.dma(sp, cat[s][:, 0:4, :], at_d[:, :, sl].rearrange("k p t -> p k t"), f"lda{s}", wr=[catA[s]])
                    fw.dma(sp, xT[s][:], res_d[:, :, sl].rearrange("k p t -> p k t"), f"ldx{s}", wr=xTt[s])

                def pooling(i):
                    s = i % 2
                    W = TT + 16
                    for g in range(4):
                        u = ut[s]
                        src = (lambda a, b_, u=u, g=g: u[:, g, a:b_])
                        srct = utt[s]
                        bufs = [(wa, wat), (wb, wbt)]
                        sh = 1
                        for lvl in range(g + 1):
                            dstb, dstt = bufs[lvl % 2]
                            lo = 2 * sh - 1
                            fw.do(dve, lambda e, dstb=dstb, src=src, sh=sh, lo=lo: e.tensor_tensor(out=dstb[:, lo:W], in0=src(lo, W), in1=src(lo - sh, W - sh), op=ALU.add),
                                  rd=[srct], wr=[dstt])
                            src, srct = (lambda a, b_, dstb=dstb: dstb[:, a:b_]), dstt
                            sh *= 2
                        w = 2 ** (g + 1)
                        j = cnt["pl"] % 2
                        cnt["pl"] += 1
                        fw.do(dve, lambda e, src=src, g=g, j=j, w=w: e.scalar_tensor_tensor(out=pl[j][:], in0=src(16, W), scalar=1.0 / w, in1=u[:, g, 16:W], op0=ALU.mult, op1=ALU.subtract),
                              rd=[srct, utt[s]], wr=[plt[j]])
                        if i == 0:
                            fw.do(dve, lambda e, src=src, g=g: e.tensor_tensor(out=fx[:], in0=src(16, 32), in1=vecs[:, V_INVC + 16 * g:V_INVC + 16 * (g + 1)], op=ALU.mult), rd=[srct], wr=[fxt])
                            fw.do(dve, lambda e, g=g, j=j: e.tensor_tensor(out=pl[j][:, 0:16], in0=fx[:], in1=u[:, g, 16:32], op=ALU.subtract), rd=[fxt, utt[s]], wr=[plt[j]])
                        b = g % 2
                        fw.group([lambda e, g=g, j=j, b=b: e.matmul(pp[b][:], lhsT=pw[:, g, :], rhs=pl[j][:], start=True, stop=True)], rd=[plt[j]], wr=[ppt[b]], extra=wtoks)
                        fw.do(act, lambda e, g=g, b=b: e.activation(out=cat[s][:, 4 + g, :], in_=pp[b][:], func=AF.Identity, scale=vcol(V_PSC, g)), rd=[ppt[b]], wr=[catP[s][g]])

                def outproj(i):
                    s = i % 2
                    for c in range(KC):
                        b = cnt["po"] % 3
                        cnt["po"] += 1
                        fns = [(lambda e, k=k, c=c, b=b: e.matmul(po[b][:], lhsT=w_out[:, k, c * 128:(c + 1) * 128], rhs=cat[s][:, k, :], start=(k == 0), stop=(k == KC - 1))) for k in range(KC)]
                        fw.group(fns, rd=[catA[s]] + catP[s], wr=[pot[b]], extra=wtoks)
                        tail.chunk(c, po[b], pot[b])

                def outfin(i):
                    s = i % 2
                    tail.finish(xT[s], xTt[s])
                    fw.dma(pool, res_d[:, :, i * TT:(i + 1) * TT].rearrange("k p t -> p k t"), xT[s][:], f"st_x{s}", rd=xTt[s])

                load(0)
                pooling(0)
                for i in range(NT):
                    if i + 1 < NT:
                        load(i + 1)
                    outproj(i)
                    if i + 1 < NT:
                        pooling(i + 1)
                    outfin(i)
                fw.finish()

        def phase_ffn_a(layer):
            with ExitStack() as st:
                fw = FW(nc)
                pe, act, dve, pool, sp = fw.pe, fw.act, fw.dve, fw.pool, fw.sp
                wg = sb_in(st, "wg", [128, KC, DFF], BF16)
                wu = sb_in(st, "wu", [128, KC, DFF], BF16)
                WB4 = [0, 512, 1024, 1536, 2048, 2560, DFF]
                wgk, wuk = [], []
                v_g = wg_d[layer].rearrange("(kc p) m -> p kc m", p=128)
                v_u = wu_d[layer].rearrange("(kc p) m -> p kc m", p=128)
                for bi in range(len(WB4) - 1):
                    wgk.append(fw.dma(pool, wg[:, :, WB4[bi]:WB4[bi + 1]], v_g[:, :, WB4[bi]:WB4[bi + 1]], f"w4g{bi}", max_dma_last_dim=4096))
                    wuk.append(fw.dma(pool, wu[:, :, WB4[bi]:WB4[bi + 1]], v_u[:, :, WB4[bi]:WB4[bi + 1]], f"w4u{bi}", max_dma_last_dim=4096))
                xT = [sb_in(st, f"xT4{i}", [128, KC, TT], F32) for i in range(2)]
                xTt = [[T() for _ in range(KC)] for _ in range(2)]
                hT = [sb_in(st, f"hT4{i}", [128, KC, TT], BF16) for i in range(2)]
                hTt = [[T() for _ in range(KC)] for _ in range(2)]
                nrm = NormCtx(fw, st, "4")
                sg = [sb_in(st, f"sg{i}", [128, TT], F32) for i in range(2)]
                sgt = [T() for _ in range(2)]
                astg = sb_in(st, "astg", [128, FC, TT], BF16)
                astt = [T() for _ in range(FC)]
                pg = [ps_in(st, f"pg{i}") for i in range(3)]
                pgt = [T() for _ in range(3)]
                pu = [ps_in(st, f"pu{i}") for i in range(3)]
                put = [T() for _ in range(3)]
                gb = V_FPRE + 8 * layer

                def load(i):
                    s = i % 2
                    fw.dma(sp, xT[s][:], res_d[:, :, i * TT:(i + 1) * TT].rearrange("k p t -> p k t"), f"ldx{s}", wr=xTt[s])

                def prepA(i):
                    s = i % 2
                    sqs = []
                    for k in range(KC):
                        sqs.append(nrm.square(xT[s][:, k, :], xTt[s][k]))
                        if k >= 1:
                            nrm.accum(sqs[k - 1], k - 1 == 0, False)
                    nrm.accum(sqs[KC - 1], False, True)

                def prepB(i):
                    s = i % 2
                    r, rt = nrm.finish()
                    for k in range(KC):
                        fw.do(dve, lambda e, k=k: e.scalar_tensor_tensor(out=hT[s][:, k, :], in0=xT[s][:, k, :], scalar=vcol(gb, k), in1=r[:], op0=ALU.mult, op1=ALU.mult),
                              rd=[xTt[s][k], rt], wr=[hTt[s][k]])

                n = [0]

                def chunk(i, c):
                    s = i % 2
                    b = n[0] % 3
                    j2 = n[0] % 2
                    j4 = n[0] % 4
                    n[0] += 1
                    fns = [(lambda e, k=k: e.matmul(pg[b][:], lhsT=wg[:, k, c * 128:(c + 1) * 128], rhs=hT[s][:, k, :], start=(k == 0), stop=(k == KC - 1))) for k in range(KC)]
                    fw.group(fns, rd=hTt[s], wr=[pgt[b]], extra=[wgk[blk_of(c * 128, WB4)]])
                    fns = [(lambda e, k=k: e.matmul(pu[b][:], lhsT=wu[:, k, c * 128:(c + 1) * 128], rhs=hT[s][:, k, :], start=(k == 0), stop=(k == KC - 1))) for k in range(KC)]
                    fw.group(fns, rd=hTt[s], wr=[put[b]], extra=[wuk[blk_of(c * 128, WB4)]])
                    fw.do(act, lambda e: e.activation(out=sg[j2][:], in_=pg[b][:], func=AF.Silu), rd=[pgt[b]], wr=[sgt[j2]])
                    fw.do(dve, lambda e: e.tensor_tensor(out=astg[:, c, :], in0=sg[j2][:], in1=pu[b][:], op=ALU.mult), rd=[sgt[j2], put[b]], wr=[astt[c]])
                    if c == FC // 2 - 1 or c == FC - 1:
                        c0 = 0 if c < FC - 1 else FC // 2
                        fw.dma(pool, a_d[c0:c + 1, :, i * TT:(i + 1) * TT].rearrange("c p t -> p c t"), astg[:, c0:c + 1, :], f"st_a{0 if c0 == 0 else 1}", rd=astt[c0:c + 1])

                load(0)
                if NT > 1:
                    load(1)
                prepA(0)
                prepB(0)
                for i in range(NT):
                    for c in range(FC):
                        chunk(i, c)
                        if c == 4 and i + 1 < NT:
                            prepA(i + 1)
                        if c == 12 and i + 1 < NT:
                            prepB(i + 1)
                    if i + 2 < NT:
                        load(i + 2)
                fw.finish()

        def phase_ffn_b(layer, final):
            with ExitStack() as st:
                fw = FW(nc)
                pe, act, dve, pool, sp = fw.pe, fw.act, fw.dve, fw.pool, fw.sp
                wd = sb_in(st, "wd", [128, FC, D], BF16)
                WB5 = [0, 256, 512, 768, D]
                wdk = load_wc(fw, wd, wd_d[layer], "w5d", WB5)
                xT = [sb_in(st, f"xT5{i}", [128, KC, TT], F32) for i in range(2)]
                xTt = [[T() for _ in range(KC)] for _ in range(2)]
                aT = [sb_in(st, f"aT5{i}", [128, FC, TT], BF16) for i in range(2)]
                aTt = [T() for _ in range(2)]
                tail = Tail(fw, st, "5", V_FPOST + 8 * layer)
                po = [ps_in(st, f"po5{i}") for i in range(3)]
                pot = [T() for _ in range(3)]
                cnt = {"po": 0, "tp": 0}
                if final:
                    yo = sb_in(st, "yo", [128, 4, D], F32); yot = T()
                    tp = [ps_in(st, f"tp5{i}") for i in range(2)]
                    tpt = [T() for _ in range(2)]

                def load(i):
                    s = i % 2
                    sl = slice(i * TT, (i + 1) * TT)
                    fw.dma(sp, aT[s][:], a_d[:, :, sl].rearrange("k p t -> p k t"), f"lda{s}", wr=[aTt[s]])
                    fw.dma(sp, xT[s][:], res_d[:, :, sl].rearrange("k p t -> p k t"), f"ldx{s}", wr=xTt[s])

                def body(i):
                    s = i % 2
                    for c in range(KC):
                        b = cnt["po"] % 3
                        cnt["po"] += 1
                        fns = [(lambda e, k=k, c=c, b=b: e.matmul(po[b][:], lhsT=wd[:, k, c * 128:(c + 1) * 128], rhs=aT[s][:, k, :], start=(k == 0), stop=(k == FC - 1))) for k in range(FC)]
                        fw.group(fns, rd=[aTt[s]], wr=[pot[b]], extra=[wdk[c // 2]])
                        tail.chunk(c, po[b], pot[b])
                    tail.finish(xT[s], xTt[s])
                    if not final:
                        fw.dma(pool, res_d[:, :, i * TT:(i + 1) * TT].rearrange("k p t -> p k t"), xT[s][:], f"st_x{s}", rd=xTt[s])
                    else:
                        for ss in range(4):
                            for half in range(2):
                                b = cnt["tp"] % 2
                                cnt["tp"] += 1
                                fns = [(lambda e, kk=kk, b=b, half=half, ss=ss: e.transpose(tp[b][:, kk * 128:(kk + 1) * 128], in_=xT[s][:, half * 4 + kk, ss * 128:(ss + 1) * 128], identity=ident[:])) for kk in range(4)]
                                fw.group(fns, rd=xTt[s][half * 4:half * 4 + 4], wr=[tpt[b]])
                                if half == 0:
                                    fw.do(act, lambda e, b=b, ss=ss: e.activation(out=yo[:, ss, 0:512], in_=tp[b][:], func=AF.Copy), rd=[tpt[b]], wr=[yot])
                                else:
                                    fw.do(dve, lambda e, b=b, ss=ss: e.tensor_copy(out=yo[:, ss, 512:1024], in_=tp[b][:]), rd=[tpt[b]], wr=[yot])
                        fw.dma(pool, y_d[i * TT:(i + 1) * TT, :].rearrange("(s p) d -> p s d", p=128), yo[:], "st_y", rd=[yot])

                load(0)
                for i in range(NT):
                    if i + 1 < NT:
                        load(i + 1)
                    body(i)
                fw.finish()

        def phase6():
            with ExitStack() as st:
                fw = FW(nc)
                pe, act, dve, pool, sp = fw.pe, fw.act, fw.dve, fw.pool, fw.sp
                w_in = sb_in(st, "w_in1", [128, KC, 2 * D], BF16)
                WB6 = [0, 512, 1024, 1536, 2 * D]
                w6k = load_wc(fw, w_in, owin_d, "w6_", WB6)
                xT = [sb_in(st, f"xT6{i}", [128, KC, TT], F32) for i in range(2)]
                xTt = [[T() for _ in range(KC)] for _ in range(2)]
                hT = [sb_in(st, f"hT6{i}", [128, KC, TT], BF16) for i in range(2)]
                hTt = [[T() for _ in range(KC)] for _ in range(2)]
                nrm = NormCtx(fw, st, "6")
                og = [sb_in(st, f"og{i}", [128, TT], F32) for i in range(4)]
                ogt = [T() for _ in range(4)]
                pj = [ps_in(st, f"pj6{i}") for i in range(4)]
                pjt = [T() for _ in range(4)]
                gb = V_PRE + 8
                n = [0]

                def load(i):
                    s = i % 2
                    fw.dma(sp, xT[s][:], res_d[:, :, i * TT:(i + 1) * TT].rearrange("k p t -> p k t"), f"ldx{s}", wr=xTt[s])

                def prepA(i):
                    s = i % 2
                    sqs = []
                    for k in range(KC):
                        sqs.append(nrm.square(xT[s][:, k, :], xTt[s][k]))
                        if k >= 1:
                            nrm.accum(sqs[k - 1], k - 1 == 0, False)
                    nrm.accum(sqs[KC - 1], False, True)

                def prepB(i):
                    s = i % 2
                    r, rt = nrm.finish()
                    for k in range(KC):
                        fw.do(dve, lambda e, k=k: e.scalar_tensor_tensor(out=hT[s][:, k, :], in0=xT[s][:, k, :], scalar=vcol(gb, k), in1=r[:], op0=ALU.mult, op1=ALU.mult),
                              rd=[xTt[s][k], rt], wr=[hTt[s][k]])

                def chunk(i, c):
                    s = i % 2
                    b = n[0] % 4
                    n[0] += 1
                    fns = [(lambda e, k=k: e.matmul(pj[b][:], lhsT=w_in[:, k, c * 128:(c + 1) * 128], rhs=hT[s][:, k, :], start=(k == 0), stop=(k == KC - 1))) for k in range(KC)]
                    fw.group(fns, rd=hTt[s], wr=[pjt[b]], extra=[w6k[c // 4]])
                    if c < KC:
                        fw.do(act, lambda e: e.activation(out=og[b][:], in_=pj[b][:], func=AF.Gelu_apprx_tanh), rd=[pjt[b]], wr=[ogt[b]])
                        fw.dma(pool, g_d[c, :, i * TT:(i + 1) * TT], og[b][:], f"st_g{b}", rd=[ogt[b]])
                    else:
                        fw.do(dve, lambda e: e.tensor_copy(out=og[b][:], in_=pj[b][:]), rd=[pjt[b]], wr=[ogt[b]])
                        fw.dma(pool, xr_d[c - KC, :, 3 + i * TT:3 + (i + 1) * TT], og[b][:], f"st_g{b}", rd=[ogt[b]])

                load(0)
                if NT > 1:
                    load(1)
                prepA(0)
                prepB(0)
                for i in range(NT):
                    for c in range(2 * KC):
                        chunk(i, c)
                        if c == 3 and i + 1 < NT:
                            prepA(i + 1)
                        if c == 9 and i + 1 < NT:
                            prepB(i + 1)
                    if i + 2 < NT:
                        load(i + 2)
                fw.finish()

        def phase7():
            with ExitStack() as st:
                fw = FW(nc)
                pe, act, dve, pool, sp = fw.pe, fw.act, fw.dve, fw.pool, fw.sp
                w_out = sb_in(st, "w_out1", [128, KC, D], BF16)
                wtoks = wload_tokens(fw, w_out, owout_d, "w7o")
                wa_ = sb_in(st, "w_a", [128, 4, 2, 256], BF16)
                wx_ = sb_in(st, "w_x", [128, 4, 2, 256], BF16)
                for hd in range(4):
                    fw.dma(pool, wa_[:, hd, :, :], owa_d[hd].rearrange("(kc p) e -> p kc e", p=128), f"w7a{hd}")
                    wtoks = wtoks + [fw.alltoks[-1]]
                    fw.dma(pool, wx_[:, hd, :, :], owx_d[hd].rearrange("(kc p) e -> p kc e", p=128), f"w7x{hd}")
                    wtoks = wtoks + [fw.alltoks[-1]]
                xT = [sb_in(st, "xT70", [128, KC, TT], F32)] * 2
                xTt = [[T() for _ in range(KC)]] * 2
                xr = [sb_in(st, "xr70", [128, KC, TT + 3], F32)] * 2
                xrt = [T()] * 2
                gt_ = [sb_in(st, "gt70", [128, KC, TT], F32)] * 2
                gtt = [T()] * 2
                xc2 = [sb_in(st, f"xc{i}", [128, KC, TT], F32) for i in range(2)]
                xct2 = [[T() for _ in range(KC)] for _ in range(2)]
                xcb2 = [sb_in(st, f"xcb{i}", [128, KC, TT], BF16) for i in range(2)]
                xcbt2 = [[T() for _ in range(KC)] for _ in range(2)]
                hsr = [sb_in(st, f"hsr{i}", [128, TT], F32) for i in range(2)]
                hsrt = [T() for _ in range(2)]
                carry = sb_in(st, "carry", [128, KC], F32)
                carryt = [T() for _ in range(KC)]
                zT = sb_in(st, "zT", [128, KC, TT], BF16)
                zTt = [T() for _ in range(KC)]
                NR = 4
                rr = [sb_in(st, f"rr{i}", [128, TT], F32) for i in range(NR)]; rrt = [T() for _ in range(NR)]
                ii = [sb_in(st, f"ii{i}", [128, TT], F32) for i in range(NR)]; iit = [T() for _ in range(NR)]
                aa = [sb_in(st, f"aa{i}", [128, TT], F32) for i in range(NR)]; aat = [T() for _ in range(NR)]
                bb = [sb_in(st, f"bb{i}", [128, TT], F32) for i in range(2)]; bbt = [T() for _ in range(2)]
                tail = Tail(fw, st, "7", V_POST + 8)
                pa = [ps_in(st, f"pa{i}") for i in range(2)]; pat = [T() for _ in range(2)]
                px = [ps_in(st, f"px{i}") for i in range(2)]; pxt = [T() for _ in range(2)]
                po = [ps_in(st, f"po7{i}") for i in range(3)]; pot = [T() for _ in range(3)]
                cnt = {"po": 0, "g": 0, "b": 0}

                def load_xr(i):
                    fw.dma(sp, xr[0][:], xr_d[:, :, i * TT:(i + 1) * TT + 3].rearrange("k p t -> p k t"), "ldr0", wr=[xrt[0]])

                def load_g(i):
                    fw.dma(sp, gt_[0][:], g_d[:, :, i * TT:(i + 1) * TT].rearrange("k p t -> p k t"), "ldg0", wr=[gtt[0]])

                def load_x(i):
                    fw.dma(sp, xT[0][:], res_d[:, :, i * TT:(i + 1) * TT].rearrange("k p t -> p k t"), "ldx0", wr=xTt[0])

                def conv_a(i):
                    xc, xct = xc2[i % 2], xct2[i % 2]
                    for c in range(KC):
                        fw.do(act, lambda e, c=c: e.activation(out=xc[:, c, :], in_=xr[0][:, c, 0:TT], func=AF.Identity, scale=vcol(V_CW, c), bias=vcol(V_CB, c)),
                              rd=[xrt[0]], wr=[xct[c]])

                def conv_b(i):
                    xc, xct = xc2[i % 2], xct2[i % 2]
                    for c in range(KC):
                        for j in range(1, 4):
                            fw.do(dve, lambda e, c=c, j=j: e.scalar_tensor_tensor(out=xc[:, c, :], in0=xr[0][:, c, j:j + TT], scalar=vcol(V_CW + 8 * j, c), in1=xc[:, c, :], op0=ALU.mult, op1=ALU.add),
                                  rd=[xrt[0]], wr=[xct[c]])

                def conv_c(i):
                    xc, xct = xc2[i % 2], xct2[i % 2]
                    xcb, xcbt = xcb2[i % 2], xcbt2[i % 2]
                    for c in range(KC):
                        fw.do(act, lambda e, c=c: e.activation(out=xcb[:, c, :], in_=xc[:, c, :], func=AF.Copy), rd=[xct[c]], wr=[xcbt[c]])
                        if debug:
                            fw.dma(sp, dbg["xc"][c, :, i * TT:(i + 1) * TT], xc[:, c, :], "dbg0", rd=[xct[c]])

                def recur(i):
                    s = i % 2
                    xc, xct = xc2[i % 2], xct2[i % 2]
                    xcb, xcbt = xcb2[i % 2], xcbt2[i % 2]
                    for grp in range(2):
                        cs = list(range(4 * grp, 4 * grp + 4))
                        for q_, c in enumerate(cs):
                            hd, mm = c // 2, c % 2
                            b = cnt["g"] % 2
                            cnt["g"] += 1
                            fns = [(lambda e, kk=kk, hd=hd, mm=mm, b=b: e.matmul(pa[b][:], lhsT=wa_[:, hd, kk, mm * 128:(mm + 1) * 128], rhs=xcb[:, 2 * hd + kk, :], start=(kk == 0), stop=(kk == 1))) for kk in range(2)]
                            fw.group(fns, rd=[xcbt[2 * hd], xcbt[2 * hd + 1]], wr=[pat[b]], extra=wtoks)
                            fns = [(lambda e, kk=kk, hd=hd, mm=mm, b=b: e.matmul(px[b][:], lhsT=wx_[:, hd, kk, mm * 128:(mm + 1) * 128], rhs=xcb[:, 2 * hd + kk, :], start=(kk == 0), stop=(kk == 1))) for kk in range(2)]
                            fw.group(fns, rd=[xcbt[2 * hd], xcbt[2 * hd + 1]], wr=[pxt[b]], extra=wtoks)
                            fw.do(act, lambda e, c=c, b=b, q_=q_: e.activation(out=rr[q_][:], in_=pa[b][:], func=AF.Sigmoid, bias=vcol(V_BA, c), scale=1.0), rd=[pat[b]], wr=[rrt[q_]])
                            fw.do(act, lambda e, c=c, b=b, q_=q_: e.activation(out=ii[q_][:], in_=px[b][:], func=AF.Sigmoid, bias=vcol(V_BX, c), scale=1.0), rd=[pxt[b]], wr=[iit[q_]])
                            if debug:
                                fw.dma(sp, dbg["r"][c, :, i * TT:(i + 1) * TT], rr[q_][:], "dbg5", rd=[rrt[q_]])
                                fw.dma(sp, dbg["i"][c, :, i * TT:(i + 1) * TT], ii[q_][:], "dbg6", rd=[iit[q_]])
                        for q_, c in enumerate(cs):
                            fw.do(act, lambda e, c=c, q_=q_: e.activation(out=aa[q_][:], in_=rr[q_][:], func=AF.Exp, scale=sc1[:, c:c + 1]), rd=[rrt[q_]], wr=[aat[q_]])
                            fw.do(act, lambda e, c=c, q_=q_: e.activation(out=rr[q_][:], in_=rr[q_][:], func=AF.Exp, scale=sc2[:, c:c + 1]), wr=[rrt[q_]])
                        for q_, c in enumerate(cs):
                            fw.do(act, lambda e, q_=q_: e.activation(out=rr[q_][:], in_=rr[q_][:], func=AF.Sqrt, bias=1.0, scale=-1.0), wr=[rrt[q_]])
                        for q_, c in enumerate(cs):
                            j = cnt["b"] % 2
                            cnt["b"] += 1
                            fw.do(dve, lambda e, c=c, q_=q_: e.tensor_tensor(out=ii[q_][:], in0=ii[q_][:], in1=xc[:, c, :], op=ALU.mult), rd=[xct[c]], wr=[iit[q_]])
                            fw.do(dve, lambda e, q_=q_, j=j: e.tensor_tensor(out=bb[j][:], in0=ii[q_][:], in1=rr[q_][:], op=ALU.mult), rd=[iit[q_], rrt[q_]], wr=[bbt[j]])
                            if i == 0:
                                fw.do(dve, lambda e, q_=q_, j=j: e.tensor_tensor_scan(out=hsr[j][:], data0=aa[q_][:], data1=bb[j][:], initial=0.0, op0=ALU.mult, op1=ALU.add),
                                      rd=[aat[q_], bbt[j]], wr=[hsrt[j]])
                            else:
                                fw.do(dve, lambda e, c=c, q_=q_, j=j: e.tensor_tensor_scan(out=hsr[j][:], data0=aa[q_][:], data1=bb[j][:], initial=carry[:, c:c + 1], op0=ALU.mult, op1=ALU.add),
                                      rd=[aat[q_], bbt[j], carryt[c]], wr=[hsrt[j]])
                            fw.do(dve, lambda e, c=c, j=j: e.tensor_copy(out=carry[:, c:c + 1], in_=hsr[j][:, TT - 1:TT]), rd=[hsrt[j]], wr=[carryt[c]])
                            fw.do(dve, lambda e, c=c, j=j: e.tensor_tensor(out=zT[:, c, :], in0=hsr[j][:], in1=gt_[0][:, c, :], op=ALU.mult), rd=[hsrt[j], gtt[0]], wr=[zTt[c]])
                            if debug:
                                fw.dma(sp, dbg["a"][c, :, i * TT:(i + 1) * TT], aa[q_][:], "dbg1", rd=[aat[q_]])
                                fw.dma(sp, dbg["b"][c, :, i * TT:(i + 1) * TT], bb[j][:], "dbg2", rd=[bbt[j]])
                                fw.dma(sp, dbg["hs"][c, :, i * TT:(i + 1) * TT], hsr[j][:], "dbg3", rd=[hsrt[j]])

                def outproj(i):
                    for c in range(KC):
                        b = cnt["po"] % 3
                        cnt["po"] += 1
                        fns = [(lambda e, k=k, c=c, b=b: e.matmul(po[b][:], lhsT=w_out[:, k, c * 128:(c + 1) * 128], rhs=zT[:, k, :], start=(k == 0), stop=(k == KC - 1))) for k in range(KC)]
                        fw.group(fns, rd=zTt, wr=[pot[b]], extra=wtoks)
                        tail.chunk(c, po[b], pot[b])

                def outfin(i):
                    tail.finish(xT[0], xTt[0])
                    fw.dma(pool, res_d[:, :, i * TT:(i + 1) * TT].rearrange("k p t -> p k t"), xT[0][:], "st_x0", rd=xTt[0])

                load_xr(0)
                load_g(0)
                load_x(0)
                conv_a(0)
                conv_b(0)
                conv_c(0)
                for i in range(NT):
                    if i + 1 < NT:
                        load_xr(i + 1)
                    recur(i)
                    if i + 1 < NT:
                        conv_a(i + 1)
                        conv_b(i + 1)
                        load_g(i + 1)
                    outproj(i)
                    if i + 1 < NT:
                        conv_c(i + 1)
                    outfin(i)
                    if i + 1 < NT:
                        load_x(i + 1)
                fw.finish()

        phase0()
        if nphase >= 1:
            with ExitStack() as st12:
                vres = sb_in(st12, "vres", [128, NB, NH, DH + 1], BF16)
                vrest = T()
                phase1(st12, vres, vrest)
                if nphase >= 2:
                    vrest2 = T(const=True)
                    phase2(vres, vrest2)
        if nphase >= 3:
            phase3()
        if nphase >= 4:
            phase_ffn_a(0)
        if nphase >= 5:
            phase_ffn_b(0, False)
        if nphase >= 6:
            phase6()
        if nphase >= 7:
            phase7()
        if nphase >= 8:
            phase_ffn_a(1)
        if nphase >= 9:
            phase_ffn_b(1, True)
    return nc


def _cm(v):
    return np.ascontiguousarray(np.asarray(v, np.float32).reshape(-1, 128).T)


def pack_vecs(inp):
    vecs = np.zeros((128, NV), np.float32)
    for l in range(2):
        vecs[:, V_PRE + 8 * l:V_PRE + 8 * l + 8] = _cm(inp["mix_pre_g"][l])
        vecs[:, V_POST + 8 * l:V_POST + 8 * l + 8] = _cm(inp["mix_post_g"][l])
        vecs[:, V_FPRE + 8 * l:V_FPRE + 8 * l + 8] = _cm(inp["ffn_pre_g"][l])
        vecs[:, V_FPOST + 8 * l:V_FPOST + 8 * l + 8] = _cm(inp["ffn_post_g"][l])
    vecs[:, V_PSC:V_PSC + 4] = _cm(inp["ev_pool_scale"][0])
    for j in range(4):
        vecs[:, V_CW + 8 * j:V_CW + 8 * j + 8] = _cm(inp["od_conv_w"][0, j])
    vecs[:, V_CB:V_CB + 8] = _cm(inp["od_conv_b"][0])
    vecs[:, V_BA:V_BA + 8] = _cm(inp["od_b_a"][0])
    vecs[:, V_BX:V_BX + 8] = _cm(inp["od_b_x"][0])
    vecs[:, V_LAM:V_LAM + 8] = _cm(inp["od_lam"][0])
    for g, w in enumerate((2, 4, 8, 16)):
        for t in range(16):
            vecs[:, V_INVC + 16 * g + t] = 1.0 / min(t + 1, w)
    return vecs


def make_in_map(inp, xb):
    f = lambda a: np.ascontiguousarray(np.asarray(a, np.float32))
    return {
        "x": f(xb),
        "vecs": pack_vecs(inp),
        "b_f": f(np.asarray(inp["ev_b_f"])[0].reshape(NH, 1)),
        "ffn_w_gate": f(inp["ffn_w_gate"]),
        "ffn_w_up": f(inp["ffn_w_up"]),
        "ffn_w_down": f(inp["ffn_w_down"]),
        "ev_w_in": f(np.asarray(inp["ev_w_in"])[0]),
        "ev_pool_w": f(np.asarray(inp["ev_pool_w"])[0]),
        "ev_w_out": f(np.asarray(inp["ev_w_out"])[0]),
        "od_w_in": f(np.asarray(inp["od_w_in"])[0]),
        "od_w_a": f(np.asarray(inp["od_w_a"])[0]),
        "od_w_x": f(np.asarray(inp["od_w_x"])[0]),
        "od_w_out": f(np.asarray(inp["od_w_out"])[0]),
    }


def kernel(**inputs):
    x = np.asarray(inputs["x"], np.float32)
    B, S, _ = x.shape
    nc = build(S)
    work = [0, 1, 4, 5][:B]
    real = [make_in_map(inputs, x[b]) for b in range(B)]
    zero = {k: np.zeros_like(v) for k, v in real[0].items()}
    in_maps = [zero] * 8
    in_maps = list(in_maps)
    for b, c in enumerate(work):
        in_maps[c] = real[b]
    res = run_bass_kernel_spmd(nc, in_maps, core_ids=list(range(8)))
    return np.stack([np.asarray(res.results[c]["y"], np.float32) for c in work], axis=0)
```

```python
import numpy as np
from contextlib import ExitStack
import concourse.bass as bass
import concourse.mybir as mybir
from concourse.bass_utils import run_bass_kernel_spmd

F32 = mybir.dt.float32
BF16 = mybir.dt.bfloat16
AF = mybir.ActivationFunctionType
ALU = mybir.AluOpType

D = 1024
KC = 8
TT = 512
DFF = 2816
FC = 22
NH = 8
DH = 64
EIN = 2056
KA = 70
EPS = 1e-6

V_PRE, V_POST, V_FPRE, V_FPOST = 0, 16, 32, 48
V_PSC = 64
V_CW = 68
V_CB, V_BA, V_BX, V_LAM = 100, 108, 116, 124
V_INVC = 132
NV = 196


class Tok:
    __slots__ = ("key", "val")

    def __init__(self, key, val):
        self.key = key
        self.val = val


class Eng:
    def __init__(self, fw, name):
        self.fw = fw
        self.name = name
        self.ops = []
        self.cnt = 0
        self.waited = {}

    def wait(self, deps):
        for t in deps:
            if t is None:
                continue
            if self.waited.get(t.key, 0) < t.val:
                self.waited[t.key] = t.val
                self.ops.append(("wait", t.key, t.val))

    def op(self, fn, deps=(), mark=True):
        self.wait(deps)
        if mark:
            self.cnt += 1
            self.ops.append(("op", fn, self.name, 1))
            return Tok(self.name, self.cnt)
        self.ops.append(("op", fn, None, 0))
        return None

    def dma(self, out, in_, slot, deps=(), **kw):
        self.wait(deps)
        fw = self.fw
        fw.dma_cnt[slot] = fw.dma_cnt.get(slot, 0) + 16
        self.ops.append(("op", lambda e: e.dma_start(out=out, in_=in_, **kw), slot, 16))
        return Tok(slot, fw.dma_cnt[slot])


class T:
    __slots__ = ("w", "r", "const")

    def __init__(self, const=False):
        self.w = None
        self.r = {}
        self.const = const


class FW:
    uid = 0

    def __init__(self, nc):
        self.nc = nc
        self.dma_cnt = {}
        self.pe = Eng(self, "pe")
        self.act = Eng(self, "act")
        self.dve = Eng(self, "dve")
        self.pool = Eng(self, "pool")
        self.sp = Eng(self, "sp")
        self.engs = [self.pe, self.act, self.dve, self.pool, self.sp]
        self.alltoks = []

    @staticmethod
    def _deps(rd, wr, extra):
        deps = []
        for b in rd:
            deps.append(b.w)
        for b in wr:
            deps.append(b.w)
            deps.extend(b.r.values())
        deps.extend(extra)
        return deps

    @staticmethod
    def _upd(tok, rd, wr):
        for b in rd:
            if not b.const:
                b.r[tok.key] = tok
        for b in wr:
            b.w = tok
            b.r = {}
        return tok

    def do(self, eng, fn, rd=(), wr=(), extra=()):
        tok = eng.op(fn, self._deps(rd, wr, extra), True)
        return self._upd(tok, rd, wr)

    def group(self, fns, rd=(), wr=(), extra=()):
        pe = self.pe
        pe.wait(self._deps(rd, wr, extra))
        for fn in fns[:-1]:
            pe.op(fn, (), False)
        tok = pe.op(fns[-1], (), True)
        return self._upd(tok, rd, wr)

    def dma(self, eng, out, in_, sem, rd=(), wr=(), extra=(), **kw):
        tok = eng.dma(out, in_, sem, self._deps(rd, wr, extra), **kw)
        self.alltoks.append(tok)
        return self._upd(tok, rd, wr)

    def finish(self):
        mx = {}
        for t in self.alltoks:
            if t.key not in mx or mx[t.key].val < t.val:
                mx[t.key] = t
        self.sp.wait(list(mx.values()))
        self.sp.wait([Tok(e.name, e.cnt) for e in self.engs if e.cnt > 0 and e is not self.sp])
        self.emit()

    def emit(self):
        nc = self.nc
        keys = [e.name for e in self.engs] + list(self.dma_cnt.keys())
        FW.uid += 1
        sems = {k: nc.alloc_semaphore(name=f"s{FW.uid}_{k}") for k in keys}
        with nc.Block() as block:

            def run(eng, h):
                for o in eng.ops:
                    if o[0] == "wait":
                        h.wait_ge(sems[o[1]], o[2])
                    else:
                        ins = o[1](h)
                        if o[2] is not None:
                            ins.then_inc(sems[o[2]], o[3])

            @block.tensor
            def _(h):
                run(self.pe, h)

            @block.scalar
            def _(h):
                run(self.act, h)

            @block.vector
            def _(h):
                run(self.dve, h)

            @block.gpsimd
            def _(h):
                run(self.pool, h)

            @block.sync
            def _(h):
                run(self.sp, h)

        nc.clear_and_free_semaphores(list(sems.values()))
        nc.all_engine_barrier()


def build(S, nphase=99, debug=False):
    NT = S // TT
    NB = S // 128
    nc = bass.Bass("TRN2", target_bir_lowering=False)

    def din(name, shape):
        return nc.dram_tensor(name, list(shape), F32, kind="ExternalInput").ap()

    x_d = din("x", [S, D])
    vecs_d = din("vecs", [128, NV])
    bf_d = din("b_f", [NH, 1])
    wg_d = din("ffn_w_gate", [2, D, DFF])
    wu_d = din("ffn_w_up", [2, D, DFF])
    wd_d = din("ffn_w_down", [2, DFF, D])
    ewin_d = din("ev_w_in", [D, EIN])
    epw_d = din("ev_pool_w", [4, 128, 128])
    ewout_d = din("ev_w_out", [D, D])
    owin_d = din("od_w_in", [D, 2 * D])
    owa_d = din("od_w_a", [4, 256, 256])
    owx_d = din("od_w_x", [4, 256, 256])
    owout_d = din("od_w_out", [D, D])
    y_d = nc.dram_tensor("y", [S, D], F32, kind="ExternalOutput").ap()

    skind = "ExternalOutput" if debug else "Internal"

    def dscr(name, shape, dt):
        return nc.dram_tensor(name, list(shape), dt, kind=skind).ap()

    res_d = dscr("res", [KC, 128, S], F32)
    q_d = dscr("qaug", [NH, KA, S], BF16)
    k_d = dscr("kaug", [NH, KA, S], BF16)
    at_d = dscr("attnT", [4, 128, S], BF16)
    u_d = dscr("uT", [4, 128, S + 16], F32)
    a_d = dscr("actT", [FC, 128, S], BF16)
    g_d = dscr("gT", [KC, 128, S], F32)
    xr_d = dscr("xrT", [KC, 128, S + 3], F32)
    if debug:
        dbg = {n: dscr("dbg_" + n, [KC, 128, S], F32) for n in ("xc", "a", "b", "hs", "z", "r", "i")}

    outer = ExitStack()
    with outer:
        nuid = [0]

        def sb_in(st, name, shape, dt):
            nuid[0] += 1
            return st.enter_context(nc.sbuf_tensor(f"sb{nuid[0]}_{name}", list(shape), dt))

        def ps_in(st, name):
            nuid[0] += 1
            return st.enter_context(nc.psum_tensor(f"ps{nuid[0]}_{name}", [128, TT], F32))

        vecs = sb_in(outer, "vecs", [128, NV], F32)
        ident = sb_in(outer, "ident", [128, 128], F32)
        ones_bf = sb_in(outer, "ones_bf", [128, 128], BF16)
        onesf = sb_in(outer, "onesf", [128, 64], F32)
        tri = sb_in(outer, "tri", [128, 128], BF16)
        sc1 = sb_in(outer, "sc1", [128, KC], F32)
        sc2 = sb_in(outer, "sc2", [128, KC], F32)
        nbf = sb_in(outer, "nbf", [NH, 1], F32)
        zer = sb_in(outer, "zer", [128, 16], F32)
        epsb = sb_in(outer, "epsb", [128, 1], F32)
        tmpc = sb_in(outer, "tmpc", [128, KC], F32)

        def vcol(base, k):
            return vecs[:, base + k:base + k + 1]

        def phase0():
            fw = FW(nc)
            sp, act, dve, pool = fw.sp, fw.act, fw.dve, fw.pool
            tv = T(); tb = T(); tz = T(); tt = T(); ti = T()
            fw.dma(sp, vecs[:], vecs_d[:, :], "c_vecs", wr=[tv])
            fw.dma(sp, nbf[:], bf_d[:, :], "c_bf", wr=[tb])
            fw.do(dve, lambda e: e.tensor_scalar(out=nbf[:], in0=nbf[:], scalar1=-1.0, scalar2=None, op0=ALU.mult), rd=[], wr=[tb])
            fw.do(pool, lambda e: e.memset(ident[:], 1.0), wr=[ti])
            fw.do(pool, lambda e: e.affine_select(out=ident[:], in_=ident[:], pattern=[[-1, 128]], compare_op=ALU.is_equal, fill=0.0, base=0, channel_multiplier=1), wr=[ti])
            fw.do(pool, lambda e: e.memset(tri[:], 1.0), wr=[tt])
            fw.do(pool, lambda e: e.affine_select(out=tri[:], in_=tri[:], pattern=[[1, 128]], compare_op=ALU.is_ge, fill=0.0, base=0, channel_multiplier=-1), wr=[tt])
            fw.do(dve, lambda e: e.memset(ones_bf[:], 1.0))
            fw.do(dve, lambda e: e.memset(onesf[:], 1.0))
            fw.do(dve, lambda e: e.memset(epsb[:], EPS))
            fw.do(dve, lambda e: e.memset(zer[:], 0.0), wr=[tz])
            for g in range(4):
                fw.dma(sp, u_d[g, :, 0:16], zer[:, 0:16], "c_z%d" % (g % 2), rd=[tz])
            for k in range(KC):
                fw.dma(sp, xr_d[k, :, 0:3], zer[:, 0:3], "c_y%d" % (k % 2), rd=[tz])
            tc_ = T()
            fw.do(act, lambda e: e.activation(out=tmpc[:], in_=vecs[:, V_LAM:V_LAM + KC], func=AF.Exp, scale=-1.0), rd=[tv], wr=[tc_])
            fw.do(act, lambda e: e.activation(out=tmpc[:], in_=tmpc[:], func=AF.Ln, bias=1.0, scale=1.0), wr=[tc_])
            fw.do(dve, lambda e: e.tensor_scalar(out=sc1[:], in0=tmpc[:], scalar1=-8.0, scalar2=None, op0=ALU.mult), rd=[tc_])
            fw.do(dve, lambda e: e.tensor_scalar(out=sc2[:], in0=tmpc[:], scalar1=-16.0, scalar2=None, op0=ALU.mult), rd=[tc_])
            fw.finish()

        class NormCtx:
            def __init__(self, fw, st, tag):
                self.fw = fw
                self.sq = [sb_in(st, f"sq{tag}{i}", [128, TT], BF16) for i in range(3)]
                self.sqt = [T() for _ in range(3)]
                self.sqi = 0
                self.rstd = [sb_in(st, f"rstd{tag}{i}", [128, TT], F32) for i in range(2)]
                self.rstdt = [T() for _ in range(2)]
                self.ri = 0
                self.stat = ps_in(st, f"stat{tag}")
                self.statt = T()

            def square(self, src, srct):
                fw = self.fw
                i = self.sqi % 3
                self.sqi += 1
                sq = self.sq[i]
                fw.do(fw.act, lambda e: e.activation(out=sq[:], in_=src, func=AF.Square), rd=[srct], wr=[self.sqt[i]])
                return i

            def accum(self, i, first, last):
                fw = self.fw
                sq = self.sq[i]
                stat = self.stat
                tok = fw.group([lambda e: e.matmul(stat[:], lhsT=ones_bf[:], rhs=sq[:], start=first, stop=last)],
                               rd=[self.sqt[i]], wr=[self.statt] if first else [])
                if not first:
                    self.statt.w = tok

            def finish(self):
                fw = self.fw
                j = self.ri % 2
                self.ri += 1
                r = self.rstd[j]
                rt = self.rstdt[j]
                stat = self.stat
                fw.do(fw.act, lambda e: e.activation(out=r[:], in_=stat[:], func=AF.Sqrt, bias=EPS, scale=1.0 / D), rd=[self.statt], wr=[rt])
                fw.do(fw.dve, lambda e: e.reciprocal(out=r[:], in_=r[:]), wr=[rt])
                return r, rt

        def load_w(fw, dst, src, sem, tr, nsplit=1):
            kcs = dst.shape[1]
            v = src.rearrange("(kc p) m -> p kc m", p=128)
            last = {}
            for k in range(kcs):
                t = fw.dma(fw.pool, dst[:, k, :], v[:, k, :], f"{sem}{k % 4}", max_dma_last_dim=4096)
                last[t.key] = t
            return list(last.values())

        def load_wc(fw, dst, src, sem, bounds):
            v = src.rearrange("(kc p) m -> p kc m", p=128)
            toks = []
            for bi in range(len(bounds) - 1):
                toks.append(fw.dma(fw.pool, dst[:, :, bounds[bi]:bounds[bi + 1]], v[:, :, bounds[bi]:bounds[bi + 1]], f"{sem}{bi}", max_dma_last_dim=4096))
            return toks

        def blk_of(col, bounds):
            for bi in range(len(bounds) - 1):
                if col < bounds[bi + 1]:
                    return bi
            raise ValueError(col)

        def phase1(st12, vres, vrest):
            with ExitStack() as st:
                fw = FW(nc)
                pe, act, dve, pool, sp = fw.pe, fw.act, fw.dve, fw.pool, fw.sp
                w_in = sb_in(st, "w_in0", [128, KC, EIN], BF16)
                wt = T(const=True)
                WB1 = [0, 512, 1024, 1544, EIN]
                wtk = load_wc(fw, w_in, ewin_d, "w1_", WB1)
                xin = [sb_in(st, "xin0", [128, 4, D], F32)] * 2
                xint = [T()] * 2
                xT = sb_in(st, "xT1", [128, KC, TT], F32)
                xTt = [T() for _ in range(KC)]
                hT = [sb_in(st, f"hT1{i}", [128, KC, TT], BF16) for i in range(2)]
                hTt = [[T() for _ in range(KC)] for _ in range(2)]
                nrm = NormCtx(fw, st, "1")
                stg = [sb_in(st, f"stg1{i}", [128, 4, TT], BF16) for i in range(2)]
                stgt = [[T() for _ in range(4)] for _ in range(2)]
                ustg = sb_in(st, "ustg", [128, 4, TT], F32)
                ustgt = [T() for _ in range(4)]
                fe = sb_in(st, "fe", [NH, TT], F32); fet = T()
                Cc = [sb_in(st, f"Cc{i}", [NH, TT], F32) for i in range(2)]
                Cct = [T() for _ in range(2)]
                r1 = sb_in(st, "r1", [NH, TT], F32); r1t = T()
                r2 = sb_in(st, "r2", [NH, TT], F32); r2t = T()
                prt = [sb_in(st, f"prt{i}", [NH, 3, TT], BF16) for i in range(2)]
                nprt = [sb_in(st, f"nprt{i}", [NH, 3, TT], BF16) for i in range(2)]
                prtt = [T() for _ in range(2)]
                nprtt = [T() for _ in range(2)]
                ones8 = sb_in(st, "ones8", [NH, TT], BF16); ones8t = T()
                onesrow = sb_in(st, "onesrow", [NH, TT], F32); onesrowt = T()
                tp = [ps_in(st, f"tp{i}") for i in range(2)]
                tpt = [T() for _ in range(2)]
                pj = [ps_in(st, f"pj{i}") for i in range(4)]
                pjt = [T() for _ in range(4)]
                cnt = {"pj": 0, "stg": 0, "ustg": 0, "tp": 0}

                fw.do(dve, lambda e: e.memset(ones8[:], 1.0), wr=[ones8t])
                fw.do(dve, lambda e: e.memset(onesrow[:], 1.0), wr=[onesrowt])
                fw.do(dve, lambda e: e.memset(vres[:, :, :, DH:DH + 1], 1.0), wr=[vrest])

                def load_x(i):
                    s = i % 2
                    fw.dma(sp, xin[s][:], x_d[i * TT:(i + 1) * TT, :].rearrange("(s p) d -> p s d", p=128), "ldx0", wr=[xint[s]])

                def prepA(i):
                    s = i % 2
                    for k in range(KC):
                        b = cnt["tp"] % 2
                        cnt["tp"] += 1
                        fns = [(lambda e, ss=ss, k=k, b=b: e.transpose(tp[b][:, ss * 128:(ss + 1) * 128], in_=xin[s][:, ss, k * 128:(k + 1) * 128], identity=ident[:])) for ss in range(4)]
                        fw.group(fns, rd=[xint[s]], wr=[tpt[b]])
                        if k % 2 == 0:
                            fw.do(act, lambda e, k=k, b=b: e.activation(out=xT[:, k, :], in_=tp[b][:], func=AF.Copy), rd=[tpt[b]], wr=[xTt[k]])
                        else:
                            fw.do(dve, lambda e, k=k, b=b: e.tensor_copy(out=xT[:, k, :], in_=tp[b][:]), rd=[tpt[b]], wr=[xTt[k]])
                    fw.dma(pool, res_d[:, :, i * TT:(i + 1) * TT].rearrange("k p t -> p k t"), xT[:], "st_res", rd=xTt)
                    sqs = []
                    for k in range(KC):
                        sqs.append(nrm.square(xT[:, k, :], xTt[k]))
                        if k >= 1:
                            nrm.accum(sqs[k - 1], k - 1 == 0, False)
                    nrm.accum(sqs[KC - 1], False, True)

                def prepB(i):
                    s = i % 2
                    r, rt = nrm.finish()
                    for k in range(KC):
                        fw.do(dve, lambda e, k=k: e.scalar_tensor_tensor(out=hT[s][:, k, :], in0=xT[:, k, :], scalar=vcol(V_PRE, k), in1=r[:], op0=ALU.mult, op1=ALU.mult),
                              rd=[xTt[k], rt], wr=[hTt[s][k]])

                def fm_group(i, col0, M):
                    s = i % 2
                    b = cnt["pj"] % 4
                    cnt["pj"] += 1
                    fns = [(lambda e, k=k: e.matmul(pj[b][0:M, :], lhsT=w_in[:, k, col0:col0 + M], rhs=hT[s][:, k, :], start=(k == 0), stop=(k == KC - 1))) for k in range(KC)]
                    fw.group(fns, rd=hTt[s], wr=[pjt[b]], extra=[wtk[blk_of(col0, WB1)]])
                    return b

                def proj_qk(i, c, isq):
                    b = fm_group(i, (0 if isq else 512) + c * 128, 128)
                    w_ = 0 if isq else 1
                    fw.do(act, lambda e: e.activation(out=stg[w_][:, c, :], in_=pj[b][:], func=AF.Copy, scale=(0.125 if isq else 1.0)), rd=[pjt[b]], wr=[stgt[w_][c]])
                    if c == 3:
                        dst = q_d if isq else k_d
                        dv = dst[:, 0:DH, i * TT:(i + 1) * TT].rearrange("(c hh) d t -> hh d c t", hh=2)
                        for hh in range(2):
                            fw.dma(pool, dv[hh], stg[w_][hh * DH:(hh + 1) * DH, :, :], f"st_qk{w_}{hh}", rd=stgt[w_])

                def proj_v(i, ss):
                    s = i % 2
                    b = cnt["pj"] % 4
                    cnt["pj"] += 1
                    fns = [(lambda e, k=k: e.matmul(pj[b][:], lhsT=hT[s][:, k, ss * 128:(ss + 1) * 128], rhs=w_in[:, k, 1024:1536], start=(k == 0), stop=(k == KC - 1))) for k in range(KC)]
                    fw.group(fns, rd=hTt[s], wr=[pjt[b]], extra=[wtk[2]])
                    blk = i * 4 + ss
                    fw.do(dve, lambda e: e.tensor_copy(out=vres[:, blk, :, 0:DH], in_=pj[b][:].rearrange("p (h d) -> p h d", h=NH)), rd=[pjt[b]], wr=[vrest])

                def proj_f(i):
                    b = fm_group(i, 1536, NH)
                    s = i % 2
                    fw.do(act, lambda e: e.activation(out=fe[:], in_=pj[b][0:NH, :], func=AF.Exp, bias=nbf[:], scale=-1.0), rd=[pjt[b]], wr=[fet])
                    fw.do(act, lambda e: e.activation(out=fe[:], in_=fe[:], func=AF.Ln, bias=1.0, scale=1.0), wr=[fet])
                    if i == 0:
                        fw.do(dve, lambda e: e.tensor_tensor_scan(out=Cc[s][:], data0=onesrow[:], data1=fe[:], initial=0.0, op0=ALU.mult, op1=ALU.add),
                              rd=[fet, onesrowt], wr=[Cct[s]])
                    else:
                        fw.do(dve, lambda e: e.tensor_tensor_scan(out=Cc[s][:], data0=onesrow[:], data1=fe[:], initial=Cc[1 - s][:, TT - 1:TT], op0=ALU.mult, op1=ALU.add),
                              rd=[fet, onesrowt, Cct[1 - s]], wr=[Cct[s]])
                    P_, N_ = prt[s], nprt[s]
                    fw.do(dve, lambda e: e.tensor_copy(out=P_[:, 0, :], in_=Cc[s][:]), rd=[Cct[s]], wr=[prtt[s]])
                    fw.do(dve, lambda e: e.tensor_tensor(out=r1[:], in0=Cc[s][:], in1=P_[:, 0, :], op=ALU.subtract), rd=[Cct[s], prtt[s]], wr=[r1t])
                    fw.do(dve, lambda e: e.tensor_copy(out=P_[:, 1, :], in_=r1[:]), rd=[r1t], wr=[prtt[s]])
                    fw.do(dve, lambda e: e.tensor_tensor(out=r2[:], in0=r1[:], in1=P_[:, 1, :], op=ALU.subtract), rd=[r1t, prtt[s]], wr=[r2t])
                    fw.do(dve, lambda e: e.tensor_copy(out=P_[:, 2, :], in_=r2[:]), rd=[r2t], wr=[prtt[s]])
                    fw.do(dve, lambda e: e.tensor_scalar(out=N_[:], in0=P_[:], scalar1=-1.0, scalar2=None, op0=ALU.mult), rd=[prtt[s]], wr=[nprtt[s]])
                    sl = slice(i * TT, (i + 1) * TT)
                    fw.dma(pool, q_d[:, DH:DH + 3, sl], N_[:], f"st_c{s}", rd=[nprtt[s]])
                    for jj in range(3):
                        fw.dma(sp, q_d[:, DH + 3 + jj, sl], ones8[:], f"st_1{s}", rd=[ones8t])
                        fw.dma(sp, k_d[:, DH + jj, sl], ones8[:], f"st_1{s}", rd=[ones8t])
                    fw.dma(pool, k_d[:, DH + 3:DH + 6, sl], P_[:], f"st_c{s}", rd=[prtt[s]])

                def proj_u(i, g):
                    b = fm_group(i, 1544 + g * 128, 128)
                    fw.do(act, lambda e: e.activation(out=ustg[:, g, :], in_=pj[b][:], func=AF.Copy), rd=[pjt[b]], wr=[ustgt[g]])
                    if g == 3:
                        fw.dma(pool, u_d[:, :, 16 + i * TT:16 + (i + 1) * TT].rearrange("g p t -> p g t"), ustg[:], "st_u0", rd=ustgt)

                load_x(0)
                prepA(0)
                prepB(0)
                for i in range(NT):
                    if i + 1 < NT:
                        load_x(i + 1)
                    for c in range(4):
                        proj_qk(i, c, True)
                    if i + 1 < NT:
                        prepA(i + 1)
                    for c in range(4):
                        proj_qk(i, c, False)
                    for ss in range(4):
                        proj_v(i, ss)
                    if i + 1 < NT:
                        prepB(i + 1)
                    proj_f(i)
                    for g in range(4):
                        proj_u(i, g)
                fw.finish()

        def phase2(vres, vrest):
            with ExitStack() as st:
                fw = FW(nc)
                pe, act, dve, pool, sp = fw.pe, fw.act, fw.dve, fw.pool, fw.sp
                Ks = [sb_in(st, f"Ks{i}", [KA, S], BF16) for i in range(2)]
                Qs = [sb_in(st, f"Qs{i}", [KA, S], BF16) for i in range(2)]
                Kt = [T() for _ in range(2)]
                Qt = [T() for _ in range(2)]
                NSB = 4
                sbk = [ps_in(st, f"sbk{i}") for i in range(NSB)]
                sbkt = [T() for _ in range(NSB)]
                Pb = [sb_in(st, f"Pb{i}", [128, TT], BF16) for i in range(NSB)]
                Pbt = [T() for _ in range(NSB)]
                ob = [ps_in(st, f"ob{i}") for i in range(2)]
                obt = [T() for _ in range(2)]
                bc = ps_in(st, "bc"); bct = T()
                rden = sb_in(st, "rden", [128, TT], F32); rdent = T()
                bcs = sb_in(st, "bcs", [DH, TT], F32); bcst = T()
                ostg = [sb_in(st, f"ostg{i}", [DH, TT], BF16) for i in range(2)]
                ostgt = [T() for _ in range(2)]

                def load_kq(h):
                    s = h % 2
                    fw.dma(sp, Ks[s][:], k_d[h, :, :], f"ldk{s}", wr=[Kt[s]])
                    fw.dma(sp, Qs[s][:], q_d[h, :, :], f"ldq{s}", wr=[Qt[s]])

                blocks = []
                for h in range(NH):
                    for qi in range(NT):
                        nkb = 4 * (qi + 1)
                        for kb in range(nkb):
                            blocks.append((h, qi, kb, nkb))
                nblk = len(blocks)

                def s_mm(n):
                    h, qi, kb, nkb = blocks[n]
                    s = h % 2
                    j = kb - 4 * qi
                    c0 = max(j, 0) * 128
                    b = n % NSB
                    fw.group([lambda e: e.matmul(sbk[b][:, c0:TT], lhsT=Ks[s][:, kb * 128:(kb + 1) * 128], rhs=Qs[s][:, qi * TT + c0:(qi + 1) * TT], start=True, stop=True)],
                             rd=[Kt[s], Qt[s]], wr=[sbkt[b]])
                    fw.do(act, lambda e: e.activation(out=Pb[b][:, c0:TT], in_=sbk[b][:, c0:TT], func=AF.Exp), rd=[sbkt[b]], wr=[Pbt[b]])
                    if j >= 0:
                        fw.do(dve, lambda e: e.tensor_tensor(out=Pb[b][:, c0:c0 + 128], in0=Pb[b][:, c0:c0 + 128], in1=tri[:], op=ALU.mult), wr=[Pbt[b]])

                fin_q = []

                def pv_mm(n):
                    h, qi, kb, nkb = blocks[n]
                    j = kb - 4 * qi
                    c0 = max(j, 0) * 128
                    b = n % NSB
                    o = (h * NT + qi) % 2
                    tok = fw.group([lambda e: e.matmul(ob[o][0:DH + 1, c0:TT], lhsT=vres[:, kb, h, 0:DH + 1], rhs=Pb[b][:, c0:TT], start=(kb == 0), stop=(kb == nkb - 1))],
                                   rd=[Pbt[b], vrest], wr=[obt[o]] if kb == 0 else [])
                    if kb > 0:
                        obt[o].w = tok
                    if kb == nkb - 1:
                        fw.do(dve, lambda e: e.reciprocal(out=rden[DH:DH + 1, :], in_=ob[o][DH:DH + 1, :]), rd=[obt[o]], wr=[rdent])
                        fin_q.append((n + 2, h, qi, o))

                def finalize(h, qi, o):
                    fw.group([lambda e: e.matmul(bc[0:DH, :], lhsT=onesf[DH:DH + 1, 0:DH], rhs=rden[DH:DH + 1, :], start=True, stop=True)], rd=[rdent], wr=[bct])
                    fw.do(act, lambda e: e.activation(out=bcs[:], in_=bc[0:DH, :], func=AF.Copy), rd=[bct], wr=[bcst])
                    g = (h * NT + qi) % 2
                    fw.do(dve, lambda e: e.tensor_tensor(out=ostg[g][:], in0=ob[o][0:DH, :], in1=bcs[:], op=ALU.mult), rd=[obt[o], bcst], wr=[ostgt[g]])
                    fw.dma(pool, at_d[h // 2, (h % 2) * DH:(h % 2 + 1) * DH, qi * TT:(qi + 1) * TT], ostg[g][:], f"st_o{g}", rd=[ostgt[g]])

                LOOK = 3
                load_kq(0)
                if NH > 1:
                    load_kq(1)
                loaded = 2
                for n in range(min(LOOK, nblk)):
                    s_mm(n)
                for n in range(nblk):
                    pv_mm(n)
                    if n + LOOK < nblk:
                        hn = blocks[n + LOOK][0]
                        s_mm(n + LOOK)
                    while fin_q and fin_q[0][0] <= n:
                        _, h_, qi_, o_ = fin_q.pop(0)
                        finalize(h_, qi_, o_)
                    h, qi, kb, nkb = blocks[n]
                    if qi == NT - 1 and kb == nkb - 1 and h + 2 < NH:
                        load_kq(h + 2)
                while fin_q:
                    _, h_, qi_, o_ = fin_q.pop(0)
                    finalize(h_, qi_, o_)
                fw.finish()

        class Tail:
            def __init__(self, fw, st, tag, gbase):
                self.fw = fw
                self.nrm = NormCtx(fw, st, tag)
                self.m = sb_in(st, f"m{tag}", [128, KC, TT], F32)
                self.mt = [T() for _ in range(KC)]
                self.gbase = gbase
                self.pend = None

            def chunk(self, c, bank, bankt):
                fw = self.fw
                m = self.m
                fw.do(fw.act, lambda e: e.activation(out=m[:, c, :], in_=bank[:], func=AF.Copy), rd=[bankt], wr=[self.mt[c]])
                i = self.nrm.square(bank[:], bankt)
                if self.pend is not None:
                    self.nrm.accum(self.pend[0], self.pend[1] == 0, False)
                self.pend = (i, c)

            def finish(self, xT, xTt):
                fw = self.fw
                self.nrm.accum(self.pend[0], False, True)
                self.pend = None
                r, rt = self.nrm.finish()
                m = self.m
                for c in range(KC):
                    fw.do(fw.dve, lambda e, c=c: e.scalar_tensor_tensor(out=m[:, c, :], in0=m[:, c, :], scalar=vcol(self.gbase, c), in1=r[:], op0=ALU.mult, op1=ALU.mult),
                          rd=[rt], wr=[self.mt[c]])
                    fw.do(fw.dve, lambda e, c=c: e.tensor_tensor(out=xT[:, c, :], in0=xT[:, c, :], in1=m[:, c, :], op=ALU.add), rd=[self.mt[c]], wr=[xTt[c]])

        def wload_tokens(fw, dst, src, sem):
            t = T(const=True)
            return load_w(fw, dst, src, sem, t)

        def phase3():
            with ExitStack() as st:
                fw = FW(nc)
                pe, act, dve, pool, sp = fw.pe, fw.act, fw.dve, fw.pool, fw.sp
                w_out = sb_in(st, "w_out0", [128, KC, D], BF16)
                wtoks = wload_tokens(fw, w_out, ewout_d, "w3_")
                pw = sb_in(st, "pw", [128, 4, 128], BF16)
                fw.dma(pool, pw[:], epw_d.rearrange("g d e -> d g e"), "w3p")
                wtoks = wtoks + [fw.alltoks[-1]]
                xT = [sb_in(st, f"xT3{i}", [128, KC, TT], F32) for i in range(2)]
                xTt = [[T() for _ in range(KC)] for _ in range(2)]
                cat = [sb_in(st, f"cat{i}", [128, KC, TT], BF16) for i in range(2)]
                catA = [T() for _ in range(2)]
                catP = [[T() for _ in range(4)] for _ in range(2)]
                ut = [sb_in(st, f"ut{i}", [128, 4, TT + 16], F32) for i in range(2)]
                utt = [T() for _ in range(2)]
                wa = sb_in(st, "wa", [128, TT + 16], F32); wat = T()
                wb = sb_in(st, "wb", [128, TT + 16], F32); wbt = T()
                pl = [sb_in(st, f"pl{i}", [128, TT], BF16) for i in range(2)]
                plt = [T() for _ in range(2)]
                fx = sb_in(st, "fx", [128, 16], F32); fxt = T()
                tail = Tail(fw, st, "3", V_POST + 0)
                pp = [ps_in(st, f"pp{i}") for i in range(2)]
                ppt = [T() for _ in range(2)]
                po = [ps_in(st, f"po{i}") for i in range(3)]
                pot = [T() for _ in range(3)]
                cnt = {"po": 0, "pl": 0}

                def load(i):
                    s = i % 2
                    sl = slice(i * TT, (i + 1) * TT)
                    fw.dma(sp, ut[s][:], u_d[:, :, i * TT:(i + 1) * TT + 16].rearrange("g p t -> p g t"), f"ldu{s}", wr=[utt[s]])
                    fw.dma(sp, cat[s][:, 0:4, :], at_d[:, :, sl].rearrange("k p t -> p k t"), f"lda{s}", wr=[catA[s]])
                    fw.dma(sp, xT[s][:], res_d[:, :, sl].rearrange("k p t -> p k t"), f"ldx{s}", wr=xTt[s])

                def pooling(i):
                    s = i % 2
                    W = TT + 16
                    for g in range(4):
                        u = ut[s]
                        src = (lambda a, b_, u=u, g=g: u[:, g, a:b_])
                        srct = utt[s]
                        bufs = [(wa, wat), (wb, wbt)]
                        sh = 1
                        for lvl in range(g + 1):
                            dstb, dstt = bufs[lvl % 2]
                            lo = 2 * sh - 1
                            fw.do(dve, lambda e, dstb=dstb, src=src, sh=sh, lo=lo: e.tensor_tensor(out=dstb[:, lo:W], in0=src(lo, W), in1=src(lo - sh, W - sh), op=ALU.add),
                                  rd=[srct], wr=[dstt])
                            src, srct = (lambda a, b_, dstb=dstb: dstb[:, a:b_]), dstt
                            sh *= 2
                        w = 2 ** (g + 1)
                        j = cnt["pl"] % 2
                        cnt["pl"] += 1
                        fw.do(dve, lambda e, src=src, g=g, j=j, w=w: e.scalar_tensor_tensor(out=pl[j][:], in0=src(16, W), scalar=1.0 / w, in1=u[:, g, 16:W], op0=ALU.mult, op1=ALU.subtract),
                              rd=[srct, utt[s]], wr=[plt[j]])
                        if i == 0:
                            fw.do(dve, lambda e, src=src, g=g: e.tensor_tensor(out=fx[:], in0=src(16, 32), in1=vecs[:, V_INVC + 16 * g:V_INVC + 16 * (g + 1)], op=ALU.mult), rd=[srct], wr=[fxt])
                            fw.do(dve, lambda e, g=g, j=j: e.tensor_tensor(out=pl[j][:, 0:16], in0=fx[:], in1=u[:, g, 16:32], op=ALU.subtract), rd=[fxt, utt[s]], wr=[plt[j]])
                        b = g % 2
                        fw.group([lambda e, g=g, j=j, b=b: e.matmul(pp[b][:], lhsT=pw[:, g, :], rhs=pl[j][:], start=True, stop=True)], rd=[plt[j]], wr=[ppt[b]], extra=wtoks)
                        fw.do(act, lambda e, g=g, b=b: e.activation(out=cat[s][:, 4 + g, :], in_=pp[b][:], func=AF.Identity, scale=vcol(V_PSC, g)), rd=[ppt[b]], wr=[catP[s][g]])

                def outproj(i):
                    s = i % 2
                    for c in range(KC):
                        b = cnt["po"] % 3
                        cnt["po"] += 1
                        fns = [(lambda e, k=k, c=c, b=b: e.matmul(po[b][:], lhsT=w_out[:, k, c * 128:(c + 1) * 128], rhs=cat[s][:, k, :], start=(k == 0), stop=(k == KC - 1))) for k in range(KC)]
                        fw.group(fns, rd=[catA[s]] + catP[s], wr=[pot[b]], extra=wtoks)
                        tail.chunk(c, po[b], pot[b])

                def outfin(i):
                    s = i % 2
                    tail.finish(xT[s], xTt[s])
                    fw.dma(pool, res_d[:, :, i * TT:(i + 1) * TT].rearrange("k p t -> p k t"), xT[s][:], f"st_x{s}", rd=xTt[s])

                load(0)
                pooling(0)
                for i in range(NT):
                    if i + 1 < NT:
                        load(i + 1)
                    outproj(i)
                    if i + 1 < NT:
                        pooling(i + 1)
                    outfin(i)
                fw.finish()

        def phase_ffn_a(layer):
            with ExitStack() as st:
                fw = FW(nc)
                pe, act, dve, pool, sp = fw.pe, fw.act, fw.dve, fw.pool, fw.sp
                wg = sb_in(st, "wg", [128, KC, DFF], BF16)
                wu = sb_in(st, "wu", [128, KC, DFF], BF16)
                WB4 = [0, 512, 1024, 1536, 2048, 2560, DFF]
                wgk, wuk = [], []
                v_g = wg_d[layer].rearrange("(kc p) m -> p kc m", p=128)
                v_u = wu_d[layer].rearrange("(kc p) m -> p kc m", p=128)
                for bi in range(len(WB4) - 1):
                    wgk.append(fw.dma(pool, wg[:, :, WB4[bi]:WB4[bi + 1]], v_g[:, :, WB4[bi]:WB4[bi + 1]], f"w4g{bi}", max_dma_last_dim=4096))
                    wuk.append(fw.dma(pool, wu[:, :, WB4[bi]:WB4[bi + 1]], v_u[:, :, WB4[bi]:WB4[bi + 1]], f"w4u{bi}", max_dma_last_dim=4096))
                xT = [sb_in(st, f"xT4{i}", [128, KC, TT], F32) for i in range(2)]
                xTt = [[T() for _ in range(KC)] for _ in range(2)]
                hT = [sb_in(st, f"hT4{i}", [128, KC, TT], BF16) for i in range(2)]
                hTt = [[T() for _ in range(KC)] for _ in range(2)]
                nrm = NormCtx(fw, st, "4")
                sg = [sb_in(st, f"sg{i}", [128, TT], F32) for i in range(2)]
                sgt = [T() for _ in range(2)]
                astg = sb_in(st, "astg", [128, FC, TT], BF16)
                astt = [T() for _ in range(FC)]
                pg = [ps_in(st, f"pg{i}") for i in range(3)]
                pgt = [T() for _ in range(3)]
                pu = [ps_in(st, f"pu{i}") for i in range(3)]
                put = [T() for _ in range(3)]
                gb = V_FPRE + 8 * layer

                def load(i):
                    s = i % 2
                    fw.dma(sp, xT[s][:], res_d[:, :, i * TT:(i + 1) * TT].rearrange("k p t -> p k t"), f"ldx{s}", wr=xTt[s])

                def prepA(i):
                    s = i % 2
                    sqs = []
                    for k in range(KC):
                        sqs.append(nrm.square(xT[s][:, k, :], xTt[s][k]))
                        if k >= 1:
                            nrm.accum(sqs[k - 1], k - 1 == 0, False)
                    nrm.accum(sqs[KC - 1], False, True)

                def prepB(i):
                    s = i % 2
                    r, rt = nrm.finish()
                    for k in range(KC):
                        fw.do(dve, lambda e, k=k: e.scalar_tensor_tensor(out=hT[s][:, k, :], in0=xT[s][:, k, :], scalar=vcol(gb, k), in1=r[:], op0=ALU.mult, op1=ALU.mult),
                              rd=[xTt[s][k], rt], wr=[hTt[s][k]])

                n = [0]

                def chunk(i, c):
                    s = i % 2
                    b = n[0] % 3
                    j2 = n[0] % 2
                    j4 = n[0] % 4
                    n[0] += 1
                    fns = [(lambda e, k=k: e.matmul(pg[b][:], lhsT=wg[:, k, c * 128:(c + 1) * 128], rhs=hT[s][:, k, :], start=(k == 0), stop=(k == KC - 1))) for k in range(KC)]
                    fw.group(fns, rd=hTt[s], wr=[pgt[b]], extra=[wgk[blk_of(c * 128, WB4)]])
                    fns = [(lambda e, k=k: e.matmul(pu[b][:], lhsT=wu[:, k, c * 128:(c + 1) * 128], rhs=hT[s][:, k, :], start=(k == 0), stop=(k == KC - 1))) for k in range(KC)]
                    fw.group(fns, rd=hTt[s], wr=[put[b]], extra=[wuk[blk_of(c * 128, WB4)]])
                    fw.do(act, lambda e: e.activation(out=sg[j2][:], in_=pg[b][:], func=AF.Silu), rd=[pgt[b]], wr=[sgt[j2]])
                    fw.do(dve, lambda e: e.tensor_tensor(out=astg[:, c, :], in0=sg[j2][:], in1=pu[b][:], op=ALU.mult), rd=[sgt[j2], put[b]], wr=[astt[c]])
                    if c == FC // 2 - 1 or c == FC - 1:
                        c0 = 0 if c < FC - 1 else FC // 2
                        fw.dma(pool, a_d[c0:c + 1, :, i * TT:(i + 1) * TT].rearrange("c p t -> p c t"), astg[:, c0:c + 1, :], f"st_a{0 if c0 == 0 else 1}", rd=astt[c0:c + 1])

                load(0)
                if NT > 1:
                    load(1)
                prepA(0)
                prepB(0)
                for i in range(NT):
                    for c in range(FC):
                        chunk(i, c)
                        if c == 4 and i + 1 < NT:
                            prepA(i + 1)
                        if c == 12 and i + 1 < NT:
                            prepB(i + 1)
                    if i + 2 < NT:
                        load(i + 2)
                fw.finish()

        def phase_ffn_b(layer, final):
            with ExitStack() as st:
                fw = FW(nc)
                pe, act, dve, pool, sp = fw.pe, fw.act, fw.dve, fw.pool, fw.sp
                wd = sb_in(st, "wd", [128, FC, D], BF16)
                WB5 = [0, 256, 512, 768, D]
                wdk = load_wc(fw, wd, wd_d[layer], "w5d", WB5)
                xT = [sb_in(st, f"xT5{i}", [128, KC, TT], F32) for i in range(2)]
                xTt = [[T() for _ in range(KC)] for _ in range(2)]
                aT = [sb_in(st, f"aT5{i}", [128, FC, TT], BF16) for i in range(2)]
                aTt = [T() for _ in range(2)]
                tail = Tail(fw, st, "5", V_FPOST + 8 * layer)
                po = [ps_in(st, f"po5{i}") for i in range(3)]
                pot = [T() for _ in range(3)]
                cnt = {"po": 0, "tp": 0}
                if final:
                    yo = sb_in(st, "yo", [128, 4, D], F32); yot = T()
                    tp = [ps_in(st, f"tp5{i}") for i in range(2)]
                    tpt = [T() for _ in range(2)]

                def load(i):
                    s = i % 2
                    sl = slice(i * TT, (i + 1) * TT)
                    fw.dma(sp, aT[s][:], a_d[:, :, sl].rearrange("k p t -> p k t"), f"lda{s}", wr=[aTt[s]])
                    fw.dma(sp, xT[s][:], res_d[:, :, sl].rearrange("k p t -> p k t"), f"ldx{s}", wr=xTt[s])

                def body(i):
                    s = i % 2
                    for c in range(KC):
                        b = cnt["po"] % 3
                        cnt["po"] += 1
                        fns = [(lambda e, k=k, c=c, b=b: e.matmul(po[b][:], lhsT=wd[:, k, c * 128:(c + 1) * 128], rhs=aT[s][:, k, :], start=(k == 0), stop=(k == FC - 1))) for k in range(FC)]
                        fw.group(fns, rd=[aTt[s]], wr=[pot[b]], extra=[wdk[c // 2]])
                        tail.chunk(c, po[b], pot[b])
                    tail.finish(xT[s], xTt[s])
                    if not final:
                        fw.dma(pool, res_d[:, :, i * TT:(i + 1) * TT].rearrange("k p t -> p k t"), xT[s][:], f"st_x{s}", rd=xTt[s])
                    else:
                        for ss in range(4):
                            for half in range(2):
                                b = cnt["tp"] % 2
                                cnt["tp"] += 1
                                fns = [(lambda e, kk=kk, b=b, half=half, ss=ss: e.transpose(tp[b][:, kk * 128:(kk + 1) * 128], in_=xT[s][:, half * 4 + kk, ss * 128:(ss + 1) * 128], identity=ident[:])) for kk in range(4)]
                                fw.group(fns, rd=xTt[s][half * 4:half * 4 + 4], wr=[tpt[b]])
                                if half == 0:
                                    fw.do(act, lambda e, b=b, ss=ss: e.activation(out=yo[:, ss, 0:512], in_=tp[b][:], func=AF.Copy), rd=[tpt[b]], wr=[yot])
                                else:
                                    fw.do(dve, lambda e, b=b, ss=ss: e.tensor_copy(out=yo[:, ss, 512:1024], in_=tp[b][:]), rd=[tpt[b]], wr=[yot])
                        fw.dma(pool, y_d[i * TT:(i + 1) * TT, :].rearrange("(s p) d -> p s d", p=128), yo[:], "st_y", rd=[yot])

                load(0)
                for i in range(NT):
                    if i + 1 < NT:
                        load(i + 1)
                    body(i)
                fw.finish()

        def phase6():
            with ExitStack() as st:
                fw = FW(nc)
                pe, act, dve, pool, sp = fw.pe, fw.act, fw.dve, fw.pool, fw.sp
                w_in = sb_in(st, "w_in1", [128, KC, 2 * D], BF16)
                WB6 = [0, 512, 1024, 1536, 2 * D]
                w6k = load_wc(fw, w_in, owin_d, "w6_", WB6)
                xT = [sb_in(st, f"xT6{i}", [128, KC, TT], F32) for i in range(2)]
                xTt = [[T() for _ in range(KC)] for _ in range(2)]
                hT = [sb_in(st, f"hT6{i}", [128, KC, TT], BF16) for i in range(2)]
                hTt = [[T() for _ in range(KC)] for _ in range(2)]
                nrm = NormCtx(fw, st, "6")
                ogs = [sb_in(st, f"ogs{i}", [128, KC, TT], F32) for i in range(2)]
                ogst = [[T() for _ in range(KC)] for _ in range(2)]
                pj = [ps_in(st, f"pj6{i}") for i in range(4)]
                pjt = [T() for _ in range(4)]
                gb = V_PRE + 8
                n = [0]

                def load(i):
                    s = i % 2
                    fw.dma(sp, xT[s][:], res_d[:, :, i * TT:(i + 1) * TT].rearrange("k p t -> p k t"), f"ldx{s}", wr=xTt[s])

                def prepA(i):
                    s = i % 2
                    sqs = []
                    for k in range(KC):
                        sqs.append(nrm.square(xT[s][:, k, :], xTt[s][k]))
                        if k >= 1:
                            nrm.accum(sqs[k - 1], k - 1 == 0, False)
                    nrm.accum(sqs[KC - 1], False, True)

                def prepB(i):
                    s = i % 2
                    r, rt = nrm.finish()
                    for k in range(KC):
                        fw.do(dve, lambda e, k=k: e.scalar_tensor_tensor(out=hT[s][:, k, :], in0=xT[s][:, k, :], scalar=vcol(gb, k), in1=r[:], op0=ALU.mult, op1=ALU.mult),
                              rd=[xTt[s][k], rt], wr=[hTt[s][k]])

                def chunk(i, c):
                    s = i % 2
                    b = n[0] % 4
                    n[0] += 1
                    fns = [(lambda e, k=k: e.matmul(pj[b][:], lhsT=w_in[:, k, c * 128:(c + 1) * 128], rhs=hT[s][:, k, :], start=(k == 0), stop=(k == KC - 1))) for k in range(KC)]
                    fw.group(fns, rd=hTt[s], wr=[pjt[b]], extra=[w6k[c // 4]])
                    if c < KC:
                        fw.do(act, lambda e: e.activation(out=ogs[0][:, c, :], in_=pj[b][:], func=AF.Gelu_apprx_tanh), rd=[pjt[b]], wr=[ogst[0][c]])
                        if c == KC - 1:
                            fw.dma(pool, g_d[:, :, i * TT:(i + 1) * TT].rearrange("k p t -> p k t"), ogs[0][:], "st_g0", rd=ogst[0])
                    else:
                        cc = c - KC
                        fw.do(dve, lambda e: e.tensor_copy(out=ogs[1][:, cc, :], in_=pj[b][:]), rd=[pjt[b]], wr=[ogst[1][cc]])
                        if cc == KC - 1:
                            fw.dma(pool, xr_d[:, :, 3 + i * TT:3 + (i + 1) * TT].rearrange("k p t -> p k t"), ogs[1][:], "st_g1", rd=ogst[1])

                load(0)
                if NT > 1:
                    load(1)
                prepA(0)
                prepB(0)
                for i in range(NT):
                    for c in range(2 * KC):
                        chunk(i, c)
                        if c == 3 and i + 1 < NT:
                            prepA(i + 1)
                        if c == 9 and i + 1 < NT:
                            prepB(i + 1)
                    if i + 2 < NT:
                        load(i + 2)
                fw.finish()

        def phase7():
            with ExitStack() as st:
                fw = FW(nc)
                pe, act, dve, pool, sp = fw.pe, fw.act, fw.dve, fw.pool, fw.sp
                w_out = sb_in(st, "w_out1", [128, KC, D], BF16)
                wtoks = wload_tokens(fw, w_out, owout_d, "w7o")
                wa_ = sb_in(st, "w_a", [128, 4, 2, 256], BF16)
                wx_ = sb_in(st, "w_x", [128, 4, 2, 256], BF16)
                for hd in range(4):
                    fw.dma(pool, wa_[:, hd, :, :], owa_d[hd].rearrange("(kc p) e -> p kc e", p=128), f"w7a{hd}")
                    wtoks = wtoks + [fw.alltoks[-1]]
                    fw.dma(pool, wx_[:, hd, :, :], owx_d[hd].rearrange("(kc p) e -> p kc e", p=128), f"w7x{hd}")
                    wtoks = wtoks + [fw.alltoks[-1]]
                xT = [sb_in(st, "xT70", [128, KC, TT], F32)] * 2
                xTt = [[T() for _ in range(KC)]] * 2
                xr = [sb_in(st, "xr70", [128, KC, TT + 3], F32)] * 2
                xrt = [T()] * 2
                gt_ = [sb_in(st, "gt70", [128, KC, TT], F32)] * 2
                gtt = [T()] * 2
                xc2 = [sb_in(st, f"xc{i}", [128, KC, TT], F32) for i in range(2)]
                xct2 = [[T() for _ in range(KC)] for _ in range(2)]
                xcb2 = [sb_in(st, f"xcb{i}", [128, KC, TT], BF16) for i in range(2)]
                xcbt2 = [[T() for _ in range(KC)] for _ in range(2)]
                hsr = [sb_in(st, f"hsr{i}", [128, TT], F32) for i in range(2)]
                hsrt = [T() for _ in range(2)]
                carry = sb_in(st, "carry", [128, KC], F32)
                carryt = [T() for _ in range(KC)]
                zT = sb_in(st, "zT", [128, KC, TT], BF16)
                zTt = [T() for _ in range(KC)]
                NR = 4
                rr = [sb_in(st, f"rr{i}", [128, TT], F32) for i in range(NR)]; rrt = [T() for _ in range(NR)]
                ii = [sb_in(st, f"ii{i}", [128, TT], F32) for i in range(NR)]; iit = [T() for _ in range(NR)]
                aa = [sb_in(st, f"aa{i}", [128, TT], F32) for i in range(NR)]; aat = [T() for _ in range(NR)]
                bb = [sb_in(st, f"bb{i}", [128, TT], F32) for i in range(2)]; bbt = [T() for _ in range(2)]
                tail = Tail(fw, st, "7", V_POST + 8)
                pa = [ps_in(st, f"pa{i}") for i in range(2)]; pat = [T() for _ in range(2)]
                px = [ps_in(st, f"px{i}") for i in range(2)]; pxt = [T() for _ in range(2)]
                po = [ps_in(st, f"po7{i}") for i in range(3)]; pot = [T() for _ in range(3)]
                cnt = {"po": 0, "g": 0, "b": 0}

                def load_xr(i):
                    fw.dma(sp, xr[0][:], xr_d[:, :, i * TT:(i + 1) * TT + 3].rearrange("k p t -> p k t"), "ldr0", wr=[xrt[0]])

                def load_g(i):
                    fw.dma(sp, gt_[0][:], g_d[:, :, i * TT:(i + 1) * TT].rearrange("k p t -> p k t"), "ldg0", wr=[gtt[0]])

                def load_x(i):
                    fw.dma(sp, xT[0][:], res_d[:, :, i * TT:(i + 1) * TT].rearrange("k p t -> p k t"), "ldx0", wr=xTt[0])

                def conv_a(i):
                    xc, xct = xc2[i % 2], xct2[i % 2]
                    for c in range(KC):
                        fw.do(act, lambda e, c=c: e.activation(out=xc[:, c, :], in_=xr[0][:, c, 0:TT], func=AF.Identity, scale=vcol(V_CW, c), bias=vcol(V_CB, c)),
                              rd=[xrt[0]], wr=[xct[c]])

                def conv_b(i):
                    xc, xct = xc2[i % 2], xct2[i % 2]
                    for c in range(KC):
                        for j in range(1, 4):
                            fw.do(dve, lambda e, c=c, j=j: e.scalar_tensor_tensor(out=xc[:, c, :], in0=xr[0][:, c, j:j + TT], scalar=vcol(V_CW + 8 * j, c), in1=xc[:, c, :], op0=ALU.mult, op1=ALU.add),
                                  rd=[xrt[0]], wr=[xct[c]])

                def conv_c(i):
                    xc, xct = xc2[i % 2], xct2[i % 2]
                    xcb, xcbt = xcb2[i % 2], xcbt2[i % 2]
                    for c in range(KC):
                        fw.do(act, lambda e, c=c: e.activation(out=xcb[:, c, :], in_=xc[:, c, :], func=AF.Copy), rd=[xct[c]], wr=[xcbt[c]])
                        if debug:
                            fw.dma(sp, dbg["xc"][c, :, i * TT:(i + 1) * TT], xc[:, c, :], "dbg0", rd=[xct[c]])

                def recur(i):
                    s = i % 2
                    xc, xct = xc2[i % 2], xct2[i % 2]
                    xcb, xcbt = xcb2[i % 2], xcbt2[i % 2]
                    for grp in range(2):
                        cs = list(range(4 * grp, 4 * grp + 4))
                        for q_, c in enumerate(cs):
                            hd, mm = c // 2, c % 2
                            b = cnt["g"] % 2
                            cnt["g"] += 1
                            fns = [(lambda e, kk=kk, hd=hd, mm=mm, b=b: e.matmul(pa[b][:], lhsT=wa_[:, hd, kk, mm * 128:(mm + 1) * 128], rhs=xcb[:, 2 * hd + kk, :], start=(kk == 0), stop=(kk == 1))) for kk in range(2)]
                            fw.group(fns, rd=[xcbt[2 * hd], xcbt[2 * hd + 1]], wr=[pat[b]], extra=wtoks)
                            fns = [(lambda e, kk=kk, hd=hd, mm=mm, b=b: e.matmul(px[b][:], lhsT=wx_[:, hd, kk, mm * 128:(mm + 1) * 128], rhs=xcb[:, 2 * hd + kk, :], start=(kk == 0), stop=(kk == 1))) for kk in range(2)]
                            fw.group(fns, rd=[xcbt[2 * hd], xcbt[2 * hd + 1]], wr=[pxt[b]], extra=wtoks)
                            fw.do(act, lambda e, c=c, b=b, q_=q_: e.activation(out=rr[q_][:], in_=pa[b][:], func=AF.Sigmoid, bias=vcol(V_BA, c), scale=1.0), rd=[pat[b]], wr=[rrt[q_]])
                            fw.do(act, lambda e, c=c, b=b, q_=q_: e.activation(out=ii[q_][:], in_=px[b][:], func=AF.Sigmoid, bias=vcol(V_BX, c), scale=1.0), rd=[pxt[b]], wr=[iit[q_]])
                            if debug:
                                fw.dma(sp, dbg["r"][c, :, i * TT:(i + 1) * TT], rr[q_][:], "dbg5", rd=[rrt[q_]])
                                fw.dma(sp, dbg["i"][c, :, i * TT:(i + 1) * TT], ii[q_][:], "dbg6", rd=[iit[q_]])
                        for q_, c in enumerate(cs):
                            fw.do(act, lambda e, c=c, q_=q_: e.activation(out=aa[q_][:], in_=rr[q_][:], func=AF.Exp, scale=sc1[:, c:c + 1]), rd=[rrt[q_]], wr=[aat[q_]])
                            fw.do(act, lambda e, c=c, q_=q_: e.activation(out=rr[q_][:], in_=rr[q_][:], func=AF.Exp, scale=sc2[:, c:c + 1]), wr=[rrt[q_]])
                        for q_, c in enumerate(cs):
                            fw.do(act, lambda e, q_=q_: e.activation(out=rr[q_][:], in_=rr[q_][:], func=AF.Sqrt, bias=1.0, scale=-1.0), wr=[rrt[q_]])
                        for q_, c in enumerate(cs):
                            j = cnt["b"] % 2
                            cnt["b"] += 1
                            fw.do(dve, lambda e, c=c, q_=q_: e.tensor_tensor(out=ii[q_][:], in0=ii[q_][:], in1=xc[:, c, :], op=ALU.mult), rd=[xct[c]], wr=[iit[q_]])
                            fw.do(dve, lambda e, q_=q_, j=j: e.tensor_tensor(out=bb[j][:], in0=ii[q_][:], in1=rr[q_][:], op=ALU.mult), rd=[iit[q_], rrt[q_]], wr=[bbt[j]])
                            if i == 0:
                                fw.do(dve, lambda e, q_=q_, j=j: e.tensor_tensor_scan(out=hsr[j][:], data0=aa[q_][:], data1=bb[j][:], initial=0.0, op0=ALU.mult, op1=ALU.add),
                                      rd=[aat[q_], bbt[j]], wr=[hsrt[j]])
                            else:
                                fw.do(dve, lambda e, c=c, q_=q_, j=j: e.tensor_tensor_scan(out=hsr[j][:], data0=aa[q_][:], data1=bb[j][:], initial=carry[:, c:c + 1], op0=ALU.mult, op1=ALU.add),
                                      rd=[aat[q_], bbt[j], carryt[c]], wr=[hsrt[j]])
                            fw.do(dve, lambda e, c=c, j=j: e.tensor_copy(out=carry[:, c:c + 1], in_=hsr[j][:, TT - 1:TT]), rd=[hsrt[j]], wr=[carryt[c]])
                            fw.do(dve, lambda e, c=c, j=j: e.tensor_tensor(out=zT[:, c, :], in0=hsr[j][:], in1=gt_[0][:, c, :], op=ALU.mult), rd=[hsrt[j], gtt[0]], wr=[zTt[c]])
                            if debug:
                                fw.dma(sp, dbg["a"][c, :, i * TT:(i + 1) * TT], aa[q_][:], "dbg1", rd=[aat[q_]])
                                fw.dma(sp, dbg["b"][c, :, i * TT:(i + 1) * TT], bb[j][:], "dbg2", rd=[bbt[j]])
                                fw.dma(sp, dbg["hs"][c, :, i * TT:(i + 1) * TT], hsr[j][:], "dbg3", rd=[hsrt[j]])

                def outproj(i):
                    for c in range(KC):
                        b = cnt["po"] % 3
                        cnt["po"] += 1
                        fns = [(lambda e, k=k, c=c, b=b: e.matmul(po[b][:], lhsT=w_out[:, k, c * 128:(c + 1) * 128], rhs=zT[:, k, :], start=(k == 0), stop=(k == KC - 1))) for k in range(KC)]
                        fw.group(fns, rd=zTt, wr=[pot[b]], extra=wtoks)
                        tail.chunk(c, po[b], pot[b])

                def outfin(i):
                    tail.finish(xT[0], xTt[0])
                    fw.dma(pool, res_d[:, :, i * TT:(i + 1) * TT].rearrange("k p t -> p k t"), xT[0][:], "st_x0", rd=xTt[0])

                load_xr(0)
                load_g(0)
                load_x(0)
                conv_a(0)
                conv_b(0)
                conv_c(0)
                for i in range(NT):
                    if i + 1 < NT:
                        load_xr(i + 1)
                    recur(i)
                    if i + 1 < NT:
                        conv_a(i + 1)
                        conv_b(i + 1)
                        load_g(i + 1)
                    outproj(i)
                    if i + 1 < NT:
                        conv_c(i + 1)
                    outfin(i)
                    if i + 1 < NT:
                        load_x(i + 1)
                fw.finish()

        phase0()
        if nphase >= 1:
            with ExitStack() as st12:
                vres = sb_in(st12, "vres", [128, NB, NH, DH + 1], BF16)
                vrest = T()
                phase1(st12, vres, vrest)
                if nphase >= 2:
                    vrest2 = T(const=True)
                    phase2(vres, vrest2)
        if nphase >= 3:
            phase3()
        if nphase >= 4:
            phase_ffn_a(0)
        if nphase >= 5:
            phase_ffn_b(0, False)
        if nphase >= 6:
            phase6()
        if nphase >= 7:
            phase7()
        if nphase >= 8:
            phase_ffn_a(1)
        if nphase >= 9:
            phase_ffn_b(1, True)
    return nc


def _cm(v):
    return np.ascontiguousarray(np.asarray(v, np.float32).reshape(-1, 128).T)


def pack_vecs(inp):
    vecs = np.zeros((128, NV), np.float32)
    for l in range(2):
        vecs[:, V_PRE + 8 * l:V_PRE + 8 * l + 8] = _cm(inp["mix_pre_g"][l])
        vecs[:, V_POST + 8 * l:V_POST + 8 * l + 8] = _cm(inp["mix_post_g"][l])
        vecs[:, V_FPRE + 8 * l:V_FPRE + 8 * l + 8] = _cm(inp["ffn_pre_g"][l])
        vecs[:, V_FPOST + 8 * l:V_FPOST + 8 * l + 8] = _cm(inp["ffn_post_g"][l])
    vecs[:, V_PSC:V_PSC + 4] = _cm(inp["ev_pool_scale"][0])
    for j in range(4):
        vecs[:, V_CW + 8 * j:V_CW + 8 * j + 8] = _cm(inp["od_conv_w"][0, j])
    vecs[:, V_CB:V_CB + 8] = _cm(inp["od_conv_b"][0])
    vecs[:, V_BA:V_BA + 8] = _cm(inp["od_b_a"][0])
    vecs[:, V_BX:V_BX + 8] = _cm(inp["od_b_x"][0])
    vecs[:, V_LAM:V_LAM + 8] = _cm(inp["od_lam"][0])
    for g, w in enumerate((2, 4, 8, 16)):
        for t in range(16):
            vecs[:, V_INVC + 16 * g + t] = 1.0 / min(t + 1, w)
    return vecs


def make_in_map(inp, xb):
    f = lambda a: np.ascontiguousarray(np.asarray(a, np.float32))
    return {
        "x": f(xb),
        "vecs": pack_vecs(inp),
        "b_f": f(np.asarray(inp["ev_b_f"])[0].reshape(NH, 1)),
        "ffn_w_gate": f(inp["ffn_w_gate"]),
        "ffn_w_up": f(inp["ffn_w_up"]),
        "ffn_w_down": f(inp["ffn_w_down"]),
        "ev_w_in": f(np.asarray(inp["ev_w_in"])[0]),
        "ev_pool_w": f(np.asarray(inp["ev_pool_w"])[0]),
        "ev_w_out": f(np.asarray(inp["ev_w_out"])[0]),
        "od_w_in": f(np.asarray(inp["od_w_in"])[0]),
        "od_w_a": f(np.asarray(inp["od_w_a"])[0]),
        "od_w_x": f(np.asarray(inp["od_w_x"])[0]),
        "od_w_out": f(np.asarray(inp["od_w_out"])[0]),
    }


def kernel(**inputs):
    x = np.asarray(inputs["x"], np.float32)
    B, S, _ = x.shape
    nc = build(S)
    work = [0, 1, 4, 5][:B]
    real = [make_in_map(inputs, x[b]) for b in range(B)]
    zero = {k: np.zeros_like(v) for k, v in real[0].items()}
    in_maps = [zero] * 8
    in_maps = list(in_maps)
    for b, c in enumerate(work):
        in_maps[c] = real[b]
    res = run_bass_kernel_spmd(nc, in_maps, core_ids=list(range(8)))
    return np.stack([np.asarray(res.results[c]["y"], np.float32) for c in work], axis=0)
```

```python
import numpy as np
from contextlib import ExitStack
import concourse.bass as bass
import concourse.mybir as mybir
from concourse.bass_utils import run_bass_kernel_spmd

F32 = mybir.dt.float32
BF16 = mybir.dt.bfloat16
AF = mybir.ActivationFunctionType
ALU = mybir.AluOpType

D = 1024
KC = 8
TT = 512
DFF = 2816
FC = 22
NH = 8
DH = 64
EIN = 2056
KA = 70
EPS = 1e-6

V_PRE, V_POST, V_FPRE, V_FPOST = 0, 16, 32, 48
V_PSC = 64
V_CW = 68
V_CB, V_BA, V_BX, V_LAM = 100, 108, 116, 124
V_INVC = 132
NV = 196


class Tok:
    __slots__ = ("key", "val")

    def __init__(self, key, val):
        self.key = key
        self.val = val


class Eng:
    def __init__(self, fw, name):
        self.fw = fw
        self.name = name
        self.ops = []
        self.cnt = 0
        self.waited = {}

    def wait(self, deps):
        for t in deps:
            if t is None:
                continue
            if self.waited.get(t.key, 0) < t.val:
                self.waited[t.key] = t.val
                self.ops.append(("wait", t.key, t.val))

    def op(self, fn, deps=(), mark=True):
        self.wait(deps)
        if mark:
            self.cnt += 1
            self.ops.append(("op", fn, self.name, 1))
            return Tok(self.name, self.cnt)
        self.ops.append(("op", fn, None, 0))
        return None

    def dma(self, out, in_, slot, deps=(), **kw):
        self.wait(deps)
        fw = self.fw
        fw.dma_cnt[slot] = fw.dma_cnt.get(slot, 0) + 16
        self.ops.append(("op", lambda e: e.dma_start(out=out, in_=in_, **kw), slot, 16))
        return Tok(slot, fw.dma_cnt[slot])


class T:
    __slots__ = ("w", "r", "const")

    def __init__(self, const=False):
        self.w = None
        self.r = {}
        self.const = const


class FW:
    uid = 0

    def __init__(self, nc):
        self.nc = nc
        self.dma_cnt = {}
        self.pe = Eng(self, "pe")
        self.act = Eng(self, "act")
        self.dve = Eng(self, "dve")
        self.pool = Eng(self, "pool")
        self.sp = Eng(self, "sp")
        self.engs = [self.pe, self.act, self.dve, self.pool, self.sp]
        self.alltoks = []

    @staticmethod
    def _deps(rd, wr, extra):
        deps = []
        for b in rd:
            deps.append(b.w)
        for b in wr:
            deps.append(b.w)
            deps.extend(b.r.values())
        deps.extend(extra)
        return deps

    @staticmethod
    def _upd(tok, rd, wr):
        for b in rd:
            if not b.const:
                b.r[tok.key] = tok
        for b in wr:
            b.w = tok
            b.r = {}
        return tok

    def do(self, eng, fn, rd=(), wr=(), extra=()):
        tok = eng.op(fn, self._deps(rd, wr, extra), True)
        return self._upd(tok, rd, wr)

    def group(self, fns, rd=(), wr=(), extra=()):
        pe = self.pe
        pe.wait(self._deps(rd, wr, extra))
        for fn in fns[:-1]:
            pe.op(fn, (), False)
        tok = pe.op(fns[-1], (), True)
        return self._upd(tok, rd, wr)

    def dma(self, eng, out, in_, sem, rd=(), wr=(), extra=(), **kw):
        tok = eng.dma(out, in_, sem, self._deps(rd, wr, extra), **kw)
        self.alltoks.append(tok)
        return self._upd(tok, rd, wr)

    def finish(self):
        mx = {}
        for t in self.alltoks:
            if t.key not in mx or mx[t.key].val < t.val:
                mx[t.key] = t
        self.sp.wait(list(mx.values()))
        self.sp.wait([Tok(e.name, e.cnt) for e in self.engs if e.cnt > 0 and e is not self.sp])
        self.emit()

    def emit(self):
        nc = self.nc
        keys = [e.name for e in self.engs] + list(self.dma_cnt.keys())
        FW.uid += 1
        sems = {k: nc.alloc_semaphore(name=f"s{FW.uid}_{k}") for k in keys}
        with nc.Block() as block:

            def run(eng, h):
                for o in eng.ops:
                    if o[0] == "wait":
                        h.wait_ge(sems[o[1]], o[2])
                    else:
                        ins = o[1](h)
                        if o[2] is not None:
                            ins.then_inc(sems[o[2]], o[3])

            @block.tensor
            def _(h):
                run(self.pe, h)

            @block.scalar
            def _(h):
                run(self.act, h)

            @block.vector
            def _(h):
                run(self.dve, h)

            @block.gpsimd
            def _(h):
                run(self.pool, h)

            @block.sync
            def _(h):
                run(self.sp, h)

        nc.clear_and_free_semaphores(list(sems.values()))
        nc.all_engine_barrier()


def build(S, nphase=99, debug=False):
    NT = S // TT
    NB = S // 128
    nc = bass.Bass("TRN2", target_bir_lowering=False)

    def din(name, shape):
        return nc.dram_tensor(name, list(shape), F32, kind="ExternalInput").ap()

    x_d = din("x", [S, D])
    vecs_d = din("vecs", [128, NV])
    bf_d = din("b_f", [NH, 1])
    wg_d = din("ffn_w_gate", [2, D, DFF])
    wu_d = din("ffn_w_up", [2, D, DFF])
    wd_d = din("ffn_w_down", [2, DFF, D])
    ewin_d = din("ev_w_in", [D, EIN])
    epw_d = din("ev_pool_w", [4, 128, 128])
    ewout_d = din("ev_w_out", [D, D])
    owin_d = din("od_w_in", [D, 2 * D])
    owa_d = din("od_w_a", [4, 256, 256])
    owx_d = din("od_w_x", [4, 256, 256])
    owout_d = din("od_w_out", [D, D])
    y_d = nc.dram_tensor("y", [S, D], F32, kind="ExternalOutput").ap()

    skind = "ExternalOutput" if debug else "Internal"

    def dscr(name, shape, dt):
        return nc.dram_tensor(name, list(shape), dt, kind=skind).ap()

    res_d = dscr("res", [KC, 128, S], F32)
    q_d = dscr("qaug", [NH, KA, S], BF16)
    k_d = dscr("kaug", [NH, KA, S], BF16)
    at_d = dscr("attnT", [4, 128, S], BF16)
    u_d = dscr("uT", [4, 128, S + 16], F32)
    a_d = dscr("actT", [FC, 128, S], BF16)
    g_d = dscr("gT", [KC, 128, S], F32)
    xr_d = dscr("xrT", [KC, 128, S + 3], F32)
    if debug:
        dbg = {n: dscr("dbg_" + n, [KC, 128, S], F32) for n in ("xc", "a", "b", "hs", "z", "r", "i")}

    outer = ExitStack()
    with outer:
        nuid = [0]

        def sb_in(st, name, shape, dt):
            nuid[0] += 1
            return st.enter_context(nc.sbuf_tensor(f"sb{nuid[0]}_{name}", list(shape), dt))

        def ps_in(st, name):
            nuid[0] += 1
            return st.enter_context(nc.psum_tensor(f"ps{nuid[0]}_{name}", [128, TT], F32))

        vecs = sb_in(outer, "vecs", [128, NV], F32)
        ident = sb_in(outer, "ident", [128, 128], F32)
        ones_bf = sb_in(outer, "ones_bf", [128, 128], BF16)
        onesf = sb_in(outer, "onesf", [128, 64], F32)
        tri = sb_in(outer, "tri", [128, 128], BF16)
        sc1 = sb_in(outer, "sc1", [128, KC], F32)
        sc2 = sb_in(outer, "sc2", [128, KC], F32)
        nbf = sb_in(outer, "nbf", [NH, 1], F32)
        zer = sb_in(outer, "zer", [128, 16], F32)
        epsb = sb_in(outer, "epsb", [128, 1], F32)
        tmpc = sb_in(outer, "tmpc", [128, KC], F32)

        def vcol(base, k):
            return vecs[:, base + k:base + k + 1]

        def phase0():
            fw = FW(nc)
            sp, act, dve, pool = fw.sp, fw.act, fw.dve, fw.pool
            tv = T(); tb = T(); tz = T(); tt = T(); ti = T()
            fw.dma(sp, vecs[:], vecs_d[:, :], "c_vecs", wr=[tv])
            fw.dma(sp, nbf[:], bf_d[:, :], "c_bf", wr=[tb])
            fw.do(dve, lambda e: e.tensor_scalar(out=nbf[:], in0=nbf[:], scalar1=-1.0, scalar2=None, op0=ALU.mult), rd=[], wr=[tb])
            fw.do(pool, lambda e: e.memset(ident[:], 1.0), wr=[ti])
            fw.do(pool, lambda e: e.affine_select(out=ident[:], in_=ident[:], pattern=[[-1, 128]], compare_op=ALU.is_equal, fill=0.0, base=0, channel_multiplier=1), wr=[ti])
            fw.do(pool, lambda e: e.memset(tri[:], 1.0), wr=[tt])
            fw.do(pool, lambda e: e.affine_select(out=tri[:], in_=tri[:], pattern=[[1, 128]], compare_op=ALU.is_ge, fill=0.0, base=0, channel_multiplier=-1), wr=[tt])
            fw.do(dve, lambda e: e.memset(ones_bf[:], 1.0))
            fw.do(dve, lambda e: e.memset(onesf[:], 1.0))
            fw.do(dve, lambda e: e.memset(epsb[:], EPS))
            fw.do(dve, lambda e: e.memset(zer[:], 0.0), wr=[tz])
            for g in range(4):
                fw.dma(sp, u_d[g, :, 0:16], zer[:, 0:16], "c_z%d" % (g % 2), rd=[tz])
            for k in range(KC):
                fw.dma(sp, xr_d[k, :, 0:3], zer[:, 0:3], "c_y%d" % (k % 2), rd=[tz])
            tc_ = T()
            fw.do(act, lambda e: e.activation(out=tmpc[:], in_=vecs[:, V_LAM:V_LAM + KC], func=AF.Exp, scale=-1.0), rd=[tv], wr=[tc_])
            fw.do(act, lambda e: e.activation(out=tmpc[:], in_=tmpc[:], func=AF.Ln, bias=1.0, scale=1.0), wr=[tc_])
            fw.do(dve, lambda e: e.tensor_scalar(out=sc1[:], in0=tmpc[:], scalar1=-8.0, scalar2=None, op0=ALU.mult), rd=[tc_])
            fw.do(dve, lambda e: e.tensor_scalar(out=sc2[:], in0=tmpc[:], scalar1=-16.0, scalar2=None, op0=ALU.mult), rd=[tc_])
            fw.finish()

        class NormCtx:
            def __init__(self, fw, st, tag):
                self.fw = fw
                self.sq = [sb_in(st, f"sq{tag}{i}", [128, TT], BF16) for i in range(3)]
                self.sqt = [T() for _ in range(3)]
                self.sqi = 0
                self.rstd = [sb_in(st, f"rstd{tag}{i}", [128, TT], F32) for i in range(2)]
                self.rstdt = [T() for _ in range(2)]
                self.ri = 0
                self.stat = ps_in(st, f"stat{tag}")
                self.statt = T()

            def square(self, src, srct):
                fw = self.fw
                i = self.sqi % 3
                self.sqi += 1
                sq = self.sq[i]
                fw.do(fw.act, lambda e: e.activation(out=sq[:], in_=src, func=AF.Square), rd=[srct], wr=[self.sqt[i]])
                return i

            def accum(self, i, first, last):
                fw = self.fw
                sq = self.sq[i]
                stat = self.stat
                tok = fw.group([lambda e: e.matmul(stat[:], lhsT=ones_bf[:], rhs=sq[:], start=first, stop=last)],
                               rd=[self.sqt[i]], wr=[self.statt] if first else [])
                if not first:
                    self.statt.w = tok

            def finish(self):
                fw = self.fw
                j = self.ri % 2
                self.ri += 1
                r = self.rstd[j]
                rt = self.rstdt[j]
                stat = self.stat
                fw.do(fw.act, lambda e: e.activation(out=r[:], in_=stat[:], func=AF.Sqrt, bias=EPS, scale=1.0 / D), rd=[self.statt], wr=[rt])
                fw.do(fw.dve, lambda e: e.reciprocal(out=r[:], in_=r[:]), wr=[rt])
                return r, rt

        def load_w(fw, dst, src, sem, tr, nsplit=1):
            kcs = dst.shape[1]
            v = src.rearrange("(kc p) m -> p kc m", p=128)
            last = {}
            for k in range(kcs):
                t = fw.dma(fw.pool, dst[:, k, :], v[:, k, :], f"{sem}{k % 4}", max_dma_last_dim=4096)
                last[t.key] = t
            return list(last.values())

        def load_wc(fw, dst, src, sem, bounds):
            v = src.rearrange("(kc p) m -> p kc m", p=128)
            toks = []
            for bi in range(len(bounds) - 1):
                toks.append(fw.dma(fw.pool, dst[:, :, bounds[bi]:bounds[bi + 1]], v[:, :, bounds[bi]:bounds[bi + 1]], f"{sem}{bi}", max_dma_last_dim=4096))
            return toks

        def blk_of(col, bounds):
            for bi in range(len(bounds) - 1):
                if col < bounds[bi + 1]:
                    return bi
            raise ValueError(col)

        def phase1(st12, vres, vrest):
            with ExitStack() as st:
                fw = FW(nc)
                pe, act, dve, pool, sp = fw.pe, fw.act, fw.dve, fw.pool, fw.sp
                w_in = sb_in(st, "w_in0", [128, KC, EIN], BF16)
                wt = T(const=True)
                WB1 = [0, 512, 1024, 1544, EIN]
                wtk = load_wc(fw, w_in, ewin_d, "w1_", WB1)
                xin = [sb_in(st, "xin0", [128, 4, D], F32)] * 2
                xint = [T()] * 2
                xT = sb_in(st, "xT1", [128, KC, TT], F32)
                xTt = [T() for _ in range(KC)]
                hT = [sb_in(st, f"hT1{i}", [128, KC, TT], BF16) for i in range(2)]
                hTt = [[T() for _ in range(KC)] for _ in range(2)]
                nrm = NormCtx(fw, st, "1")
                stg = [sb_in(st, f"stg1{i}", [128, 4, TT], BF16) for i in range(2)]
                stgt = [[T() for _ in range(4)] for _ in range(2)]
                ustg = sb_in(st, "ustg", [128, 4, TT], F32)
                ustgt = [T() for _ in range(4)]
                fe = sb_in(st, "fe", [NH, TT], F32); fet = T()
                Cc = [sb_in(st, f"Cc{i}", [NH, TT], F32) for i in range(2)]
                Cct = [T() for _ in range(2)]
                r1 = sb_in(st, "r1", [NH, TT], F32); r1t = T()
                r2 = sb_in(st, "r2", [NH, TT], F32); r2t = T()
                prt = [sb_in(st, f"prt{i}", [NH, 3, TT], BF16) for i in range(2)]
                nprt = [sb_in(st, f"nprt{i}", [NH, 3, TT], BF16) for i in range(2)]
                prtt = [T() for _ in range(2)]
                nprtt = [T() for _ in range(2)]
                ones8 = sb_in(st, "ones8", [NH, TT], BF16); ones8t = T()
                onesrow = sb_in(st, "onesrow", [NH, TT], F32); onesrowt = T()
                tp = [ps_in(st, f"tp{i}") for i in range(2)]
                tpt = [T() for _ in range(2)]
                pj = [ps_in(st, f"pj{i}") for i in range(4)]
                pjt = [T() for _ in range(4)]
                cnt = {"pj": 0, "stg": 0, "ustg": 0, "tp": 0}

                fw.do(dve, lambda e: e.memset(ones8[:], 1.0), wr=[ones8t])
                fw.do(dve, lambda e: e.memset(onesrow[:], 1.0), wr=[onesrowt])
                fw.do(dve, lambda e: e.memset(vres[:, :, :, DH:DH + 1], 1.0), wr=[vrest])

                def load_x(i):
                    s = i % 2
                    fw.dma(sp, xin[s][:], x_d[i * TT:(i + 1) * TT, :].rearrange("(s p) d -> p s d", p=128), "ldx0", wr=[xint[s]])

                def prepA(i):
                    s = i % 2
                    for k in range(KC):
                        b = cnt["tp"] % 2
                        cnt["tp"] += 1
                        fns = [(lambda e, ss=ss, k=k, b=b: e.transpose(tp[b][:, ss * 128:(ss + 1) * 128], in_=xin[s][:, ss, k * 128:(k + 1) * 128], identity=ident[:])) for ss in range(4)]
                        fw.group(fns, rd=[xint[s]], wr=[tpt[b]])
                        if k % 2 == 0:
                            fw.do(act, lambda e, k=k, b=b: e.activation(out=xT[:, k, :], in_=tp[b][:], func=AF.Copy), rd=[tpt[b]], wr=[xTt[k]])
                        else:
                            fw.do(dve, lambda e, k=k, b=b: e.tensor_copy(out=xT[:, k, :], in_=tp[b][:]), rd=[tpt[b]], wr=[xTt[k]])
                    fw.dma(pool, res_d[:, :, i * TT:(i + 1) * TT].rearrange("k p t -> p k t"), xT[:], "st_res", rd=xTt)
                    sqs = []
                    for k in range(KC):
                        sqs.append(nrm.square(xT[:, k, :], xTt[k]))
                        if k >= 1:
                            nrm.accum(sqs[k - 1], k - 1 == 0, False)
                    nrm.accum(sqs[KC - 1], False, True)

                def prepB(i):
                    s = i % 2
                    r, rt = nrm.finish()
                    for k in range(KC):
                        fw.do(dve, lambda e, k=k: e.scalar_tensor_tensor(out=hT[s][:, k, :], in0=xT[:, k, :], scalar=vcol(V_PRE, k), in1=r[:], op0=ALU.mult, op1=ALU.mult),
                              rd=[xTt[k], rt], wr=[hTt[s][k]])

                def fm_group(i, col0, M):
                    s = i % 2
                    b = cnt["pj"] % 4
                    cnt["pj"] += 1
                    fns = [(lambda e, k=k: e.matmul(pj[b][0:M, :], lhsT=w_in[:, k, col0:col0 + M], rhs=hT[s][:, k, :], start=(k == 0), stop=(k == KC - 1))) for k in range(KC)]
                    fw.group(fns, rd=hTt[s], wr=[pjt[b]], extra=[wtk[blk_of(col0, WB1)]])
                    return b

                def proj_qk(i, c, isq):
                    b = fm_group(i, (0 if isq else 512) + c * 128, 128)
                    w_ = 0 if isq else 1
                    fw.do(act, lambda e: e.activation(out=stg[w_][:, c, :], in_=pj[b][:], func=AF.Copy, scale=(0.125 if isq else 1.0)), rd=[pjt[b]], wr=[stgt[w_][c]])
                    if c == 3:
                        dst = q_d if isq else k_d
                        dv = dst[:, 0:DH, i * TT:(i + 1) * TT].rearrange("(c hh) d t -> hh d c t", hh=2)
                        for hh in range(2):
                            fw.dma(pool, dv[hh], stg[w_][hh * DH:(hh + 1) * DH, :, :], f"st_qk{w_}{hh}", rd=stgt[w_])

                def proj_v(i, ss):
                    s = i % 2
                    b = cnt["pj"] % 4
                    cnt["pj"] += 1
                    fns = [(lambda e, k=k: e.matmul(pj[b][:], lhsT=hT[s][:, k, ss * 128:(ss + 1) * 128], rhs=w_in[:, k, 1024:1536], start=(k == 0), stop=(k == KC - 1))) for k in range(KC)]
                    fw.group(fns, rd=hTt[s], wr=[pjt[b]], extra=[wtk[2]])
                    blk = i * 4 + ss
                    fw.do(dve, lambda e: e.tensor_copy(out=vres[:, blk, :, 0:DH], in_=pj[b][:].rearrange("p (h d) -> p h d", h=NH)), rd=[pjt[b]], wr=[vrest])

                def proj_f(i):
                    b = fm_group(i, 1536, NH)
                    s = i % 2
                    fw.do(act, lambda e: e.activation(out=fe[:], in_=pj[b][0:NH, :], func=AF.Exp, bias=nbf[:], scale=-1.0), rd=[pjt[b]], wr=[fet])
                    fw.do(act, lambda e: e.activation(out=fe[:], in_=fe[:], func=AF.Ln, bias=1.0, scale=1.0), wr=[fet])
                    if i == 0:
                        fw.do(dve, lambda e: e.tensor_tensor_scan(out=Cc[s][:], data0=onesrow[:], data1=fe[:], initial=0.0, op0=ALU.mult, op1=ALU.add),
                              rd=[fet, onesrowt], wr=[Cct[s]])
                    else:
                        fw.do(dve, lambda e: e.tensor_tensor_scan(out=Cc[s][:], data0=onesrow[:], data1=fe[:], initial=Cc[1 - s][:, TT - 1:TT], op0=ALU.mult, op1=ALU.add),
                              rd=[fet, onesrowt, Cct[1 - s]], wr=[Cct[s]])
                    P_, N_ = prt[s], nprt[s]
                    fw.do(dve, lambda e: e.tensor_copy(out=P_[:, 0, :], in_=Cc[s][:]), rd=[Cct[s]], wr=[prtt[s]])
                    fw.do(dve, lambda e: e.tensor_tensor(out=r1[:], in0=Cc[s][:], in1=P_[:, 0, :], op=ALU.subtract), rd=[Cct[s], prtt[s]], wr=[r1t])
                    fw.do(dve, lambda e: e.tensor_copy(out=P_[:, 1, :], in_=r1[:]), rd=[r1t], wr=[prtt[s]])
                    fw.do(dve, lambda e: e.tensor_tensor(out=r2[:], in0=r1[:], in1=P_[:, 1, :], op=ALU.subtract), rd=[r1t, prtt[s]], wr=[r2t])
                    fw.do(dve, lambda e: e.tensor_copy(out=P_[:, 2, :], in_=r2[:]), rd=[r2t], wr=[prtt[s]])
                    fw.do(dve, lambda e: e.tensor_scalar(out=N_[:], in0=P_[:], scalar1=-1.0, scalar2=None, op0=ALU.mult), rd=[prtt[s]], wr=[nprtt[s]])
                    sl = slice(i * TT, (i + 1) * TT)
                    fw.dma(pool, q_d[:, DH:DH + 3, sl], N_[:], f"st_c{s}", rd=[nprtt[s]])
                    for jj in range(3):
                        fw.dma(sp, q_d[:, DH + 3 + jj, sl], ones8[:], f"st_1{s}", rd=[ones8t])
                        fw.dma(sp, k_d[:, DH + jj, sl], ones8[:], f"st_1{s}", rd=[ones8t])
                    fw.dma(pool, k_d[:, DH + 3:DH + 6, sl], P_[:], f"st_c{s}", rd=[prtt[s]])

                def proj_u(i, g):
                    b = fm_group(i, 1544 + g * 128, 128)
                    fw.do(act, lambda e: e.activation(out=ustg[:, g, :], in_=pj[b][:], func=AF.Copy), rd=[pjt[b]], wr=[ustgt[g]])
                    if g == 3:
                        fw.dma(pool, u_d[:, :, 16 + i * TT:16 + (i + 1) * TT].rearrange("g p t -> p g t"), ustg[:], "st_u0", rd=ustgt)

                load_x(0)
                prepA(0)
                prepB(0)
                for i in range(NT):
                    if i + 1 < NT:
                        load_x(i + 1)
                    for c in range(4):
                        proj_qk(i, c, True)
                    if i + 1 < NT:
                        prepA(i + 1)
                    for c in range(4):
                        proj_qk(i, c, False)
                    for ss in range(4):
                        proj_v(i, ss)
                    if i + 1 < NT:
                        prepB(i + 1)
                    proj_f(i)
                    for g in range(4):
                        proj_u(i, g)
                fw.finish()

        def phase2(vres, vrest):
            with ExitStack() as st:
                fw = FW(nc)
                pe, act, dve, pool, sp = fw.pe, fw.act, fw.dve, fw.pool, fw.sp
                Ks = [sb_in(st, f"Ks{i}", [KA, S], BF16) for i in range(2)]
                Qs = [sb_in(st, f"Qs{i}", [KA, S], BF16) for i in range(2)]
                Kt = [T() for _ in range(2)]
                Qt = [T() for _ in range(2)]
                NSB = 5
                sbk = [ps_in(st, f"sbk{i}") for i in range(NSB)]
                sbkt = [T() for _ in range(NSB)]
                Pb = [sb_in(st, f"Pb{i}", [128, TT], BF16) for i in range(NSB)]
                Pbt = [T() for _ in range(NSB)]
                ob = [ps_in(st, f"ob{i}") for i in range(2)]
                obt = [T() for _ in range(2)]
                bc = ps_in(st, "bc"); bct = T()
                rden = sb_in(st, "rden", [128, TT], F32); rdent = T()
                bcs = sb_in(st, "bcs", [DH, TT], F32); bcst = T()
                ostg = [sb_in(st, f"ostg{i}", [DH, TT], BF16) for i in range(2)]
                ostgt = [T() for _ in range(2)]

                def load_kq(h):
                    s = h % 2
                    fw.dma(sp, Ks[s][:], k_d[h, :, :], f"ldk{s}", wr=[Kt[s]])
                    fw.dma(sp, Qs[s][:], q_d[h, :, :], f"ldq{s}", wr=[Qt[s]])

                blocks = []
                for h in range(NH):
                    for qi in range(NT):
                        nkb = 4 * (qi + 1)
                        for kb in range(nkb):
                            blocks.append((h, qi, kb, nkb))
                nblk = len(blocks)

                def s_mm(n):
                    h, qi, kb, nkb = blocks[n]
                    s = h % 2
                    j = kb - 4 * qi
                    c0 = max(j, 0) * 128
                    b = n % NSB
                    fw.group([lambda e: e.matmul(sbk[b][:, c0:TT], lhsT=Ks[s][:, kb * 128:(kb + 1) * 128], rhs=Qs[s][:, qi * TT + c0:(qi + 1) * TT], start=True, stop=True)],
                             rd=[Kt[s], Qt[s]], wr=[sbkt[b]])
                    fw.do(act, lambda e: e.activation(out=Pb[b][:, c0:TT], in_=sbk[b][:, c0:TT], func=AF.Exp), rd=[sbkt[b]], wr=[Pbt[b]])
                    if j >= 0:
                        fw.do(dve, lambda e: e.tensor_tensor(out=Pb[b][:, c0:c0 + 128], in0=Pb[b][:, c0:c0 + 128], in1=tri[:], op=ALU.mult), wr=[Pbt[b]])

                fin_q = []

                def pv_mm(n):
                    h, qi, kb, nkb = blocks[n]
                    j = kb - 4 * qi
                    c0 = max(j, 0) * 128
                    b = n % NSB
                    o = (h * NT + qi) % 2
                    tok = fw.group([lambda e: e.matmul(ob[o][0:DH + 1, c0:TT], lhsT=vres[:, kb, h, 0:DH + 1], rhs=Pb[b][:, c0:TT], start=(kb == 0), stop=(kb == nkb - 1))],
                                   rd=[Pbt[b], vrest], wr=[obt[o]] if kb == 0 else [])
                    if kb > 0:
                        obt[o].w = tok
                    if kb == nkb - 1:
                        fw.do(dve, lambda e: e.reciprocal(out=rden[DH:DH + 1, :], in_=ob[o][DH:DH + 1, :]), rd=[obt[o]], wr=[rdent])
                        fin_q.append((n + 2, h, qi, o))

                def finalize(h, qi, o):
                    fw.group([lambda e: e.matmul(bc[0:DH, :], lhsT=onesf[DH:DH + 1, 0:DH], rhs=rden[DH:DH + 1, :], start=True, stop=True)], rd=[rdent], wr=[bct])
                    fw.do(act, lambda e: e.activation(out=bcs[:], in_=bc[0:DH, :], func=AF.Copy), rd=[bct], wr=[bcst])
                    g = (h * NT + qi) % 2
                    fw.do(dve, lambda e: e.tensor_tensor(out=ostg[g][:], in0=ob[o][0:DH, :], in1=bcs[:], op=ALU.mult), rd=[obt[o], bcst], wr=[ostgt[g]])
                    fw.dma(pool, at_d[h // 2, (h % 2) * DH:(h % 2 + 1) * DH, qi * TT:(qi + 1) * TT], ostg[g][:], f"st_o{g}", rd=[ostgt[g]])

                LOOK = 4
                load_kq(0)
                if NH > 1:
                    load_kq(1)
                loaded = 2
                for n in range(min(LOOK, nblk)):
                    s_mm(n)
                for n in range(nblk):
                    pv_mm(n)
                    if n + LOOK < nblk:
                        hn = blocks[n + LOOK][0]
                        s_mm(n + LOOK)
                    while fin_q and fin_q[0][0] <= n:
                        _, h_, qi_, o_ = fin_q.pop(0)
                        finalize(h_, qi_, o_)
                    h, qi, kb, nkb = blocks[n]
                    if qi == NT - 1 and kb == nkb - 1 and h + 2 < NH:
                        load_kq(h + 2)
                while fin_q:
                    _, h_, qi_, o_ = fin_q.pop(0)
                    finalize(h_, qi_, o_)
                fw.finish()

        class Tail:
            def __init__(self, fw, st, tag, gbase):
                self.fw = fw
                self.nrm = NormCtx(fw, st, tag)
                self.m = sb_in(st, f"m{tag}", [128, KC, TT], F32)
                self.mt = [T() for _ in range(KC)]
                self.gbase = gbase
                self.pend = None

            def chunk(self, c, bank, bankt):
                fw = self.fw
                m = self.m
                fw.do(fw.act, lambda e: e.activation(out=m[:, c, :], in_=bank[:], func=AF.Copy), rd=[bankt], wr=[self.mt[c]])
                i = self.nrm.square(bank[:], bankt)
                if self.pend is not None:
                    self.nrm.accum(self.pend[0], self.pend[1] == 0, False)
                self.pend = (i, c)

            def finish(self, xT, xTt):
                fw = self.fw
                self.nrm.accum(self.pend[0], False, True)
                self.pend = None
                r, rt = self.nrm.finish()
                m = self.m
                for c in range(KC):
                    fw.do(fw.dve, lambda e, c=c: e.scalar_tensor_tensor(out=m[:, c, :], in0=m[:, c, :], scalar=vcol(self.gbase, c), in1=r[:], op0=ALU.mult, op1=ALU.mult),
                          rd=[rt], wr=[self.mt[c]])
                    fw.do(fw.dve, lambda e, c=c: e.tensor_tensor(out=xT[:, c, :], in0=xT[:, c, :], in1=m[:, c, :], op=ALU.add), rd=[self.mt[c]], wr=[xTt[c]])

        def wload_tokens(fw, dst, src, sem):
            t = T(const=True)
            return load_w(fw, dst, src, sem, t)

        def phase3():
            with ExitStack() as st:
                fw = FW(nc)
                pe, act, dve, pool, sp = fw.pe, fw.act, fw.dve, fw.pool, fw.sp
                w_out = sb_in(st, "w_out0", [128, KC, D], BF16)
                wtoks = wload_tokens(fw, w_out, ewout_d, "w3_")
                pw = sb_in(st, "pw", [128, 4, 128], BF16)
                fw.dma(pool, pw[:], epw_d.rearrange("g d e -> d g e"), "w3p")
                wtoks = wtoks + [fw.alltoks[-1]]
                xT = [sb_in(st, f"xT3{i}", [128, KC, TT], F32) for i in range(2)]
                xTt = [[T() for _ in range(KC)] for _ in range(2)]
                cat = [sb_in(st, f"cat{i}", [128, KC, TT], BF16) for i in range(2)]
                catA = [T() for _ in range(2)]
                catP = [[T() for _ in range(4)] for _ in range(2)]
                ut = [sb_in(st, f"ut{i}", [128, 4, TT + 16], F32) for i in range(2)]
                utt = [T() for _ in range(2)]
                wa = sb_in(st, "wa", [128, TT + 16], F32); wat = T()
                wb = sb_in(st, "wb", [128, TT + 16], F32); wbt = T()
                pl = [sb_in(st, f"pl{i}", [128, TT], BF16) for i in range(2)]
                plt = [T() for _ in range(2)]
                fx = sb_in(st, "fx", [128, 16], F32); fxt = T()
                tail = Tail(fw, st, "3", V_POST + 0)
                pp = [ps_in(st, f"pp{i}") for i in range(2)]
                ppt = [T() for _ in range(2)]
                po = [ps_in(st, f"po{i}") for i in range(3)]
                pot = [T() for _ in range(3)]
                cnt = {"po": 0, "pl": 0}

                def load(i):
                    s = i % 2
                    sl = slice(i * TT, (i + 1) * TT)
                    fw.dma(sp, ut[s][:], u_d[:, :, i * TT:(i + 1) * TT + 16].rearrange("g p t -> p g t"), f"ldu{s}", wr=[utt[s]])
                    fw.dma(sp, cat[s][:, 0:4, :], at_d[:, :, sl].rearrange("k p t -> p k t"), f"lda{s}", wr=[catA[s]])
                    fw.dma(sp, xT[s][:], res_d[:, :, sl].rearrange("k p t -> p k t"), f"ldx{s}", wr=xTt[s])

                def pooling(i):
                    s = i % 2
                    W = TT + 16
                    for g in range(4):
                        u = ut[s]
                        src = (lambda a, b_, u=u, g=g: u[:, g, a:b_])
                        srct = utt[s]
                        bufs = [(wa, wat), (wb, wbt)]
                        sh = 1
                        for lvl in range(g + 1):
                            dstb, dstt = bufs[lvl % 2]
                            lo = 2 * sh - 1
                            fw.do(dve, lambda e, dstb=dstb, src=src, sh=sh, lo=lo: e.tensor_tensor(out=dstb[:, lo:W], in0=src(lo, W), in1=src(lo - sh, W - sh), op=ALU.add),
                                  rd=[srct], wr=[dstt])
                            src, srct = (lambda a, b_, dstb=dstb: dstb[:, a:b_]), dstt
                            sh *= 2
                        w = 2 ** (g + 1)
                        j = cnt["pl"] % 2
                        cnt["pl"] += 1
                        fw.do(dve, lambda e, src=src, g=g, j=j, w=w: e.scalar_tensor_tensor(out=pl[j][:], in0=src(16, W), scalar=1.0 / w, in1=u[:, g, 16:W], op0=ALU.mult, op1=ALU.subtract),
                              rd=[srct, utt[s]], wr=[plt[j]])
                        if i == 0:
                            fw.do(dve, lambda e, src=src, g=g: e.tensor_tensor(out=fx[:], in0=src(16, 32), in1=vecs[:, V_INVC + 16 * g:V_INVC + 16 * (g + 1)], op=ALU.mult), rd=[srct], wr=[fxt])
                            fw.do(dve, lambda e, g=g, j=j: e.tensor_tensor(out=pl[j][:, 0:16], in0=fx[:], in1=u[:, g, 16:32], op=ALU.subtract), rd=[fxt, utt[s]], wr=[plt[j]])
                        b = g % 2
                        fw.group([lambda e, g=g, j=j, b=b: e.matmul(pp[b][:], lhsT=pw[:, g, :], rhs=pl[j][:], start=True, stop=True)], rd=[plt[j]], wr=[ppt[b]], extra=wtoks)
                        fw.do(act, lambda e, g=g, b=b: e.activation(out=cat[s][:, 4 + g, :], in_=pp[b][:], func=AF.Identity, scale=vcol(V_PSC, g)), rd=[ppt[b]], wr=[catP[s][g]])

                def outproj(i):
                    s = i % 2
                    for c in range(KC):
                        b = cnt["po"] % 3
                        cnt["po"] += 1
                        fns = [(lambda e, k=k, c=c, b=b: e.matmul(po[b][:], lhsT=w_out[:, k, c * 128:(c + 1) * 128], rhs=cat[s][:, k, :], start=(k == 0), stop=(k == KC - 1))) for k in range(KC)]
                        fw.group(fns, rd=[catA[s]] + catP[s], wr=[pot[b]], extra=wtoks)
                        tail.chunk(c, po[b], pot[b])

                def outfin(i):
                    s = i % 2
                    tail.finish(xT[s], xTt[s])
                    fw.dma(pool, res_d[:, :, i * TT:(i + 1) * TT].rearrange("k p t -> p k t"), xT[s][:], f"st_x{s}", rd=xTt[s])

                load(0)
                pooling(0)
                for i in range(NT):
                    if i + 1 < NT:
                        load(i + 1)
                    outproj(i)
                    if i + 1 < NT:
                        pooling(i + 1)
                    outfin(i)
                fw.finish()

        def phase_ffn_a(layer):
            with ExitStack() as st:
                fw = FW(nc)
                pe, act, dve, pool, sp = fw.pe, fw.act, fw.dve, fw.pool, fw.sp
                wg = sb_in(st, "wg", [128, KC, DFF], BF16)
                wu = sb_in(st, "wu", [128, KC, DFF], BF16)
                WB4 = [0, 512, 1024, 1536, 2048, 2560, DFF]
                wgk, wuk = [], []
                v_g = wg_d[layer].rearrange("(kc p) m -> p kc m", p=128)
                v_u = wu_d[layer].rearrange("(kc p) m -> p kc m", p=128)
                for bi in range(len(WB4) - 1):
                    wgk.append(fw.dma(pool, wg[:, :, WB4[bi]:WB4[bi + 1]], v_g[:, :, WB4[bi]:WB4[bi + 1]], f"w4g{bi}", max_dma_last_dim=4096))
                    wuk.append(fw.dma(pool, wu[:, :, WB4[bi]:WB4[bi + 1]], v_u[:, :, WB4[bi]:WB4[bi + 1]], f"w4u{bi}", max_dma_last_dim=4096))
                xT = [sb_in(st, f"xT4{i}", [128, KC, TT], F32) for i in range(2)]
                xTt = [[T() for _ in range(KC)] for _ in range(2)]
                hT = [sb_in(st, f"hT4{i}", [128, KC, TT], BF16) for i in range(2)]
                hTt = [[T() for _ in range(KC)] for _ in range(2)]
                nrm = NormCtx(fw, st, "4")
                sg = [sb_in(st, f"sg{i}", [128, TT], F32) for i in range(2)]
                sgt = [T() for _ in range(2)]
                astg = sb_in(st, "astg", [128, FC, TT], BF16)
                astt = [T() for _ in range(FC)]
                pg = [ps_in(st, f"pg{i}") for i in range(3)]
                pgt = [T() for _ in range(3)]
                pu = [ps_in(st, f"pu{i}") for i in range(3)]
                put = [T() for _ in range(3)]
                gb = V_FPRE + 8 * layer

                def load(i):
                    s = i % 2
                    fw.dma(sp, xT[s][:], res_d[:, :, i * TT:(i + 1) * TT].rearrange("k p t -> p k t"), f"ldx{s}", wr=xTt[s])

                def prepA(i):
                    s = i % 2
                    sqs = []
                    for k in range(KC):
                        sqs.append(nrm.square(xT[s][:, k, :], xTt[s][k]))
                        if k >= 1:
                            nrm.accum(sqs[k - 1], k - 1 == 0, False)
                    nrm.accum(sqs[KC - 1], False, True)

                def prepB(i):
                    s = i % 2
                    r, rt = nrm.finish()
                    for k in range(KC):
                        fw.do(dve, lambda e, k=k: e.scalar_tensor_tensor(out=hT[s][:, k, :], in0=xT[s][:, k, :], scalar=vcol(gb, k), in1=r[:], op0=ALU.mult, op1=ALU.mult),
                              rd=[xTt[s][k], rt], wr=[hTt[s][k]])

                n = [0]

                def chunk(i, c):
                    s = i % 2
                    b = n[0] % 3
                    j2 = n[0] % 2
                    j4 = n[0] % 4
                    n[0] += 1
                    fns = [(lambda e, k=k: e.matmul(pg[b][:], lhsT=wg[:, k, c * 128:(c + 1) * 128], rhs=hT[s][:, k, :], start=(k == 0), stop=(k == KC - 1))) for k in range(KC)]
                    fw.group(fns, rd=hTt[s], wr=[pgt[b]], extra=[wgk[blk_of(c * 128, WB4)]])
                    fns = [(lambda e, k=k: e.matmul(pu[b][:], lhsT=wu[:, k, c * 128:(c + 1) * 128], rhs=hT[s][:, k, :], start=(k == 0), stop=(k == KC - 1))) for k in range(KC)]
                    fw.group(fns, rd=hTt[s], wr=[put[b]], extra=[wuk[blk_of(c * 128, WB4)]])
                    fw.do(act, lambda e: e.activation(out=sg[j2][:], in_=pg[b][:], func=AF.Silu), rd=[pgt[b]], wr=[sgt[j2]])
                    fw.do(dve, lambda e: e.tensor_tensor(out=astg[:, c, :], in0=sg[j2][:], in1=pu[b][:], op=ALU.mult), rd=[sgt[j2], put[b]], wr=[astt[c]])
                    if c == FC // 2 - 1 or c == FC - 1:
                        c0 = 0 if c < FC - 1 else FC // 2
                        fw.dma(pool, a_d[c0:c + 1, :, i * TT:(i + 1) * TT].rearrange("c p t -> p c t"), astg[:, c0:c + 1, :], f"st_a{0 if c0 == 0 else 1}", rd=astt[c0:c + 1])

                load(0)
                if NT > 1:
                    load(1)
                prepA(0)
                prepB(0)
                for i in range(NT):
                    for c in range(FC):
                        chunk(i, c)
                        if c == 4 and i + 1 < NT:
                            prepA(i + 1)
                        if c == 12 and i + 1 < NT:
                            prepB(i + 1)
                    if i + 2 < NT:
                        load(i + 2)
                fw.finish()

        def phase_ffn_b(layer, final):
            with ExitStack() as st:
                fw = FW(nc)
                pe, act, dve, pool, sp = fw.pe, fw.act, fw.dve, fw.pool, fw.sp
                wd = sb_in(st, "wd", [128, FC, D], BF16)
                WB5 = [0, 256, 512, 768, D]
                wdk = load_wc(fw, wd, wd_d[layer], "w5d", WB5)
                xT = [sb_in(st, f"xT5{i}", [128, KC, TT], F32) for i in range(2)]
                xTt = [[T() for _ in range(KC)] for _ in range(2)]
                aT = [sb_in(st, f"aT5{i}", [128, FC, TT], BF16) for i in range(2)]
                aTt = [T() for _ in range(2)]
                tail = Tail(fw, st, "5", V_FPOST + 8 * layer)
                po = [ps_in(st, f"po5{i}") for i in range(3)]
                pot = [T() for _ in range(3)]
                cnt = {"po": 0, "tp": 0}
                if final:
                    yo = sb_in(st, "yo", [128, 4, D], F32); yot = T()
                    tp = [ps_in(st, f"tp5{i}") for i in range(2)]
                    tpt = [T() for _ in range(2)]

                def load(i):
                    s = i % 2
                    sl = slice(i * TT, (i + 1) * TT)
                    fw.dma(sp, aT[s][:], a_d[:, :, sl].rearrange("k p t -> p k t"), f"lda{s}", wr=[aTt[s]])
                    fw.dma(sp, xT[s][:], res_d[:, :, sl].rearrange("k p t -> p k t"), f"ldx{s}", wr=xTt[s])

                def body(i):
                    s = i % 2
                    for c in range(KC):
                        b = cnt["po"] % 3
                        cnt["po"] += 1
                        fns = [(lambda e, k=k, c=c, b=b: e.matmul(po[b][:], lhsT=wd[:, k, c * 128:(c + 1) * 128], rhs=aT[s][:, k, :], start=(k == 0), stop=(k == FC - 1))) for k in range(FC)]
                        fw.group(fns, rd=[aTt[s]], wr=[pot[b]], extra=[wdk[c // 2]])
                        tail.chunk(c, po[b], pot[b])
                    tail.finish(xT[s], xTt[s])
                    if not final:
                        fw.dma(pool, res_d[:, :, i * TT:(i + 1) * TT].rearrange("k p t -> p k t"), xT[s][:], f"st_x{s}", rd=xTt[s])
                    else:
                        for ss in range(4):
                            for half in range(2):
                                b = cnt["tp"] % 2
                                cnt["tp"] += 1
                                fns = [(lambda e, kk=kk, b=b, half=half, ss=ss: e.transpose(tp[b][:, kk * 128:(kk + 1) * 128], in_=xT[s][:, half * 4 + kk, ss * 128:(ss + 1) * 128], identity=ident[:])) for kk in range(4)]
                                fw.group(fns, rd=xTt[s][half * 4:half * 4 + 4], wr=[tpt[b]])
                                if half == 0:
                                    fw.do(act, lambda e, b=b, ss=ss: e.activation(out=yo[:, ss, 0:512], in_=tp[b][:], func=AF.Copy), rd=[tpt[b]], wr=[yot])
                                else:
                                    fw.do(dve, lambda e, b=b, ss=ss: e.tensor_copy(out=yo[:, ss, 512:1024], in_=tp[b][:]), rd=[tpt[b]], wr=[yot])
                        fw.dma(pool, y_d[i * TT:(i + 1) * TT, :].rearrange("(s p) d -> p s d", p=128), yo[:], "st_y", rd=[yot])

                load(0)
                for i in range(NT):
                    if i + 1 < NT:
                        load(i + 1)
                    body(i)
                fw.finish()

        def phase6():
            with ExitStack() as st:
                fw = FW(nc)
                pe, act, dve, pool, sp = fw.pe, fw.act, fw.dve, fw.pool, fw.sp
                w_in = sb_in(st, "w_in1", [128, KC, 2 * D], BF16)
                WB6 = [0, 512, 1024, 1536, 2 * D]
                w6k = load_wc(fw, w_in, owin_d, "w6_", WB6)
                xT = [sb_in(st, f"xT6{i}", [128, KC, TT], F32) for i in range(2)]
                xTt = [[T() for _ in range(KC)] for _ in range(2)]
                hT = [sb_in(st, f"hT6{i}", [128, KC, TT], BF16) for i in range(2)]
                hTt = [[T() for _ in range(KC)] for _ in range(2)]
                nrm = NormCtx(fw, st, "6")
                ogs = [sb_in(st, f"ogs{i}", [128, KC, TT], F32) for i in range(2)]
                ogst = [[T() for _ in range(KC)] for _ in range(2)]
                pj = [ps_in(st, f"pj6{i}") for i in range(4)]
                pjt = [T() for _ in range(4)]
                gb = V_PRE + 8
                n = [0]

                def load(i):
                    s = i % 2
                    fw.dma(sp, xT[s][:], res_d[:, :, i * TT:(i + 1) * TT].rearrange("k p t -> p k t"), f"ldx{s}", wr=xTt[s])

                def prepA(i):
                    s = i % 2
                    sqs = []
                    for k in range(KC):
                        sqs.append(nrm.square(xT[s][:, k, :], xTt[s][k]))
                        if k >= 1:
                            nrm.accum(sqs[k - 1], k - 1 == 0, False)
                    nrm.accum(sqs[KC - 1], False, True)

                def prepB(i):
                    s = i % 2
                    r, rt = nrm.finish()
                    for k in range(KC):
                        fw.do(dve, lambda e, k=k: e.scalar_tensor_tensor(out=hT[s][:, k, :], in0=xT[s][:, k, :], scalar=vcol(gb, k), in1=r[:], op0=ALU.mult, op1=ALU.mult),
                              rd=[xTt[s][k], rt], wr=[hTt[s][k]])

                def chunk(i, c):
                    s = i % 2
                    b = n[0] % 4
                    n[0] += 1
                    fns = [(lambda e, k=k: e.matmul(pj[b][:], lhsT=w_in[:, k, c * 128:(c + 1) * 128], rhs=hT[s][:, k, :], start=(k == 0), stop=(k == KC - 1))) for k in range(KC)]
                    fw.group(fns, rd=hTt[s], wr=[pjt[b]], extra=[w6k[c // 4]])
                    if c < KC:
                        fw.do(act, lambda e: e.activation(out=ogs[0][:, c, :], in_=pj[b][:], func=AF.Gelu_apprx_tanh), rd=[pjt[b]], wr=[ogst[0][c]])
                        if c == KC - 1:
                            fw.dma(pool, g_d[:, :, i * TT:(i + 1) * TT].rearrange("k p t -> p k t"), ogs[0][:], "st_g0", rd=ogst[0])
                    else:
                        cc = c - KC
                        fw.do(dve, lambda e: e.tensor_copy(out=ogs[1][:, cc, :], in_=pj[b][:]), rd=[pjt[b]], wr=[ogst[1][cc]])
                        if cc == KC - 1:
                            fw.dma(pool, xr_d[:, :, 3 + i * TT:3 + (i + 1) * TT].rearrange("k p t -> p k t"), ogs[1][:], "st_g1", rd=ogst[1])

                load(0)
                if NT > 1:
                    load(1)
                prepA(0)
                prepB(0)
                for i in range(NT):
                    for c in range(2 * KC):
                        chunk(i, c)
                        if c == 3 and i + 1 < NT:
                            prepA(i + 1)
                        if c == 9 and i + 1 < NT:
                            prepB(i + 1)
                    if i + 2 < NT:
                        load(i + 2)
                fw.finish()

        def phase7():
            with ExitStack() as st:
                fw = FW(nc)
                pe, act, dve, pool, sp = fw.pe, fw.act, fw.dve, fw.pool, fw.sp
                w_out = sb_in(st, "w_out1", [128, KC, D], BF16)
                wtoks = wload_tokens(fw, w_out, owout_d, "w7o")
                wa_ = sb_in(st, "w_a", [128, 4, 2, 256], BF16)
                wx_ = sb_in(st, "w_x", [128, 4, 2, 256], BF16)
                for hd in range(4):
                    fw.dma(pool, wa_[:, hd, :, :], owa_d[hd].rearrange("(kc p) e -> p kc e", p=128), f"w7a{hd}")
                    wtoks = wtoks + [fw.alltoks[-1]]
                    fw.dma(pool, wx_[:, hd, :, :], owx_d[hd].rearrange("(kc p) e -> p kc e", p=128), f"w7x{hd}")
                    wtoks = wtoks + [fw.alltoks[-1]]
                xT = [sb_in(st, "xT70", [128, KC, TT], F32)] * 2
                xTt = [[T() for _ in range(KC)]] * 2
                xr = [sb_in(st, "xr70", [128, KC, TT + 3], F32)] * 2
                xrt = [T()] * 2
                gt_ = [sb_in(st, "gt70", [128, KC, TT], F32)] * 2
                gtt = [T()] * 2
                xc2 = [sb_in(st, f"xc{i}", [128, KC, TT], F32) for i in range(2)]
                xct2 = [[T() for _ in range(KC)] for _ in range(2)]
                xcb2 = [sb_in(st, f"xcb{i}", [128, KC, TT], BF16) for i in range(2)]
                xcbt2 = [[T() for _ in range(KC)] for _ in range(2)]
                hsr = [sb_in(st, f"hsr{i}", [128, TT], F32) for i in range(2)]
                hsrt = [T() for _ in range(2)]
                carry = sb_in(st, "carry", [128, KC], F32)
                carryt = [T() for _ in range(KC)]
                zT = sb_in(st, "zT", [128, KC, TT], BF16)
                zTt = [T() for _ in range(KC)]
                NR = 4
                rr = [sb_in(st, f"rr{i}", [128, TT], F32) for i in range(NR)]; rrt = [T() for _ in range(NR)]
                ii = [sb_in(st, f"ii{i}", [128, TT], F32) for i in range(NR)]; iit = [T() for _ in range(NR)]
                aa = [sb_in(st, f"aa{i}", [128, TT], F32) for i in range(NR)]; aat = [T() for _ in range(NR)]
                bb = [sb_in(st, f"bb{i}", [128, TT], F32) for i in range(2)]; bbt = [T() for _ in range(2)]
                tail = Tail(fw, st, "7", V_POST + 8)
                pa = [ps_in(st, f"pa{i}") for i in range(2)]; pat = [T() for _ in range(2)]
                px = [ps_in(st, f"px{i}") for i in range(2)]; pxt = [T() for _ in range(2)]
                po = [ps_in(st, f"po7{i}") for i in range(3)]; pot = [T() for _ in range(3)]
                cnt = {"po": 0, "g": 0, "b": 0}

                def load_xr(i):
                    fw.dma(sp, xr[0][:], xr_d[:, :, i * TT:(i + 1) * TT + 3].rearrange("k p t -> p k t"), "ldr0", wr=[xrt[0]])

                def load_g(i):
                    fw.dma(sp, gt_[0][:], g_d[:, :, i * TT:(i + 1) * TT].rearrange("k p t -> p k t"), "ldg0", wr=[gtt[0]])

                def load_x(i):
                    fw.dma(sp, xT[0][:], res_d[:, :, i * TT:(i + 1) * TT].rearrange("k p t -> p k t"), "ldx0", wr=xTt[0])

                def conv_a(i):
                    xc, xct = xc2[i % 2], xct2[i % 2]
                    for c in range(KC):
                        fw.do(act, lambda e, c=c: e.activation(out=xc[:, c, :], in_=xr[0][:, c, 0:TT], func=AF.Identity, scale=vcol(V_CW, c), bias=vcol(V_CB, c)),
                              rd=[xrt[0]], wr=[xct[c]])

                def conv_b(i):
                    xc, xct = xc2[i % 2], xct2[i % 2]
                    for c in range(KC):
                        for j in range(1, 4):
                            fw.do(dve, lambda e, c=c, j=j: e.scalar_tensor_tensor(out=xc[:, c, :], in0=xr[0][:, c, j:j + TT], scalar=vcol(V_CW + 8 * j, c), in1=xc[:, c, :], op0=ALU.mult, op1=ALU.add),
                                  rd=[xrt[0]], wr=[xct[c]])

                def conv_c(i):
                    xc, xct = xc2[i % 2], xct2[i % 2]
                    xcb, xcbt = xcb2[i % 2], xcbt2[i % 2]
                    for c in range(KC):
                        fw.do(act, lambda e, c=c: e.activation(out=xcb[:, c, :], in_=xc[:, c, :], func=AF.Copy), rd=[xct[c]], wr=[xcbt[c]])
                        if debug:
                            fw.dma(sp, dbg["xc"][c, :, i * TT:(i + 1) * TT], xc[:, c, :], "dbg0", rd=[xct[c]])

                def recur(i):
                    s = i % 2
                    xc, xct = xc2[i % 2], xct2[i % 2]
                    xcb, xcbt = xcb2[i % 2], xcbt2[i % 2]
                    for grp in range(2):
                        cs = list(range(4 * grp, 4 * grp + 4))
                        for q_, c in enumerate(cs):
                            hd, mm = c // 2, c % 2
                            b = cnt["g"] % 2
                            cnt["g"] += 1
                            fns = [(lambda e, kk=kk, hd=hd, mm=mm, b=b: e.matmul(pa[b][:], lhsT=wa_[:, hd, kk, mm * 128:(mm + 1) * 128], rhs=xcb[:, 2 * hd + kk, :], start=(kk == 0), stop=(kk == 1))) for kk in range(2)]
                            fw.group(fns, rd=[xcbt[2 * hd], xcbt[2 * hd + 1]], wr=[pat[b]], extra=wtoks)
                            fns = [(lambda e, kk=kk, hd=hd, mm=mm, b=b: e.matmul(px[b][:], lhsT=wx_[:, hd, kk, mm * 128:(mm + 1) * 128], rhs=xcb[:, 2 * hd + kk, :], start=(kk == 0), stop=(kk == 1))) for kk in range(2)]
                            fw.group(fns, rd=[xcbt[2 * hd], xcbt[2 * hd + 1]], wr=[pxt[b]], extra=wtoks)
                            fw.do(act, lambda e, c=c, b=b, q_=q_: e.activation(out=rr[q_][:], in_=pa[b][:], func=AF.Sigmoid, bias=vcol(V_BA, c), scale=1.0), rd=[pat[b]], wr=[rrt[q_]])
                            fw.do(act, lambda e, c=c, b=b, q_=q_: e.activation(out=ii[q_][:], in_=px[b][:], func=AF.Sigmoid, bias=vcol(V_BX, c), scale=1.0), rd=[pxt[b]], wr=[iit[q_]])
                            if debug:
                                fw.dma(sp, dbg["r"][c, :, i * TT:(i + 1) * TT], rr[q_][:], "dbg5", rd=[rrt[q_]])
                                fw.dma(sp, dbg["i"][c, :, i * TT:(i + 1) * TT], ii[q_][:], "dbg6", rd=[iit[q_]])
                        for q_, c in enumerate(cs):
                            fw.do(act, lambda e, c=c, q_=q_: e.activation(out=aa[q_][:], in_=rr[q_][:], func=AF.Exp, scale=sc1[:, c:c + 1]), rd=[rrt[q_]], wr=[aat[q_]])
                            fw.do(act, lambda e, c=c, q_=q_: e.activation(out=rr[q_][:], in_=rr[q_][:], func=AF.Exp, scale=sc2[:, c:c + 1]), wr=[rrt[q_]])
                        for q_, c in enumerate(cs):
                            fw.do(act, lambda e, q_=q_: e.activation(out=rr[q_][:], in_=rr[q_][:], func=AF.Sqrt, bias=1.0, scale=-1.0), wr=[rrt[q_]])
                        for q_, c in enumerate(cs):
                            j = cnt["b"] % 2
                            cnt["b"] += 1
                            fw.do(dve, lambda e, c=c, q_=q_: e.tensor_tensor(out=ii[q_][:], in0=ii[q_][:], in1=xc[:, c, :], op=ALU.mult), rd=[xct[c]], wr=[iit[q_]])
                            fw.do(dve, lambda e, q_=q_, j=j: e.tensor_tensor(out=bb[j][:], in0=ii[q_][:], in1=rr[q_][:], op=ALU.mult), rd=[iit[q_], rrt[q_]], wr=[bbt[j]])
                            if i == 0:
                                fw.do(dve, lambda e, q_=q_, j=j: e.tensor_tensor_scan(out=hsr[j][:], data0=aa[q_][:], data1=bb[j][:], initial=0.0, op0=ALU.mult, op1=ALU.add),
                                      rd=[aat[q_], bbt[j]], wr=[hsrt[j]])
                            else:
                                fw.do(dve, lambda e, c=c, q_=q_, j=j: e.tensor_tensor_scan(out=hsr[j][:], data0=aa[q_][:], data1=bb[j][:], initial=carry[:, c:c + 1], op0=ALU.mult, op1=ALU.add),
                                      rd=[aat[q_], bbt[j], carryt[c]], wr=[hsrt[j]])
                            fw.do(dve, lambda e, c=c, j=j: e.tensor_copy(out=carry[:, c:c + 1], in_=hsr[j][:, TT - 1:TT]), rd=[hsrt[j]], wr=[carryt[c]])
                            fw.do(dve, lambda e, c=c, j=j: e.tensor_tensor(out=zT[:, c, :], in0=hsr[j][:], in1=gt_[0][:, c, :], op=ALU.mult), rd=[hsrt[j], gtt[0]], wr=[zTt[c]])
                            if debug:
                                fw.dma(sp, dbg["a"][c, :, i * TT:(i + 1) * TT], aa[q_][:], "dbg1", rd=[aat[q_]])
                                fw.dma(sp, dbg["b"][c, :, i * TT:(i + 1) * TT], bb[j][:], "dbg2", rd=[bbt[j]])
                                fw.dma(sp, dbg["hs"][c, :, i * TT:(i + 1) * TT], hsr[j][:], "dbg3", rd=[hsrt[j]])

                def outproj(i):
                    for c in range(KC):
                        b = cnt["po"] % 3
                        cnt["po"] += 1
                        fns = [(lambda e, k=k, c=c, b=b: e.matmul(po[b][:], lhsT=w_out[:, k, c * 128:(c + 1) * 128], rhs=zT[:, k, :], start=(k == 0), stop=(k == KC - 1))) for k in range(KC)]
                        fw.group(fns, rd=zTt, wr=[pot[b]], extra=wtoks)
                        tail.chunk(c, po[b], pot[b])

                def outfin(i):
                    tail.finish(xT[0], xTt[0])
                    fw.dma(pool, res_d[:, :, i * TT:(i + 1) * TT].rearrange("k p t -> p k t"), xT[0][:], "st_x0", rd=xTt[0])

                load_xr(0)
                load_g(0)
                load_x(0)
                conv_a(0)
                conv_b(0)
                conv_c(0)
                for i in range(NT):
                    if i + 1 < NT:
                        load_xr(i + 1)
                    recur(i)
                    if i + 1 < NT:
                        conv_a(i + 1)
                        conv_b(i + 1)
                        load_g(i + 1)
                    outproj(i)
                    if i + 1 < NT:
                        conv_c(i + 1)
                    outfin(i)
                    if i + 1 < NT:
                        load_x(i + 1)
                fw.finish()

        phase0()
        if nphase >= 1:
            with ExitStack() as st12:
                vres = sb_in(st12, "vres", [128, NB, NH, DH + 1], BF16)
                vrest = T()
                phase1(st12, vres, vrest)
                if nphase >= 2:
                    vrest2 = T(const=True)
                    phase2(vres, vrest2)
        if nphase >= 3:
            phase3()
        if nphase >= 4:
            phase_ffn_a(0)
        if nphase >= 5:
            phase_ffn_b(0, False)
        if nphase >= 6:
            phase6()
        if nphase >= 7:
            phase7()
        if nphase >= 8:
            phase_ffn_a(1)
        if nphase >= 9:
            phase_ffn_b(1, True)
    return nc


def _cm(v):
    return np.ascontiguousarray(np.asarray(v, np.float32).reshape(-1, 128).T)


def pack_vecs(inp):
    vecs = np.zeros((128, NV), np.float32)
    for l in range(2):
        vecs[:, V_PRE + 8 * l:V_PRE + 8 * l + 8] = _cm(inp["mix_pre_g"][l])
        vecs[:, V_POST + 8 * l:V_POST + 8 * l + 8] = _cm(inp["mix_post_g"][l])
        vecs[:, V_FPRE + 8 * l:V_FPRE + 8 * l + 8] = _cm(inp["ffn_pre_g"][l])
        vecs[:, V_FPOST + 8 * l:V_FPOST + 8 * l + 8] = _cm(inp["ffn_post_g"][l])
    vecs[:, V_PSC:V_PSC + 4] = _cm(inp["ev_pool_scale"][0])
    for j in range(4):
        vecs[:, V_CW + 8 * j:V_CW + 8 * j + 8] = _cm(inp["od_conv_w"][0, j])
    vecs[:, V_CB:V_CB + 8] = _cm(inp["od_conv_b"][0])
    vecs[:, V_BA:V_BA + 8] = _cm(inp["od_b_a"][0])
    vecs[:, V_BX:V_BX + 8] = _cm(inp["od_b_x"][0])
    vecs[:, V_LAM:V_LAM + 8] = _cm(inp["od_lam"][0])
    for g, w in enumerate((2, 4, 8, 16)):
        for t in range(16):
            vecs[:, V_INVC + 16 * g + t] = 1.0 / min(t + 1, w)
    return vecs


def make_in_map(inp, xb):
    f = lambda a: np.ascontiguousarray(np.asarray(a, np.float32))
    return {
        "x": f(xb),
        "vecs": pack_vecs(inp),
        "b_f": f(np.asarray(inp["ev_b_f"])[0].reshape(NH, 1)),
        "ffn_w_gate": f(inp["ffn_w_gate"]),
        "ffn_w_up": f(inp["ffn_w_up"]),
        "ffn_w_down": f(inp["ffn_w_down"]),
        "ev_w_in": f(np.asarray(inp["ev_w_in"])[0]),
        "ev_pool_w": f(np.asarray(inp["ev_pool_w"])[0]),
        "ev_w_out": f(np.asarray(inp["ev_w_out"])[0]),
        "od_w_in": f(np.asarray(inp["od_w_in"])[0]),
        "od_w_a": f(np.asarray(inp["od_w_a"])[0]),
        "od_w_x": f(np.asarray(inp["od_w_x"])[0]),
        "od_w_out": f(np.asarray(inp["od_w_out"])[0]),
    }


def kernel(**inputs):
    x = np.asarray(inputs["x"], np.float32)
    B, S, _ = x.shape
    nc = build(S)
    work = [0, 1, 4, 5][:B]
    real = [make_in_map(inputs, x[b]) for b in range(B)]
    zero = {k: np.zeros_like(v) for k, v in real[0].items()}
    in_maps = [zero] * 8
    in_maps = list(in_maps)
    for b, c in enumerate(work):
        in_maps[c] = real[b]
    res = run_bass_kernel_spmd(nc, in_maps, core_ids=list(range(8)))
    return np.stack([np.asarray(res.results[c]["y"], np.float32) for c in work], axis=0)
```
